# Optimizing a Trainium2 kernel written in Bass

```python
import jax, jax.numpy as jnp
from jax import lax
import numpy as np

D_MODEL = 4096
BATCH = 16
SEQ = 256
DEPTH = 1
DEC_BATCH = 8
DEC_SEQ = 1024
PAST_LEN = 256

GRID_W = 64
CHUNK = 64
GLA_HEADS = 4
GLA_QK = D_MODEL // 2
GLA_V = D_MODEL
GLA_DK = GLA_QK // GLA_HEADS
GLA_DV = GLA_V // GLA_HEADS
GLA_RANK = 16
GLA_TAU = 16.0
MLSTM_HEADS = 8
MLSTM_QK = D_MODEL // 2
MLSTM_V = D_MODEL
MLSTM_DK = MLSTM_QK // MLSTM_HEADS
MLSTM_DV = MLSTM_V // MLSTM_HEADS
EPS = 1e-6
SPLIT_SIZES = (GLA_QK, GLA_QK, GLA_V, GLA_V, 2 * GLA_RANK,
               MLSTM_QK, MLSTM_QK, MLSTM_V, MLSTM_V, MLSTM_V, 2 * MLSTM_HEADS, 2 * MLSTM_HEADS,
               2 * D_MODEL)
N_IN = sum(SPLIT_SIZES)

kernel_name = "bidir_gla_mlstm_diffusion_step"


def _rms(x):
    xf = x.astype(jnp.float32)
    return xf * lax.rsqrt(jnp.mean(xf * xf, axis=-1, keepdims=True) + EPS)


def _head_ln(x):
    xf = x.astype(jnp.float32)
    xc = xf - jnp.mean(xf, axis=-1, keepdims=True)
    return xc * lax.rsqrt(jnp.mean(xc * xc, axis=-1, keepdims=True) + EPS)


def _flip(x):
    return jnp.flip(x, axis=1)


def _to_col(x):
    b, t = x.shape[:2]
    rows = t // GRID_W
    return x.reshape((b, rows, GRID_W) + x.shape[2:]).swapaxes(1, 2).reshape(x.shape)


def _from_col(x):
    b, t = x.shape[:2]
    rows = t // GRID_W
    return x.reshape((b, GRID_W, rows) + x.shape[2:]).swapaxes(1, 2).reshape(x.shape)


def _chunks(x):
    b, t = x.shape[:2]
    x = x.reshape((b, t // CHUNK, CHUNK) + x.shape[2:])
    return jnp.moveaxis(x, (1, 3), (0, 2))


def _unchunks(x):
    x = jnp.moveaxis(x, (0, 2), (1, 3))
    return x.reshape((x.shape[0], x.shape[1] * x.shape[2]) + x.shape[3:])


def gla_scan(q, k, v, log_a, s0):
    qc, kc, vc, lac = _chunks(q), _chunks(k), _chunks(v), _chunks(log_a)
    cum = jnp.cumsum(lac, axis=3)
    tot = cum[:, :, :, -1]
    q_dec = qc * jnp.exp(cum)
    k_inv = kc * jnp.exp(-cum)
    k_end = kc * jnp.exp(tot[:, :, :, None] - cum)
    causal = jnp.tril(jnp.ones((CHUNK, CHUNK), dtype=bool))
    att = jnp.where(causal, jnp.einsum('nbhtd,nbhsd->nbhts', q_dec, k_inv), 0.0)
    o_intra = jnp.einsum('nbhts,nbhse->nbhte', att, vc)

    def step(s, xs):
        qd, ke, vv, tt = xs
        o = jnp.einsum('bhtd,bhde->bhte', qd, s)
        s = jnp.exp(tt)[..., None] * s + jnp.einsum('bhsd,bhse->bhde', ke, vv)
        return s, o

    s_fin, o_inter = lax.scan(step, s0, (q_dec, k_end, vc, tot))
    return _unchunks(o_intra + o_inter), s_fin


def mlstm_scan(q, k, v, ig, lf, c0, n0, m0):
    qc, kc, vc = _chunks(q), _chunks(k), _chunks(v)
    igc, lfc = _chunks(ig), _chunks(lf)
    b = jnp.cumsum(lfc, axis=-1)
    g_end = b[..., -1]
    w_end = g_end[..., None] - b + igc

    def step(carry, xs):
        cs, ns, ms = carry
        qq, kk, vv, bb, ge, we = xs
        num_i = jnp.einsum('bhtd,bhde->bhte', qq, cs)
        den_i = jnp.einsum('bhtd,bhd->bht', qq, ns)
        log_i = bb + ms[..., None]
        m_new = jnp.maximum(ge + ms, jnp.max(we, axis=-1))
        decay = jnp.exp(ge + ms - m_new)
        wgt = jnp.exp(we - m_new[..., None])
        cs = decay[..., None, None] * cs + jnp.einsum('bhs,bhsd,bhse->bhde', wgt, kk, vv)
        ns = decay[..., None] * ns + jnp.einsum('bhs,bhsd->bhd', wgt, kk)
        return (cs, ns, m_new), (num_i, den_i, log_i)

    (c_f, n_f, m_f), (num_i, den_i, log_i) = lax.scan(step, (c0, n0, m0), (qc, kc, vc, b, g_end, w_end))
    causal = jnp.tril(jnp.ones((CHUNK, CHUNK), dtype=bool))
    dmat = jnp.where(causal, b[..., :, None] - b[..., None, :] + igc[..., None, :], -jnp.inf)
    m_t = jnp.maximum(log_i, jnp.max(dmat, axis=-1))
    s = jnp.einsum('nbhtd,nbhsd->nbhts', qc, kc) * jnp.exp(dmat - m_t[..., None])
    sc = jnp.exp(log_i - m_t)
    num = jnp.einsum('nbhts,nbhse->nbhte', s, vc) + sc[..., None] * num_i
    den = jnp.sum(s, axis=-1) + sc * den_i
    h = num / jnp.maximum(jnp.abs(den), jnp.exp(-m_t))[..., None]
    return _unchunks(h), (c_f, n_f, m_f)


def mixer(h, latent, gla_s0, m_c0, m_n0, m_m0, w_in, gla_w_a2, gla_b_a, m_b_i, m_b_f,
          b_merge, gla_nw, m_nw, w_gp, w_mp, w_o):
    f32 = jnp.float32
    bsz, t, _ = h.shape
    proj = jnp.einsum('btd,dn->btn', h, w_in)
    points = [int(p) for p in np.cumsum(SPLIT_SIZES)[:-1]]
    (gq, gk, gv, gz, ga, mq, mk, mv, mz, mo, mi, mf, mg) = jnp.split(proj, points, axis=-1)

    q = gq.reshape(bsz, t, GLA_HEADS, GLA_DK).astype(f32) * (GLA_DK ** -0.5)
    k = gk.reshape(bsz, t, GLA_HEADS, GLA_DK).astype(f32)
    v = gv.reshape(bsz, t, GLA_HEADS, GLA_DV).astype(f32)
    a_logit = jnp.einsum('btjr,jrk->btjk', ga.reshape(bsz, t, 2, GLA_RANK).astype(f32),
                         gla_w_a2.astype(f32)) + gla_b_a.astype(f32)
    log_a = (jax.nn.log_sigmoid(a_logit) / GLA_TAU).reshape(bsz, t, 2, GLA_HEADS, GLA_DK)
    o_f, s_f = gla_scan(q, k, v, log_a[:, :, 0], gla_s0[:, 0].astype(f32))
    o_b, s_b = gla_scan(_flip(q), _flip(k), _flip(v), _flip(log_a[:, :, 1]), gla_s0[:, 1].astype(f32))
    o = _rms(o_f + _flip(o_b)).reshape(bsz, t, GLA_V) * gla_nw
    y_gla = jnp.einsum('btc,cd->btd', o.astype(h.dtype) * jax.nn.silu(gz), w_gp)

    order = _to_col if latent else (lambda a: a)
    unorder = _from_col if latent else (lambda a: a)
    q2 = order(mq.reshape(bsz, t, MLSTM_HEADS, MLSTM_DK).astype(f32)) * (MLSTM_DK ** -0.5)
    k2 = order(mk.reshape(bsz, t, MLSTM_HEADS, MLSTM_DK).astype(f32))
    v2 = order(mv.reshape(bsz, t, MLSTM_HEADS, MLSTM_DV).astype(f32))
    ig = order(mi.reshape(bsz, t, 2, MLSTM_HEADS).astype(f32) + m_b_i.astype(f32))
    lf = jax.nn.log_sigmoid(order(mf.reshape(bsz, t, 2, MLSTM_HEADS).astype(f32) + m_b_f.astype(f32)))
    h_f, st_f = mlstm_scan(q2, k2, v2, ig[:, :, 0], lf[:, :, 0],
                           m_c0[:, 0].astype(f32), m_n0[:, 0].astype(f32), m_m0[:, 0].astype(f32))
    h_b, st_b = mlstm_scan(_flip(q2), _flip(k2), _flip(v2), _flip(ig[:, :, 1]), _flip(lf[:, :, 1]),
                           m_c0[:, 1].astype(f32), m_n0[:, 1].astype(f32), m_m0[:, 1].astype(f32))
    hm = _head_ln(unorder(h_f + _flip(h_b))).reshape(bsz, t, MLSTM_V) * m_nw
    y_m = jnp.einsum('btc,cd->btd', hm.astype(h.dtype) * jax.nn.sigmoid(mo) * jax.nn.silu(mz), w_mp)

    gates = jax.nn.sigmoid(mg + b_merge).reshape(bsz, t, 2, D_MODEL)
    out = jnp.einsum('btd,de->bte', gates[:, :, 0] * y_gla + gates[:, :, 1] * y_m, w_o)
    gla_state = jnp.stack([s_f, s_b], axis=1)
    c_state = jnp.stack([st_f[0], st_b[0]], axis=1)
    n_state = jnp.stack([st_f[1], st_b[1]], axis=1)
    m_state = jnp.stack([st_f[2], st_b[2]], axis=1)
    return out, gla_state, c_state, n_state, m_state


def setup_inputs(seed: int = 0) -> dict:
    key = jax.random.key(seed)
    ks = jax.random.split(key, 24)
    f32 = jnp.float32

    def nrm(k, shape, s):
        return jax.random.normal(k, shape, f32) * s

    return {
        "x_prompt": nrm(ks[0], (BATCH, SEQ, D_MODEL), 1.0),
        "x_sample": nrm(ks[1], (DEC_BATCH, DEC_SEQ, D_MODEL), 1.0),
        "c": nrm(ks[2], (DEC_BATCH, D_MODEL), 1.0),
        "state_gla_S": nrm(ks[3], (DEC_BATCH, DEPTH, 2, GLA_HEADS, GLA_DK, GLA_DV), 1.0),
        "state_mlstm_C": nrm(ks[4], (DEC_BATCH, DEPTH, 2, MLSTM_HEADS, MLSTM_DK, MLSTM_DV), 0.5),
        "state_mlstm_n": nrm(ks[5], (DEC_BATCH, DEPTH, 2, MLSTM_HEADS, MLSTM_DK), 0.5),
        "state_mlstm_m": nrm(ks[6], (DEC_BATCH, DEPTH, 2, MLSTM_HEADS), 1.0),
        "c_ctx": nrm(ks[7], (D_MODEL,), 1.0),
        "w_ada": nrm(ks[8], (DEPTH, D_MODEL, 3 * D_MODEL), 0.5 * D_MODEL ** -0.5),
        "b_ada": nrm(ks[9], (DEPTH, 3 * D_MODEL), 0.02),
        "w_in": nrm(ks[10], (DEPTH, D_MODEL, N_IN), D_MODEL ** -0.5),
        "gla_w_a2": nrm(ks[11], (DEPTH, 2, GLA_RANK, GLA_QK), GLA_RANK ** -0.5),
        "gla_b_a": nrm(ks[12], (DEPTH, 2, GLA_QK), 0.02),
        "mlstm_b_i": nrm(ks[13], (DEPTH, 2, MLSTM_HEADS), 0.02),
        "mlstm_b_f": jnp.broadcast_to(jnp.linspace(3.0, 6.0, MLSTM_HEADS, dtype=f32), (DEPTH, 2, MLSTM_HEADS))
                     + nrm(ks[14], (DEPTH, 2, MLSTM_HEADS), 0.1),
        "b_merge": nrm(ks[15], (DEPTH, 2 * D_MODEL), 0.02),
        "gla_norm_w": 1.0 + nrm(ks[16], (DEPTH, GLA_V), 0.02),
        "mlstm_norm_w": 1.0 + nrm(ks[17], (DEPTH, MLSTM_V), 0.02),
        "w_gla_proj": nrm(ks[18], (DEPTH, GLA_V, D_MODEL), GLA_V ** -0.5),
        "w_mlstm_proj": nrm(ks[19], (DEPTH, MLSTM_V, D_MODEL), MLSTM_V ** -0.5),
        "w_out": nrm(ks[20], (DEPTH, D_MODEL, D_MODEL), D_MODEL ** -0.5),
        "final_norm_w": 1.0 + nrm(ks[21], (D_MODEL,), 0.02),
    }


def reference(x_prompt, x_sample, c, state_gla_S, state_mlstm_C, state_mlstm_n, state_mlstm_m, c_ctx,
              w_ada, b_ada, w_in, gla_w_a2, gla_b_a, mlstm_b_i, mlstm_b_f, b_merge, gla_norm_w,
              mlstm_norm_w, w_gla_proj, w_mlstm_proj, w_out, final_norm_w):
    f32 = jnp.float32
    bp = x_prompt.shape[0]
    zero_s = jnp.zeros((bp, 2, GLA_HEADS, GLA_DK, GLA_DV), f32)
    zero_c = jnp.zeros((bp, 2, MLSTM_HEADS, MLSTM_DK, MLSTM_DV), f32)
    zero_n = jnp.zeros((bp, 2, MLSTM_HEADS, MLSTM_DK), f32)
    zero_m = jnp.zeros((bp, 2, MLSTM_HEADS), f32)
    xp, xs = x_prompt, x_sample
    new_s, new_c, new_n, new_m = [], [], [], []
    for l in range(DEPTH):
        wl = (w_in[l], gla_w_a2[l], gla_b_a[l], mlstm_b_i[l], mlstm_b_f[l], b_merge[l],
              gla_norm_w[l], mlstm_norm_w[l], w_gla_proj[l], w_mlstm_proj[l], w_out[l])
        shift_p, scale_p, gate_p = jnp.split(jax.nn.silu(c_ctx) @ w_ada[l] + b_ada[l], 3, axis=-1)
        hp = (_rms(xp) * (1.0 + scale_p) + shift_p).astype(xp.dtype)
        op, s_l, c_l, n_l, m_l = mixer(hp, False, zero_s, zero_c, zero_n, zero_m, *wl)
        xp = xp + gate_p * op
        new_s.append(s_l)
        new_c.append(c_l)
        new_n.append(n_l)
        new_m.append(m_l)
        mod_s = jax.nn.silu(c) @ w_ada[l] + b_ada[l]
        shift_s, scale_s, gate_s = jnp.split(mod_s[:, None, :], 3, axis=-1)
        hs = (_rms(xs) * (1.0 + scale_s) + shift_s).astype(xs.dtype)
        os_, _, _, _, _ = mixer(hs, True, state_gla_S[:, l], state_mlstm_C[:, l], state_mlstm_n[:, l],
                                state_mlstm_m[:, l], *wl)
        xs = xs + gate_s * os_
    y_prompt = (_rms(xp) * final_norm_w).astype(x_prompt.dtype)
    y_sample = (_rms(xs) * final_norm_w).astype(x_sample.dtype)
    new_gla_S = jnp.stack(new_s, axis=1).astype(x_prompt.dtype)
    new_mlstm_C = jnp.stack(new_c, axis=1).astype(x_prompt.dtype)
    new_mlstm_n = jnp.stack(new_n, axis=1).astype(x_prompt.dtype)
    new_mlstm_m = jnp.stack(new_m, axis=1).astype(x_prompt.dtype)
    return (y_prompt, y_sample, new_gla_S, new_mlstm_C, new_mlstm_n, new_mlstm_m)
```

```python
import numpy as np
from contextlib import ExitStack
import concourse.bass as bass
import concourse.mybir as mybir
from concourse.bass_utils import run_bass_kernel_spmd

F32 = mybir.dt.float32
BF16 = mybir.dt.bfloat16
AF = mybir.ActivationFunctionType
ALU = mybir.AluOpType
AX = mybir.AxisListType

D = 4096
NTOK = 1536
TP = 256
TS = 1024
N_IN = 36928
EPS = 1e-6
SEQS = [(0, 256, False), (256, 256, False), (512, 1024, True)]

C_GQ, C_GK, C_GV, C_GZ, C_GA = 0, 2048, 4096, 8192, 12288
C_MQ, C_MK, C_MV, C_MZ, C_MO, C_MI, C_MF, C_MG = 12320, 14368, 16416, 20512, 24608, 28704, 28720, 28736


class Buf:
    __slots__ = ("name", "w", "r")

    def __init__(self, name=""):
        self.name = name
        self.w = None
        self.r = {}


class DSem:
    def __init__(self, sem, name):
        self.sem = sem
        self.count = 0
        self.name = name


class Eng:
    def __init__(self, name, eng, sem, is_pe=False):
        self.name = name
        self.eng = eng
        self.sem = sem
        self.count = 0
        self.seen = {}
        self.ops = []
        self.is_pe = is_pe


class Prog:
    def __init__(self, nc, es):
        self.nc = nc
        self.es = es
        self.E = {}
        for name, eng in (("pe", nc.tensor), ("act", nc.scalar), ("dve", nc.vector),
                          ("pool", nc.gpsimd), ("sp", nc.sync)):
            sem = es.enter_context(nc.semaphore("s_" + name))
            self.E[name] = Eng(name, eng, sem, is_pe=(name == "pe"))
        self.dsems = []
        self.psum_banks = []
        self.psum_i = 0
        self.nbuf = 0

    def buf(self, name=""):
        self.nbuf += 1
        return Buf(name or f"b{self.nbuf}")

    def dsem(self, name):
        self.nbuf += 1
        sem = self.es.enter_context(self.nc.semaphore(f"d_{name}_{self.nbuf}"))
        d = DSem(sem, name)
        self.dsems.append(d)
        return d

    def init_psum(self):
        for i in range(8):
            t = self.es.enter_context(self.nc.psum_tensor(f"psb{i}", [128, 512], F32))
            self.psum_banks.append((t, self.buf(f"psb{i}")))

    def psum(self):
        t, b = self.psum_banks[self.psum_i % 8]
        self.psum_i += 1
        return t, b

    def _deps(self, reads, writes):
        deps = []
        for b in reads:
            if b.w is not None:
                deps.append(b.w)
        for b in writes:
            if b.w is not None:
                deps.append(b.w)
            deps.extend(b.r.values())
        return deps

    def _reduce(self, E, deps, skip_sem=None):
        need = {}
        for sem, val in deps:
            if sem is skip_sem:
                continue
            if E.is_pe and sem is E.sem:
                continue
            k = id(sem)
            if E.seen.get(k, 0) >= val:
                continue
            if k not in need or need[k][1] < val:
                need[k] = (sem, val)
        for k, (sem, val) in need.items():
            E.seen[k] = val
        return list(need.values())

    def _mark(self, tok, reads, writes):
        sem, val = tok
        k = id(sem)
        for b in reads:
            if k not in b.r or b.r[k][1] < val:
                b.r[k] = tok
        for b in writes:
            b.w = tok
            b.r = {}

    def op(self, eng, fn, reads=(), writes=(), signal=True):
        E = self.E[eng]
        if not E.is_pe:
            signal = True
        deps = self._deps(reads, writes)
        waits = self._reduce(E, deps)
        tok = (E.sem, E.count + 1)
        if signal:
            E.count += 1
        E.ops.append(("op", fn, waits, signal))
        self._mark(tok, reads, writes)
        return tok

    def dma(self, q, out, in_, ds, reads=(), writes=()):
        E = self.E[q]
        deps = []
        for b in reads:
            if b.w is not None:
                deps.append(b.w)
        for b in writes:
            if b.w is not None and b.w[0] is not ds.sem:
                deps.append(b.w)
            deps.extend(b.r.values())
        for sem, val in deps:
            assert not (sem is ds.sem and val >= ds.count + 16), "self-dependency on DMA semaphore"
        waits = self._reduce(E, deps)
        ds.count += 16
        tok = (ds.sem, ds.count)
        E.ops.append(("dma", (out, in_), waits, ds.sem))
        self._mark(tok, reads, writes)
        return tok

    def barrier(self):
        toks = [(e.sem, e.count) for e in self.E.values() if e.count > 0]
        toks += [(d.sem, d.count) for d in self.dsems if d.count > 0]
        for E in self.E.values():
            waits = self._reduce(E, [t for t in toks if not (t[0] is E.sem)])
            if waits:
                E.ops.append(("wait", None, waits, False))

    def emit(self):
        nc = self.nc
        for e in self.E.values():
            assert e.count < 65000, (e.name, e.count)
        for d in self.dsems:
            assert d.count < 65000, (d.name, d.count)

        def run(E, eng):
            for kind, payload, waits, sig in E.ops:
                for sem, val in waits:
                    eng.wait_ge(sem, val)
                if kind == "op":
                    ins = payload(eng)
                    if sig:
                        ins.then_inc(E.sem, 1)
                elif kind == "dma":
                    out, in_ = payload
                    eng.dma_start(out=out, in_=in_).then_inc(sig, 16)
            E.ops = []

        with nc.Block() as block:
            @block.tensor
            def _(eng):
                run(self.E["pe"], eng)

            @block.scalar
            def _(eng):
                run(self.E["act"], eng)

            @block.vector
            def _(eng):
                run(self.E["dve"], eng)

            @block.gpsimd
            def _(eng):
                run(self.E["pool"], eng)

            @block.sync
            def _(eng):
                run(self.E["sp"], eng)

    def mm(self, out, lhsT, rhs, start, stop, reads, writes):
        self.op("pe", lambda e: e.matmul(out, lhsT, rhs, start=start, stop=stop),
                reads, writes, signal=stop)

    def tr(self, out, in_, ident, reads, writes):
        self.op("pe", lambda e: e.transpose(out, in_, ident), reads, writes, signal=True)

    def act(self, out, in_, func, reads, writes, **kw):
        self.op("act", lambda e: e.activation(out=out, in_=in_, func=func, **kw), reads, writes)

    def ts(self, eng, out, in0, s1, s2, op0, op1, reads, writes):
        if s2 is None:
            self.op(eng, lambda e: e.tensor_scalar(out=out, in0=in0, scalar1=s1, scalar2=None, op0=op0),
                    reads, writes)
        else:
            self.op(eng, lambda e: e.tensor_scalar(out=out, in0=in0, scalar1=s1, scalar2=s2, op0=op0, op1=op1),
                    reads, writes)

    def tt(self, eng, out, in0, in1, op, reads, writes):
        self.op(eng, lambda e: e.tensor_tensor(out=out, in0=in0, in1=in1, op=op), reads, writes)

    def stt(self, out, in0, scalar, in1, op0, op1, reads, writes):
        self.op("dve", lambda e: e.scalar_tensor_tensor(out=out, in0=in0, scalar=scalar, in1=in1,
                                                        op0=op0, op1=op1), reads, writes)

    def cp(self, eng, out, in_, reads, writes):
        self.op(eng, lambda e: e.tensor_copy(out=out, in_=in_), reads, writes)

    def memset(self, eng, out, val, writes):
        self.op(eng, lambda e: e.memset(out, val), (), writes)


class Ring:
    def __init__(self, P, es, name, n, shape, dtype, dsem=False):
        self.slots = []
        for i in range(n):
            P.nbuf += 1
            t = es.enter_context(P.nc.sbuf_tensor(f"{name}{i}_{P.nbuf}", shape, dtype))
            self.slots.append((t, P.buf(f"{name}{i}"), P.dsem(f"{name}{i}") if dsem else None))
        self.i = 0

    def next(self):
        s = self.slots[self.i % len(self.slots)]
        self.i += 1
        return s


def bc_last(ap, n):
    return bass.AP(ap.tensor, ap.offset, [list(x) for x in ap.ap] + [[0, n]])


def build_program(debug=False, stop_after=99):
    nc = bass.Bass("TRN2", target_bir_lowering=False)

    def din(name, shape):
        return nc.dram_tensor(name, shape, F32, kind="ExternalInput").ap()

    def dout(name, shape):
        return nc.dram_tensor(name, shape, F32, kind="ExternalOutput").ap()

    dbg = {}

    def dscr(name, shape, dt=F32):
        if debug and name in debug:
            t = nc.dram_tensor(name, shape, dt, kind="ExternalOutput").ap()
            dbg[name] = t
            return t
        return nc.dram_tensor(name, shape, dt, kind="Internal").ap()

    x_all = din("x_all", [NTOK, D])
    cvec = din("cvec", [2, D])
    st_S = din("st_S", [2, 4, 512, 1024])
    st_C = din("st_C", [2, 8, 256, 512])
    st_n = din("st_n", [2, 8, 256])
    st_m = din("st_m", [2, 8])
    w_ada = din("w_ada", [D, 3 * D])
    b_ada = din("b_ada", [1, 3 * D])
    w_in = din("w_in", [D, N_IN])
    w_a2 = din("w_a2", [2, 16, 2048])
    b_a = din("b_a", [2, 2048])
    b_i = din("b_i", [2, 8])
    b_f = din("b_f", [2, 8])
    b_mg = din("b_mg", [1, 2 * D])
    g_nw = din("g_nw", [1, D])
    m_nw = din("m_nw", [1, D])
    w_gp = din("w_gp", [D, D])
    w_mp = din("w_mp", [D, D])
    w_o = din("w_o", [D, D])
    f_nw = din("f_nw", [1, D])
    y_all = dout("y_all", [NTOK, D])
    o_S = dout("o_S", [2, 2, 4, 512, 1024])
    o_C = dout("o_C", [2, 2, 8, 256, 512])
    o_n = dout("o_n", [2, 2, 8, 256])
    o_m = dout("o_m", [2, 2, 8])
    MODD = dscr("MODD", [2, 3 * D])
    QT = dscr("QT", [2048, NTOK], BF16)
    KT = dscr("KT", [2048, NTOK], BF16)
    VT = dscr("VT", [4096, NTOK], BF16)
    SZT = dscr("SZT", [4096, NTOK], BF16)
    GAT = dscr("GAT", [2, 16, NTOK])
    MQT = dscr("MQT", [2048, NTOK], BF16)
    MKT = dscr("MKT", [2048, NTOK], BF16)
    MVT = dscr("MVT", [4096, NTOK], BF16)
    SZM = dscr("SZM", [4096, NTOK], BF16)
    SGO = dscr("SGO", [4096, NTOK], BF16)
    GIF = dscr("GIF", [4, 8, NTOK])
    G01 = dscr("G01", [8192, NTOK], BF16)
    DECD = dscr("DECD", [1, 2 * 8 * 24])
    PARTG = dscr("PARTG", [NTOK, 4096])
    OG = dscr("OG", [NTOK, 4096])
    PARTM = dscr("PARTM", [NTOK, 4096])
    OM = dscr("OM", [NTOK, 4096])
    UT = dscr("UT", [4096, NTOK], BF16)
    UMT = dscr("UMT", [4096, NTOK], BF16)
    T1 = dscr("T1", [4096, NTOK])
    MIXT = dscr("MIXT", [4096, NTOK], BF16)
    OUTT = dscr("OUTT", [4096, NTOK])

    es = ExitStack()
    with es:
        P = Prog(nc, es)
        P.init_psum()
        es.enter_context(nc.allow_non_contiguous_dma(reason="small strided loads"))

        uniq = [0]

        def sb(stack, name, shape, dt=F32):
            uniq[0] += 1
            return stack.enter_context(nc.sbuf_tensor(f"{name}_{uniq[0]}", shape, dt))

        cb = P.buf("consts")
        ident = sb(es, "ident", [128, 128])
        identb = sb(es, "identb", [128, 128], BF16)
        tri = {}
        P.memset("pool", ident[:], 0.0, [cb])
        P.op("pool", lambda e: e.affine_select(out=ident[:], in_=ident[:], compare_op=ALU.not_equal, fill=1.0,
                                               base=0, pattern=[[-1, 128]], channel_multiplier=1), [cb], [cb])
        P.cp("pool", identb[:], ident[:], [cb], [cb])
        specs = {"U_incl": (-1, 1, ALU.is_ge), "L_incl": (1, -1, ALU.is_ge),
                 "L_strict": (1, -1, ALU.is_gt), "U_strict": (-1, 1, ALU.is_gt)}
        for nm, (cm, st, cmp_) in specs.items():
            t1 = sb(es, "m1_" + nm, [64, 64])
            t2 = sb(es, "m2_" + nm, [64, 64])
            P.memset("pool", t1[:], 1.0, [cb])

            def _sel(e, t1=t1, cm=cm, st=st, cmp_=cmp_):
                return e.affine_select(out=t1[:], in_=t1[:], compare_op=cmp_, fill=0.0, base=0,
                                       pattern=[[st, 64]], channel_multiplier=cm)
            P.op("pool", _sel, [cb], [cb])
            P.ts("pool", t2[:], t1[:], -1.0 / 16.0, None, ALU.mult, None, [cb], [cb])
            tri[nm] = (t1, t2)
        onesb = sb(es, "onesb", [128, 1], BF16)
        P.memset("pool", onesb[:], 1.0, [cb])
        ones8 = sb(es, "ones8", [8, 1024])
        P.memset("pool", ones8[:], 1.0, [cb])
        scale1T = sb(es, "scale1T", [128, 32, 2])
        shiftT = sb(es, "shiftT", [128, 32, 2])
        gateT = sb(es, "gateT", [128, 32, 2])
        modT_b = P.buf("modT")
        gnwT = sb(es, "gnwT", [128, 32])
        mnwT = sb(es, "mnwT", [128, 32])
        bmgT = sb(es, "bmgT", [128, 64])
        vec_b = P.buf("vecs")
        dv_ = P.dsem("vecs")
        P.dma("sp", gnwT[:], g_nw.rearrange("o (k p) -> p (o k)", p=128), dv_, writes=[vec_b])
        P.dma("sp", mnwT[:], m_nw.rearrange("o (k p) -> p (o k)", p=128), dv_, writes=[vec_b])
        P.dma("sp", bmgT[:], b_mg.rearrange("o (k p) -> p (o k)", p=128), dv_, writes=[vec_b])

        with ExitStack() as ph:
            c_sb = sb(ph, "c_sb", [2, D]); c_b = P.buf()
            d0 = P.dsem("p0c")
            P.dma("sp", c_sb[:], cvec[:, :], d0, writes=[c_b])
            P.act(c_sb[:], c_sb[:], AF.Silu, [c_b], [c_b])
            scT = sb(ph, "scT", [128, 32, 2]); scT_b = P.buf()
            ps, pb = P.psum()
            for k in range(32):
                P.tr(ps[:, 2 * k:2 * k + 2], c_sb[:, k * 128:(k + 1) * 128], ident[0:2, 0:2], [c_b, cb], [pb])
            P.cp("dve", scT[:].rearrange("p k j -> p (k j)"), ps[:, 0:64], [pb], [scT_b])
            mod_sb = sb(ph, "mod_sb", [2, 3 * D]); mod_b = P.buf()
            bada = sb(ph, "bada", [2, 3 * D]); bada_b = P.buf()
            d1 = P.dsem("p0b")
            P.dma("sp", bada[:], b_ada.partition_broadcast(2)[:, 0, :], d1, writes=[bada_b])
            wr = Ring(P, ph, "wada", 2, [128, 32, 256], F32, dsem=True)
            for nb in range(48):
                wt, wb_, wd = wr.next()
                P.dma("sp", wt[:], w_ada[:, nb * 256:(nb + 1) * 256].rearrange("(k p) n -> p k n", p=128), wd,
                      writes=[wb_])
                ps, pb = P.psum()
                for k in range(32):
                    P.mm(ps[0:2, 0:256], scT[:, k, :], wt[:, k, :], k == 0, k == 31, [scT_b, wb_], [pb])
                P.tt("dve", mod_sb[:, nb * 256:(nb + 1) * 256], ps[0:2, 0:256], bada[:, nb * 256:(nb + 1) * 256],
                     ALU.add, [pb, bada_b], [mod_b])
            P.ts("dve", mod_sb[:, D:2 * D], mod_sb[:, D:2 * D], 1.0, None, ALU.add, None, [mod_b], [mod_b])
            dm = P.dsem("p0m")
            modd_b = P.buf("MODD")
            P.dma("sp", MODD[:, :], mod_sb[:], dm, reads=[mod_b], writes=[modd_b])
            for part, dst in ((0, shiftT), (1, scale1T), (2, gateT)):
                ps, pb = P.psum()
                for k in range(32):
                    P.tr(ps[:, 2 * k:2 * k + 2], mod_sb[:, part * D + k * 128: part * D + (k + 1) * 128],
                         ident[0:2, 0:2], [mod_b, cb], [pb])
                P.cp("dve", dst[:].rearrange("p k j -> p (k j)"), ps[:, 0:64], [pb], [modT_b])
            P.barrier()
            P.emit()
        if stop_after <= 0:
            return nc, dbg

        with ExitStack() as ph:
            hT = sb(ph, "hT", [128, 32, NTOK], BF16); hT_b = P.buf("hT")
            with ExitStack() as p1:
                xr = Ring(P, p1, "xin", 2, [128, D], F32, dsem=True)
                junk = sb(p1, "junk", [128, D], BF16); junk_b = P.buf()
                st_r = Ring(P, p1, "stat", 2, [128, 4], F32)
                for i in range(12):
                    j = 0 if i < 4 else 1
                    xt, xb, xd = xr.next()
                    P.dma("sp", xt[:], x_all[i * 128:(i + 1) * 128, :], xd, writes=[xb])
                    stt_, sbf, _ = st_r.next()
                    P.act(junk[:], xt[:], AF.Square, [xb], [junk_b, sbf], accum_out=stt_[:, 0:1])
                    P.ts("dve", stt_[:, 1:2], stt_[:, 0:1], 1.0 / D, EPS, ALU.mult, ALU.add, [sbf], [sbf])
                    P.act(stt_[:, 2:3], stt_[:, 1:2], AF.Sqrt, [sbf], [sbf])
                    P.op("dve", lambda e, o=stt_[:, 3:4], i_=stt_[:, 2:3]: e.reciprocal(out=o, in_=i_), [sbf], [sbf])
                    P.ts("dve", xt[:], xt[:], stt_[:, 3:4], None, ALU.mult, None, [xb, sbf], [xb])
                    for g in range(8):
                        ps, pb = P.psum()
                        for kk in range(4):
                            k = g * 4 + kk
                            P.tr(ps[:, kk * 128:(kk + 1) * 128], xt[:, k * 128:(k + 1) * 128], ident[:], [xb, cb], [pb])
                        for kk in range(4):
                            k = g * 4 + kk
                            if kk % 2 == 0:
                                P.ts("dve", hT[:, k, i * 128:(i + 1) * 128], ps[:, kk * 128:(kk + 1) * 128],
                                     scale1T[:, k, j:j + 1], shiftT[:, k, j:j + 1], ALU.mult, ALU.add,
                                     [pb, modT_b], [hT_b])
                            else:
                                P.act(hT[:, k, i * 128:(i + 1) * 128], ps[:, kk * 128:(kk + 1) * 128], AF.Identity,
                                      [pb, modT_b], [hT_b], scale=scale1T[:, k, j:j + 1], bias=shiftT[:, k, j:j + 1])
                P.barrier()
                P.emit()

            with ExitStack() as p2:
                wst = Ring(P, p2, "wst", 3, [128, 32, 128], F32, dsem=True)
                wbf = Ring(P, p2, "wbf", 2, [128, 32, 128], BF16)
                stg = Ring(P, p2, "stg", 2, [128, NTOK], BF16, dsem=True)
                stg32 = Ring(P, p2, "stg32", 1, [16, NTOK], F32, dsem=True)
                scr_b = P.buf("scratch_p2")

                def perm_views(out_t, ps, tb, nrows):
                    r0 = (tb - 1) * 8
                    o = out_t[0:nrows, 512:1536].rearrange("p (c r) -> p c r", r=16)[:, :, r0:r0 + 8]
                    i_ = ps[0:nrows, 0:512].rearrange("p (r c) -> p c r", c=64)
                    return o, i_

                def evac(kind, out_t, out_b, ps, pb, tb, nrows, bias_ap=None, scale=1.0, perm=False):
                    if perm and tb > 0:
                        o, i_ = perm_views(out_t, ps, tb, nrows)
                    else:
                        o, i_ = out_t[0:nrows, tb * 512:(tb + 1) * 512], ps[0:nrows, 0:512]
                    if kind == "copy":
                        if scale != 1.0:
                            P.ts("dve", o, i_, scale, None, ALU.mult, None, [pb], [out_b])
                        else:
                            P.cp("dve", o, i_, [pb], [out_b])
                    elif kind == "silu":
                        P.act(o, i_, AF.Silu, [pb], [out_b])
                    elif kind == "sigmoid":
                        if bias_ap is not None:
                            P.act(o, i_, AF.Sigmoid, [pb, vec_b], [out_b], bias=bias_ap)
                        else:
                            P.act(o, i_, AF.Sigmoid, [pb], [out_b])

                sections = [
                    (C_GQ, 16, QT, "copy", 512 ** -0.5, False),
                    (C_GK, 16, KT, "copy", 1.0, False),
                    (C_GV, 32, VT, "copy", 1.0, False),
                    (C_GZ, 32, SZT, "silu", 1.0, False),
                    (C_MQ, 16, MQT, "copy", 256 ** -0.5, True),
                    (C_MK, 16, MKT, "copy", 1.0, True),
                    (C_MV, 32, MVT, "copy", 1.0, True),
                    (C_MZ, 32, SZM, "silu", 1.0, False),
                    (C_MO, 32, SGO, "sigmoid", 1.0, False),
                    (C_MG, 64, G01, "sigmoidb", 1.0, False),
                ]
                blocks = []
                for (c0, nblk, dst, kind, scale, perm) in sections:
                    for b in range(nblk):
                        blocks.append(("reg", c0 + b * 128, dst, b, kind, scale, perm))
                blocks.append(("ga", C_GA, None, 0, None, 1.0, False))
                blocks.append(("gif", C_MI, None, 0, None, 1.0, True))
                nblk_total = len(blocks)

                def issue_load(bi):
                    typ, c0 = blocks[bi][0], blocks[bi][1]
                    wt, wb_, wd = wst.next()
                    ncol = 128 if typ == "reg" else 32
                    P.dma("sp", wt[:, :, 0:ncol], w_in[:, c0:c0 + ncol].rearrange("(k p) n -> p k n", p=128), wd,
                          writes=[wb_])
                    return (wt, wb_)

                loaded = {}
                loaded[0] = issue_load(0)
                loaded[1] = issue_load(1)
                casted = {}

                def do_cast(bi):
                    wt, wb_ = loaded.pop(bi)
                    wbt, wbb, _ = wbf.next()
                    ncol = 128 if blocks[bi][0] == "reg" else 32
                    P.cp("dve", wbt[:, 0:16, 0:ncol], wt[:, 0:16, 0:ncol], [wb_], [wbb])
                    P.cp("pool", wbt[:, 16:32, 0:ncol], wt[:, 16:32, 0:ncol], [wb_], [wbb])
                    casted[bi] = (wbt, wbb)
                do_cast(0)
                for bi, (typ, c0, dst, b, kind, scale, perm) in enumerate(blocks):
                    if bi + 2 < nblk_total:
                        loaded[bi + 2] = issue_load(bi + 2)
                    if bi + 1 < nblk_total:
                        do_cast(bi + 1)
                    wbt, wbb = casted.pop(bi)
                    if typ == "reg":
                        st_t, st_b, st_d = stg.next()
                        for tb in range(3):
                            ps, pb = P.psum()
                            for k in range(32):
                                P.mm(ps[:, 0:512], wbt[:, k, :], hT[:, k, tb * 512:(tb + 1) * 512], k == 0, k == 31,
                                     [wbb, hT_b], [pb])
                            if kind == "sigmoidb":
                                evac("sigmoid", st_t, st_b, ps, pb, tb, 128, bias_ap=bmgT[:, b:b + 1])
                            else:
                                evac(kind, st_t, st_b, ps, pb, tb, 128, scale=scale, perm=perm)
                        P.dma("sp", dst[b * 128:(b + 1) * 128, :], st_t[:], st_d, reads=[st_b], writes=[scr_b])
                    elif typ == "ga":
                        for dr in range(2):
                            st_t, st_b, st_d = stg32.next()
                            for tb in range(3):
                                ps, pb = P.psum()
                                for k in range(32):
                                    P.mm(ps[0:16, 0:512], wbt[:, k, dr * 16:(dr + 1) * 16],
                                         hT[:, k, tb * 512:(tb + 1) * 512], k == 0, k == 31, [wbb, hT_b], [pb])
                                evac("copy", st_t, st_b, ps, pb, tb, 16)
                            P.dma("sp", GAT[dr, :, :], st_t[0:16, :], st_d, reads=[st_b], writes=[scr_b])
                    else:
                        for q in range(4):
                            st_t, st_b, st_d = stg32.next()
                            for tb in range(3):
                                ps, pb = P.psum()
                                for k in range(32):
                                    P.mm(ps[0:8, 0:512], wbt[:, k, q * 8:(q + 1) * 8],
                                         hT[:, k, tb * 512:(tb + 1) * 512], k == 0, k == 31, [wbb, hT_b], [pb])
                                evac("copy", st_t, st_b, ps, pb, tb, 8, perm=True)
                            P.dma("sp", GIF[q, :, :], st_t[0:8, :], st_d, reads=[st_b], writes=[scr_b])
                P.barrier()
                P.emit()
        if stop_after <= 2:
            return nc, dbg

        with ExitStack() as ph:
            gaA = [sb(ph, f"gaA{d_}", [17, NTOK]) for d_ in range(2)]
            w2A = [sb(ph, f"w2A{d_}", [17, 2048]) for d_ in range(2)]
            ga_b = P.buf("gaA")
            dga = P.dsem("gaA")
            for d_ in range(2):
                P.memset("pool", gaA[d_][:], 1.0, [ga_b])
            for d_ in range(2):
                P.dma("sp", gaA[d_][0:16, :], GAT[d_, :, :], dga, writes=[ga_b])
                P.dma("sp", w2A[d_][0:16, :], w_a2[d_, :, :], dga, writes=[ga_b])
                P.dma("sp", w2A[d_][16:17, :], b_a[d_:d_ + 1, :], dga, writes=[ga_b])

            WT = sb(ph, "WT", [64, 24, 32]); WT_b = P.buf("WT")
            DEC = sb(ph, "DEC", [128, 384]); DEC_b = P.buf("DEC")
            decd_b = P.buf("DECD")
            with ExitStack() as p3b:
                bi_sb = sb(p3b, "bi_sb", [8, 2]); bf_sb = sb(p3b, "bf_sb", [8, 2]); nbf_sb = sb(p3b, "nbf_sb", [8, 2])
                m0_sb = sb(p3b, "m0_sb", [8, 2]); nm0_sb = sb(p3b, "nm0_sb", [8, 2])
                gb = P.buf("gbias")
                dgb = P.dsem("gbias")
                P.dma("sp", bi_sb[:], b_i.rearrange("d h -> h d"), dgb, writes=[gb])
                P.dma("sp", bf_sb[:], b_f.rearrange("d h -> h d"), dgb, writes=[gb])
                P.dma("sp", m0_sb[:], st_m.rearrange("d h -> h d"), dgb, writes=[gb])
                P.ts("dve", nbf_sb[:], bf_sb[:], -1.0, None, ALU.mult, None, [gb], [gb])
                P.ts("dve", nm0_sb[:], m0_sb[:], -1.0, None, ALU.mult, None, [gb], [gb])
                gr = Ring(P, p3b, "graw", 2, [8, 2, 1024], F32, dsem=True)
                tmpr = {nm: Ring(P, p3b, "g_" + nm, 2, [8, 1024], F32) for nm in
                        ("ig", "lf", "B", "m", "A", "w", "t")}
                smr = Ring(P, p3b, "gsm", 2, [8, 64], F32)
                dec_st = sb(p3b, "dec_st", [8, 2, 24]); dec_stb = P.buf()
                mo_d = [P.dsem("mout0"), P.dsem("mout1")]
                gch = 0
                for si, (tok0, T, is_s) in enumerate(SEQS):
                    nch = T // 64
                    wq = {}
                    for dr in range(2):
                        rv = (lambda a: a[:, ::-1]) if dr == 1 else (lambda a: a)
                        raw, rb, rd = gr.next()
                        P.dma("sp", raw[:, 0, 0:T], GIF[dr, :, tok0:tok0 + T], rd, writes=[rb])
                        P.dma("sp", raw[:, 1, 0:T], GIF[2 + dr, :, tok0:tok0 + T], rd, writes=[rb])
                        ig, igb, _ = tmpr["ig"].next(); lf, lfb, _ = tmpr["lf"].next()
                        Bt, Bb, _ = tmpr["B"].next(); mt, mb, _ = tmpr["m"].next()
                        At, Ab, _ = tmpr["A"].next(); wt_, wtb, _ = tmpr["w"].next(); tt_, ttb, _ = tmpr["t"].next()
                        sm, smb, _ = smr.next()
                        P.ts("dve", ig[:, 0:T], raw[:, 0, 0:T], bi_sb[:, dr:dr + 1], None, ALU.add, None, [rb, gb], [igb])
                        P.act(lf[:, 0:T], raw[:, 1, 0:T], AF.Exp, [rb, gb], [lfb], scale=-1.0, bias=nbf_sb[:, dr:dr + 1])
                        P.act(lf[:, 0:T], lf[:, 0:T], AF.Ln, [lfb], [lfb], bias=1.0)
                        P.ts("dve", lf[:, 0:T], lf[:, 0:T], -1.0, None, ALU.mult, None, [lfb], [lfb])
                        P.op("dve", lambda e, o=rv(Bt[:, 0:T]), a=rv(ones8[:, 0:T]), b_=rv(lf[:, 0:T]):
                             e.tensor_tensor_scan(out=o, data0=a, data1=b_, initial=0.0, op0=ALU.mult, op1=ALU.add),
                             [lfb, cb], [Bb])
                        init = m0_sb[:, dr:dr + 1] if is_s else 0.0
                        P.op("dve", lambda e, o=rv(mt[:, 0:T]), a=rv(lf[:, 0:T]), b_=rv(ig[:, 0:T]), init=init:
                             e.tensor_tensor_scan(out=o, data0=a, data1=b_, initial=init, op0=ALU.add, op1=ALU.max),
                             [lfb, igb, gb], [mb])
                        last = 63 if dr == 0 else 0
                        Bv = Bt[:, 0:T].rearrange("p (c s) -> p c s", s=64)[:, :, last]
                        mv_ = mt[:, 0:T].rearrange("p (c s) -> p c s", s=64)[:, :, last]
                        Z = sm[:, 0:nch]; Zp = sm[:, 16:16 + nch]; dc = sm[:, 32:32 + nch]
                        P.tt("dve", Z, Bv, mv_, ALU.subtract, [Bb, mb], [smb])
                        if dr == 0:
                            if nch > 1:
                                P.cp("dve", sm[:, 17:16 + nch], sm[:, 0:nch - 1], [smb], [smb])
                            first = sm[:, 16:17]
                        else:
                            if nch > 1:
                                P.cp("dve", sm[:, 16:16 + nch - 1], sm[:, 1:nch], [smb], [smb])
                            first = sm[:, 16 + nch - 1:16 + nch]
                        if is_s:
                            P.cp("dve", first, nm0_sb[:, dr:dr + 1], [gb, smb], [smb])
                        else:
                            P.memset("dve", first, 0.0, [smb])
                        P.tt("dve", dc, Z, Zp, ALU.subtract, [smb], [smb])
                        P.act(dec_st[:, dr, gch:gch + nch], dc, AF.Exp, [smb], [dec_stb])
                        P.tt("dve", At[:, 0:T], ig[:, 0:T], Bt[:, 0:T], ALU.subtract, [igb, Bb], [Ab])
                        Zbc = bc_last(Z, 64)
                        P.tt("dve", wt_[:, 0:T].rearrange("p (c s) -> p c s", s=64),
                             At[:, 0:T].rearrange("p (c s) -> p c s", s=64), Zbc, ALU.add, [Ab, smb], [wtb])
                        P.act(wt_[:, 0:T], wt_[:, 0:T], AF.Exp, [wtb], [wtb])
                        P.tt("dve", tt_[:, 0:T].rearrange("p (c s) -> p c s", s=64), Zbc,
                             Bt[:, 0:T].rearrange("p (c s) -> p c s", s=64), ALU.subtract, [Bb, smb], [ttb])
                        P.act(tt_[:, 0:T], tt_[:, 0:T], AF.Exp, [ttb], [ttb])
                        wq[dr] = (wt_, wtb, tt_, ttb)
                        if not is_s:
                            fin = mt[:, T - 1:T] if dr == 0 else mt[:, 0:1]
                            P.dma("sp", o_m[si, dr:dr + 1, :].rearrange("d h -> h d"), fin, mo_d[dr], reads=[mb])
                    for c in range(nch):
                        ps, pb = P.psum()
                        for dr in range(2):
                            wt_, wtb, tt_, ttb = wq[dr]
                            P.mm(ps[0:64, dr * 8:dr * 8 + 8], wt_[:, c * 64:(c + 1) * 64], ident[0:8, 0:8], True, True,
                                 [wtb, cb], [pb])
                            P.mm(ps[0:64, 16 + dr * 8:24 + dr * 8], tt_[:, c * 64:(c + 1) * 64], ident[0:8, 0:8], True, True,
                                 [ttb, cb], [pb])
                        P.cp("dve", WT[:, gch + c, :], ps[0:64, 0:32], [pb], [WT_b])
                    gch += nch
                ddec = P.dsem("decd")
                P.dma("sp", DECD.rearrange("o (d h c) -> h (o d) c", d=2, h=8), dec_st[:], ddec,
                      reads=[dec_stb], writes=[decd_b])
                ddec2 = P.dsem("decd2")
                P.dma("sp", DEC[:], DECD.partition_broadcast(128)[:, 0, :], ddec2,
                      reads=[decd_b], writes=[DEC_b])
                P.barrier()
                P.emit()

            def scan_units(kind):
                gla = kind == "gla"
                NH = 4 if gla else 8
                NJ = 4 if gla else 2
                DK = 128 * NJ
                DV = 1024 if gla else 512
                NV = DV // 128
                qsrc, ksrc, vsrc = (QT, KT, VT) if gla else (MQT, MKT, MVT)
                PART, OUT = (PARTG, OG) if gla else (PARTM, OM)
                with ExitStack() as sc:
                    qT = sb(sc, "u_qT", [128, NJ, 1024], BF16); kT = sb(sc, "u_kT", [128, NJ, 1024], BF16)
                    vT = sb(sc, "u_vT", [128, NV, 1024], BF16)
                    u_b = P.buf("unit"); u_d = P.dsem("unit")
                    ktr = Ring(P, sc, "ktok", 4, [64, DK], BF16); vtr = Ring(P, sc, "vtok", 4, [64, DV], BF16)
                    S = [sb(sc, f"S{d_}", [128, NJ, DV]) for d_ in range(2)]
                    Sb = [sb(sc, f"Sb{d_}", [128, NJ, DV], BF16) for d_ in range(2)]
                    S_b = [P.buf(f"S{d_}") for d_ in range(2)]
                    Sb_b = [P.buf(f"Sb{d_}") for d_ in range(2)]
                    S_d = [P.dsem(f"S{d_}") for d_ in range(2)]
                    if not gla:
                        nst = [sb(sc, f"nst{d_}", [128, 2]) for d_ in range(2)]
                        nbt = [sb(sc, f"nbt{d_}", [128, 2], BF16) for d_ in range(2)]
                    so_d = [P.dsem("stout0"), P.dsem("stout1")]
                    osb = Ring(P, sc, "osb", 3, [64, DV], F32, dsem=True)
                    prt = Ring(P, sc, "prt", 2, [64, DV], F32, dsem=True)
                    if gla:
                        e1r = Ring(P, sc, "e1", 2, [64, 512], F32)
                        lar = Ring(P, sc, "la", 2, [64, 512], F32)
                        epr = Ring(P, sc, "ep", 4, [128, 256], F32)
                        enr = Ring(P, sc, "en", 2, [128, 256], F32)
                        err = Ring(P, sc, "er", 2, [64, 512], F32)
                    qdr = Ring(P, sc, "qd", 4, [128, NJ, 64], BF16)
                    kir = Ring(P, sc, "ki", 4, [128, NJ, 64], BF16)
                    ker = Ring(P, sc, "ke", 4, [64, DK], BF16)
                    atr = Ring(P, sc, "at", 4, [64, 64], BF16)
                    smr2 = Ring(P, sc, "dn", 2, [64, 4], F32)
                    part_bufs = {}
                    gch0 = 0
                    for si, (tok0, T, is_s) in enumerate(SEQS):
                        nch = T // 64
                        for hd in range(NH):
                            P.dma("sp", qT[:, :, 0:T], qsrc[hd * DK:(hd + 1) * DK, tok0:tok0 + T].rearrange(
                                "(j p) t -> p j t", p=128), u_d, writes=[u_b])
                            P.dma("sp", kT[:, :, 0:T], ksrc[hd * DK:(hd + 1) * DK, tok0:tok0 + T].rearrange(
                                "(j p) t -> p j t", p=128), u_d, writes=[u_b])
                            P.dma("sp", vT[:, :, 0:T], vsrc[hd * DV:(hd + 1) * DV, tok0:tok0 + T].rearrange(
                                "(j p) t -> p j t", p=128), u_d, writes=[u_b])
                            for dr in range(2):
                                if is_s:
                                    src = (st_S if gla else st_C)[dr, hd, :, :].rearrange("(j p) e -> p j e", p=128)
                                    P.dma("sp", S[dr][:], src, S_d[dr], writes=[S_b[dr]])
                                    if not gla:
                                        P.dma("sp", nst[dr][:], st_n[dr, hd, :].rearrange("(j p) -> p j", p=128), S_d[dr],
                                              writes=[S_b[dr]])
                                else:
                                    P.memset("pool", S[dr][:], 0.0, [S_b[dr]])
                                    if not gla:
                                        P.memset("pool", nst[dr][:], 0.0, [S_b[dr]])
                                if gla:
                                    P.cp("pool", Sb[dr][:], S[dr][:], [S_b[dr]], [Sb_b[dr]])
                            def stageA(it, dr):
                                c = it if dr == 0 else nch - 1 - it
                                second = it >= nch // 2
                                t0 = c * 64
                                g0 = tok0 + t0
                                mask = tri["U_incl" if dr == 0 else "L_incl"][0]
                                ktok, ktb, _ = ktr.next(); vtok, vtb, _ = vtr.next()
                                ps, pb = P.psum()
                                for j in range(NJ):
                                    P.mm(ps[0:64, j * 128:(j + 1) * 128], kT[:, j, t0:t0 + 64], identb[:], True, True,
                                         [u_b, cb], [pb])
                                P.cp("dve", ktok[:], ps[0:64, 0:DK], [pb], [ktb])
                                for h2 in range(DV // 512):
                                    ps, pb = P.psum()
                                    for j in range(4):
                                        P.mm(ps[0:64, j * 128:(j + 1) * 128], vT[:, h2 * 4 + j, t0:t0 + 64], identb[:],
                                             True, True, [u_b, cb], [pb])
                                    P.act(vtok[:, h2 * 512:(h2 + 1) * 512], ps[0:64, 0:512], AF.Copy, [pb], [vtb])
                                if gla:
                                    ps, pb = P.psum()
                                    P.mm(ps[0:64, 0:512], gaA[dr][:, g0:g0 + 64], w2A[dr][:, hd * 512:(hd + 1) * 512],
                                         True, True, [ga_b], [pb])
                                    e1, e1b, _ = e1r.next(); la, lab, _ = lar.next()
                                    P.act(e1[:], ps[0:64, 0:512], AF.Exp, [pb], [e1b], scale=-1.0)
                                    P.act(la[:], e1[:], AF.Ln, [e1b], [lab], bias=1.0)
                                    tric = tri["U_incl" if dr == 0 else "L_incl"][1]
                                    tris = tri["L_strict" if dr == 0 else "U_strict"][1]
                                    psc, pcb = P.psum()
                                    for j in range(4):
                                        P.mm(psc[:, j * 64:(j + 1) * 64], la[:, j * 128:(j + 1) * 128], tric[:], True, True,
                                             [lab, cb], [pcb])
                                    psr, prb = P.psum()
                                    P.mm(psr[0:64, 0:512], tris[:], la[:], True, True, [lab, cb], [prb])
                                    ep, epb, _ = epr.next(); en, enb, _ = enr.next(); er, erb, _ = err.next()
                                    P.act(ep[:], psc[:, 0:256], AF.Exp, [pcb], [epb])
                                    P.act(en[:], psc[:, 0:256], AF.Exp, [pcb], [enb], scale=-1.0)
                                    P.act(er[:], psr[0:64, 0:512], AF.Exp, [prb], [erb])
                                    qd, qdb, _ = qdr.next(); ki, kib, _ = kir.next(); ke, keb, _ = ker.next()
                                    P.tt("dve", qd[:], qT[:, :, t0:t0 + 64], ep[:].rearrange("p (j t) -> p j t", t=64),
                                         ALU.mult, [u_b, epb], [qdb])
                                    P.tt("dve", ki[:], kT[:, :, t0:t0 + 64], en[:].rearrange("p (j t) -> p j t", t=64),
                                         ALU.mult, [u_b, enb], [kib])
                                    P.tt("dve", ke[:], ktok[:], er[:], ALU.mult, [ktb, erb], [keb])
                                    q_ap = [qd[:, j, :] for j in range(NJ)]; q_bufs = [qdb]
                                    k_ap = [ki[:, j, :] for j in range(NJ)]; k_bufs = [kib]
                                    last = 63 if dr == 0 else 0
                                    rowsc = [ep[:, j * 64 + last:j * 64 + last + 1] for j in range(NJ)]
                                    rowsc_bufs = [epb]
                                    wcol = None
                                else:
                                    gc = gch0 + c
                                    wcol = WT[:, gc, dr * 8 + hd:dr * 8 + hd + 1]
                                    tcol = WT[:, gc, 16 + dr * 8 + hd:16 + dr * 8 + hd + 1]
                                    di = (dr * 8 + hd) * 24 + gc
                                    dcol = DEC[:, di:di + 1]
                                    ke, keb, _ = ker.next()
                                    P.ts("dve", ke[:], ktok[:], wcol, None, ALU.mult, None, [ktb, WT_b], [keb])
                                    q_ap = [qT[:, j, t0:t0 + 64] for j in range(NJ)]; q_bufs = [u_b]
                                    k_ap = [kT[:, j, t0:t0 + 64] for j in range(NJ)]; k_bufs = [u_b]
                                    rowsc = [dcol] * NJ
                                    rowsc_bufs = [DEC_b]
                                psa, pab = P.psum()
                                for j in range(NJ):
                                    P.mm(psa[0:64, 0:64], k_ap[j], q_ap[j], j == 0, j == NJ - 1, k_bufs + q_bufs, [pab])
                                at, atb, _ = atr.next()
                                if gla:
                                    P.tt("dve", at[:], psa[0:64, 0:64], mask[:], ALU.mult, [pab, cb], [atb])
                                else:
                                    P.stt(at[:], psa[0:64, 0:64], wcol, mask[:], ALU.mult, ALU.mult, [pab, cb, WT_b], [atb])
                                return dict(it=it, dr=dr, c=c, second=second, t0=t0, g0=g0, ktok=ktok, ktb=ktb, vtok=vtok, vtb=vtb,
                                            q_ap=q_ap, q_bufs=q_bufs, rowsc=rowsc, rowsc_bufs=rowsc_bufs, ke=ke, keb=keb, at=at, atb=atb,
                                            dcol=(None if gla else dcol), tcol=(None if gla else tcol))

                            def stageB(x):
                                it, dr, c, second, t0, g0 = x["it"], x["dr"], x["c"], x["second"], x["t0"], x["g0"]
                                ktok, ktb, vtok, vtb = x["ktok"], x["ktb"], x["vtok"], x["vtb"]
                                q_ap, q_bufs, rowsc, rowsc_bufs = x["q_ap"], x["q_bufs"], x["rowsc"], x["rowsc_bufs"]
                                ke, keb, at, atb, dcol, tcol = x["ke"], x["keb"], x["at"], x["atb"], x["dcol"], x["tcol"]
                                if not gla:
                                    P.act(Sb[dr][:], S[dr][:], AF.Copy, [S_b[dr], DEC_b], [Sb_b[dr]], scale=dcol)
                                    P.act(nbt[dr][:], nst[dr][:], AF.Copy, [S_b[dr], DEC_b], [Sb_b[dr]], scale=dcol)
                                ot, otb, otd = osb.next()
                                if second:
                                    pt, ptb, ptd = prt.next()
                                    pbuf = part_bufs[(si, hd, c)]
                                    P.dma("sp", pt[:], PART[g0:g0 + 64, hd * DV:(hd + 1) * DV], ptd, reads=[pbuf], writes=[ptb])
                                if not gla:
                                    psd, pdb = P.psum()
                                    P.mm(psd[0:64, 0:1], at[:], onesb[0:64, :], True, False, [atb, cb], [pdb])
                                    for j in range(NJ):
                                        P.mm(psd[0:64, 0:1], q_ap[j], nbt[dr][:, j:j + 1], False, j == NJ - 1,
                                             q_bufs + [Sb_b[dr]], [pdb])
                                    dn, dnb, _ = smr2.next()
                                    P.act(dn[:, 0:1], psd[0:64, 0:1], AF.Abs, [pdb], [dnb])
                                    P.ts("dve", dn[:, 1:2], dn[:, 0:1], tcol, None, ALU.max, None, [dnb, WT_b], [dnb])
                                    P.op("dve", lambda e, o=dn[:, 2:3], i_=dn[:, 1:2]: e.reciprocal(out=o, in_=i_), [dnb], [dnb])
                                for h2 in range(DV // 512):
                                    pso, pob = P.psum()
                                    P.mm(pso[0:64, 0:512], at[:], vtok[:, h2 * 512:(h2 + 1) * 512], True, False,
                                         [atb, vtb], [pob])
                                    for j in range(NJ):
                                        P.mm(pso[0:64, 0:512], q_ap[j], Sb[dr][:, j, h2 * 512:(h2 + 1) * 512], False, j == NJ - 1,
                                             q_bufs + [Sb_b[dr]], [pob])
                                    osl = ot[:, h2 * 512:(h2 + 1) * 512]
                                    if gla:
                                        if second:
                                            P.tt("dve", osl, pso[0:64, 0:512], pt[:, h2 * 512:(h2 + 1) * 512], ALU.add, [pob, ptb], [otb])
                                        else:
                                            P.act(osl, pso[0:64, 0:512], AF.Copy, [pob], [otb])
                                    else:
                                        if second:
                                            P.stt(osl, pso[0:64, 0:512], dn[:, 2:3], pt[:, h2 * 512:(h2 + 1) * 512], ALU.mult, ALU.add,
                                                  [pob, ptb, dnb], [otb])
                                        else:
                                            P.ts("dve", osl, pso[0:64, 0:512], dn[:, 2:3], None, ALU.mult, None, [pob, dnb], [otb])
                                if second:
                                    if (not gla) and is_s:
                                        dst3 = OUT[tok0:tok0 + T, hd * DV:(hd + 1) * DV].rearrange("(r c) e -> c r e", c=64)
                                        for cl in range(4):
                                            P.dma("sp", dst3[4 * c + cl, :, :], ot[cl * 16:(cl + 1) * 16, :], otd, reads=[otb])
                                    else:
                                        P.dma("sp", OUT[g0:g0 + 64, hd * DV:(hd + 1) * DV], ot[:], otd, reads=[otb])
                                else:
                                    pbuf = P.buf()
                                    part_bufs[(si, hd, c)] = pbuf
                                    P.dma("sp", PART[g0:g0 + 64, hd * DV:(hd + 1) * DV], ot[:], otd, reads=[otb], writes=[pbuf])
                                for j in range(NJ):
                                    for h2 in range(DV // 512):
                                        pss, psb_ = P.psum()
                                        P.mm(pss[:, 0:512], ke[:, j * 128:(j + 1) * 128], vtok[:, h2 * 512:(h2 + 1) * 512], True, True,
                                             [keb, vtb], [psb_])
                                        ssl = S[dr][:, j, h2 * 512:(h2 + 1) * 512]
                                        P.stt(ssl, ssl, rowsc[j], pss[:, 0:512], ALU.mult, ALU.add,
                                              [psb_, S_b[dr], Sb_b[dr]] + rowsc_bufs, [S_b[dr]])
                                    if gla:
                                        P.act(Sb[dr][:, j, :], S[dr][:, j, :], AF.Copy, [S_b[dr]], [Sb_b[dr]])
                                if not gla:
                                    psn, pnb = P.psum()
                                    for j in range(NJ):
                                        P.mm(psn[:, j:j + 1], ke[:, j * 128:(j + 1) * 128], onesb[0:64, :], True, True,
                                             [keb, cb], [pnb])
                                    P.stt(nst[dr][:], nst[dr][:], dcol, psn[:, 0:2], ALU.mult, ALU.add,
                                          [pnb, S_b[dr], Sb_b[dr], DEC_b], [S_b[dr]])

                            pend = {}
                            for it in range(nch + 1):
                                if it < nch:
                                    for dr in range(2):
                                        pend[(it, dr)] = stageA(it, dr)
                                if it >= 1:
                                    for dr in range(2):
                                        stageB(pend.pop((it - 1, dr)))
                            if not is_s:
                                for dr in range(2):
                                    dstS = (o_S if gla else o_C)[si, dr, hd, :, :].rearrange("(j p) e -> p j e", p=128)
                                    P.dma("sp", dstS, S[dr][:], so_d[dr], reads=[S_b[dr]])
                                    if not gla:
                                        P.dma("sp", o_n[si, dr, hd, :].rearrange("(j p) -> p j", p=128), nst[dr][:], so_d[dr],
                                              reads=[S_b[dr]])
                        gch0 += nch
                    P.barrier()
                    P.emit()

            scan_units("gla")
            if stop_after > 3:
                scan_units("mlstm")
        if stop_after <= 4:
            return nc, dbg

        for branch in ("gla", "mlstm"):
            with ExitStack() as ph:
                src = OG if branch == "gla" else OM
                dstT = UT if branch == "gla" else UMT
                nwT = gnwT if branch == "gla" else mnwT
                g1 = sb(ph, "g1", [128, 32, 512], BF16); g1b = P.buf(); g1d = P.dsem("g1")
                if branch == "mlstm":
                    g2 = sb(ph, "g2", [128, 32, 512], BF16); g2b = P.buf(); g2d = P.dsem("g2")
                ust = sb(ph, "ust", [128, 32, 512], BF16); ustb = P.buf(); ustd = P.dsem("ust")
                oin = Ring(P, ph, "oin", 2, [128, D], F32, dsem=True)
                tmp = sb(ph, "p41tmp", [128, D], F32); tmpb = P.buf()
                sts = Ring(P, ph, "p41s", 2, [128, 40], F32)
                for tb in range(3):
                    P.dma("sp", g1[:], (SZT if branch == "gla" else SZM)[:, tb * 512:(tb + 1) * 512].rearrange(
                        "(k p) t -> p k t", p=128), g1d, writes=[g1b])
                    if branch == "mlstm":
                        P.dma("sp", g2[:], SGO[:, tb * 512:(tb + 1) * 512].rearrange("(k p) t -> p k t", p=128), g2d,
                              writes=[g2b])
                        for k4 in range(4):
                            P.tt("pool", g1[:, k4 * 8:(k4 + 1) * 8, :], g1[:, k4 * 8:(k4 + 1) * 8, :],
                                 g2[:, k4 * 8:(k4 + 1) * 8, :], ALU.mult, [g1b, g2b], [g1b])
                    for ti in range(4):
                        i = tb * 4 + ti
                        ot, ob, od = oin.next()
                        P.dma("sp", ot[:], src[i * 128:(i + 1) * 128, :], od, writes=[ob])
                        s_, s_b, _ = sts.next()
                        NHh = 4 if branch == "gla" else 8
                        E_ = D // NHh
                        o3 = ot[:].rearrange("p (h e) -> p h e", e=E_)
                        sq = tmp[:].rearrange("p (h e) -> p h e", e=E_)
                        if branch == "mlstm":
                            P.op("dve", lambda e, o=s_[:, 0:8], i_=o3: e.tensor_reduce(out=o, in_=i_, op=ALU.add, axis=AX.X),
                                 [ob], [s_b])
                            P.ts("dve", s_[:, 8:16], s_[:, 0:8], 1.0 / E_, None, ALU.mult, None, [s_b], [s_b])
                            P.tt("dve", o3, o3, bc_last(s_[:, 8:16], E_), ALU.subtract, [ob, s_b], [ob])
                        P.tt("pool", sq, o3, o3, ALU.mult, [ob], [tmpb])
                        P.op("dve", lambda e, o=s_[:, 16:16 + NHh], i_=sq: e.tensor_reduce(out=o, in_=i_, op=ALU.add, axis=AX.X),
                             [tmpb], [s_b])
                        P.ts("dve", s_[:, 24:24 + NHh], s_[:, 16:16 + NHh], 1.0 / E_, EPS, ALU.mult, ALU.add, [s_b], [s_b])
                        P.act(s_[:, 24:24 + NHh], s_[:, 24:24 + NHh], AF.Sqrt, [s_b], [s_b])
                        P.op("dve", lambda e, o=s_[:, 32:32 + NHh], i_=s_[:, 24:24 + NHh]: e.reciprocal(out=o, in_=i_), [s_b], [s_b])
                        P.tt("dve", o3, o3, bc_last(s_[:, 32:32 + NHh], E_), ALU.mult, [ob, s_b], [ob])
                        for g in range(8):
                            ps, pb = P.psum()
                            for kk in range(4):
                                k = g * 4 + kk
                                P.tr(ps[:, kk * 128:(kk + 1) * 128], ot[:, k * 128:(k + 1) * 128], ident[:], [ob, cb], [pb])
                            for kk in range(4):
                                k = g * 4 + kk
                                P.stt(ust[:, k, ti * 128:(ti + 1) * 128], ps[:, kk * 128:(kk + 1) * 128], nwT[:, k:k + 1],
                                      g1[:, k, ti * 128:(ti + 1) * 128], ALU.mult, ALU.mult, [pb, vec_b, g1b], [ustb])
                    P.dma("sp", dstT[:, tb * 512:(tb + 1) * 512].rearrange("(k p) t -> p k t", p=128), ust[:], ustd,
                          reads=[ustb])
                P.barrier()
                P.emit()
        if stop_after <= 5:
            return nc, dbg

        tmp32r = [None]

        def gemm_stage(name, W, actsrc, evac_fn, dst, dst_dt):
            with ExitStack() as ph:
                aT = sb(ph, "aT", [128, 32, NTOK], BF16); aT_b = P.buf(); aT_d = P.dsem("aT" + name)
                for tb in range(3):
                    P.dma("sp", aT[:, :, tb * 512:(tb + 1) * 512], actsrc[:, tb * 512:(tb + 1) * 512].rearrange(
                        "(k p) t -> p k t", p=128), aT_d, writes=[aT_b])
                wst = Ring(P, ph, "gwst", 3, [128, 32, 128], F32, dsem=True)
                wbf = Ring(P, ph, "gwbf", 2, [128, 32, 128], BF16)
                stg = Ring(P, ph, "gstg", 2, [128, NTOK], dst_dt, dsem=True)
                aux = Ring(P, ph, "gaux", 2, [128, NTOK], F32, dsem=True) if name == "B" else None
                auxb = Ring(P, ph, "gauxb", 2, [128, NTOK], BF16, dsem=True) if name in ("A", "B") else None
                tmp32r[0] = Ring(P, ph, "gtmp", 2, [128, 512], F32)

                def issue(bi):
                    wt, wb_, wd = wst.next()
                    P.dma("sp", wt[:], W[:, bi * 128:(bi + 1) * 128].rearrange("(k p) n -> p k n", p=128), wd, writes=[wb_])
                    return wt, wb_
                loaded = {0: issue(0), 1: issue(1)}
                casted = {}

                def do_cast(bi):
                    wt, wb_ = loaded.pop(bi)
                    wbt, wbb, _ = wbf.next()
                    P.act(wbt[:, 0:16, :], wt[:, 0:16, :], AF.Copy, [wb_], [wbb])
                    P.act(wbt[:, 16:32, :], wt[:, 16:32, :], AF.Copy, [wb_], [wbb])
                    casted[bi] = (wbt, wbb)
                do_cast(0)
                for bi in range(32):
                    if bi + 2 < 32:
                        loaded[bi + 2] = issue(bi + 2)
                    if bi + 1 < 32:
                        do_cast(bi + 1)
                    wbt, wbb = casted.pop(bi)
                    st_t, st_b, st_d = stg.next()
                    ctx = evac_fn("pre", bi, aux, auxb)
                    for tb in range(3):
                        ps, pb = P.psum()
                        for k in range(32):
                            P.mm(ps[:, 0:512], wbt[:, k, :], aT[:, k, tb * 512:(tb + 1) * 512], k == 0, k == 31, [wbb, aT_b], [pb])
                        evac_fn("evac", bi, aux, auxb, ctx=ctx, ps=ps, pb=pb, tb=tb, st_t=st_t, st_b=st_b)
                    P.dma("sp", dst[bi * 128:(bi + 1) * 128, :], st_t[:], st_d, reads=[st_b])
                P.barrier()
                P.emit()

        def evac_A(mode, bi, aux, auxb, ctx=None, ps=None, pb=None, tb=None, st_t=None, st_b=None):
            if mode == "pre":
                gt, gb_, gd = auxb.next()
                P.dma("sp", gt[:], G01[bi * 128:(bi + 1) * 128, :], gd, writes=[gb_])
                return (gt, gb_)
            gt, gb_ = ctx
            P.tt("dve", st_t[:, tb * 512:(tb + 1) * 512], ps[:, 0:512], gt[:, tb * 512:(tb + 1) * 512], ALU.mult,
                 [pb, gb_], [st_b])

        def evac_B(mode, bi, aux, auxb, ctx=None, ps=None, pb=None, tb=None, st_t=None, st_b=None):
            if mode == "pre":
                gt, gb_, gd = auxb.next()
                P.dma("sp", gt[:], G01[4096 + bi * 128:4096 + (bi + 1) * 128, :], gd, writes=[gb_])
                t1t, t1b, t1d = aux.next()
                P.dma("sp", t1t[:], T1[bi * 128:(bi + 1) * 128, :], t1d, writes=[t1b])
                return (gt, gb_, t1t, t1b)
            gt, gb_, t1t, t1b = ctx
            sl = slice(tb * 512, (tb + 1) * 512)
            tm, tmb, _ = tmp32r[0].next()
            P.tt("dve", tm[:], ps[:, 0:512], gt[:, sl], ALU.mult, [pb, gb_], [tmb])
            P.tt("pool", st_t[:, sl], tm[:], t1t[:, sl], ALU.add, [tmb, t1b], [st_b])

        def evac_C(mode, bi, aux, auxb, ctx=None, ps=None, pb=None, tb=None, st_t=None, st_b=None):
            if mode == "pre":
                return None
            j = 0 if tb == 0 else 1
            P.ts("dve", st_t[:, tb * 512:(tb + 1) * 512], ps[:, 0:512], gateT[:, bi, j:j + 1], None, ALU.mult, None,
                 [pb, modT_b], [st_b])

        gemm_stage("A", w_gp, UT, evac_A, T1, F32)
        gemm_stage("B", w_mp, UMT, evac_B, MIXT, BF16)
        gemm_stage("C", w_o, MIXT, evac_C, OUTT, F32)
        if stop_after <= 6:
            return nc, dbg

        with ExitStack() as ph:
            fnw = sb(ph, "fnw", [128, D]); fnw_b = P.buf(); fd = P.dsem("fnw")
            P.dma("sp", fnw[:], f_nw.partition_broadcast(128)[:, 0, :], fd, writes=[fnw_b])
            xin = Ring(P, ph, "x5", 2, [128, D], F32, dsem=True)
            oin = Ring(P, ph, "o5", 2, [128, 32, 128], F32, dsem=True)
            yst = Ring(P, ph, "y5", 2, [128, D], F32, dsem=True)
            junk = sb(ph, "junk5", [128, D], BF16); junk_b = P.buf()
            sts = Ring(P, ph, "s5", 2, [128, 4], F32)
            for i in range(12):
                xt, xb, xd = xin.next()
                P.dma("sp", xt[:], x_all[i * 128:(i + 1) * 128, :], xd, writes=[xb])
                ot, ob, od = oin.next()
                P.dma("sp", ot[:], OUTT[:, i * 128:(i + 1) * 128].rearrange("(k p) t -> p k t", p=128), od, writes=[ob])
                for g in range(8):
                    ps, pb = P.psum()
                    for kk in range(4):
                        k = g * 4 + kk
                        P.tr(ps[:, kk * 128:(kk + 1) * 128], ot[:, k, :], ident[:], [ob, cb], [pb])
                    P.tt("dve", xt[:, g * 512:(g + 1) * 512], ps[:, 0:512], xt[:, g * 512:(g + 1) * 512], ALU.add,
                         [pb, xb], [xb])
                s_, s_b, _ = sts.next()
                P.act(junk[:], xt[:], AF.Square, [xb], [junk_b, s_b], accum_out=s_[:, 0:1])
                P.ts("dve", s_[:, 1:2], s_[:, 0:1], 1.0 / D, EPS, ALU.mult, ALU.add, [s_b], [s_b])
                P.act(s_[:, 2:3], s_[:, 1:2], AF.Sqrt, [s_b], [s_b])
                P.op("dve", lambda e, o=s_[:, 3:4], i_=s_[:, 2:3]: e.reciprocal(out=o, in_=i_), [s_b], [s_b])
                yt, yb, yd = yst.next()
                P.stt(yt[:], xt[:], s_[:, 3:4], fnw[:], ALU.mult, ALU.mult, [xb, s_b, fnw_b], [yb])
                P.dma("sp", y_all[i * 128:(i + 1) * 128, :], yt[:], yd, reads=[yb])
            P.barrier()
            P.emit()
    return nc, dbg


_CACHE = {}


def _get_nc():
    if "nc" not in _CACHE:
        _CACHE["nc"] = build_program()[0]
    return _CACHE["nc"]


def make_in_maps(inputs):
    f = lambda a: np.ascontiguousarray(np.asarray(a, dtype=np.float32))
    x_prompt, x_sample, c = f(inputs["x_prompt"]), f(inputs["x_sample"]), f(inputs["c"])
    shared = {
        "w_ada": f(inputs["w_ada"][0]), "b_ada": f(inputs["b_ada"]), "w_in": f(inputs["w_in"][0]),
        "w_a2": f(inputs["gla_w_a2"][0]), "b_a": f(inputs["gla_b_a"][0]), "b_i": f(inputs["mlstm_b_i"][0]),
        "b_f": f(inputs["mlstm_b_f"][0]), "b_mg": f(inputs["b_merge"]), "g_nw": f(inputs["gla_norm_w"]),
        "m_nw": f(inputs["mlstm_norm_w"]), "w_gp": f(inputs["w_gla_proj"][0]), "w_mp": f(inputs["w_mlstm_proj"][0]),
        "w_o": f(inputs["w_out"][0]), "f_nw": f(inputs["final_norm_w"]).reshape(1, D),
    }
    c_ctx = f(inputs["c_ctx"])
    sS, sC, sn, sm = (f(inputs[k]) for k in ("state_gla_S", "state_mlstm_C", "state_mlstm_n", "state_mlstm_m"))
    maps = []
    for core in range(8):
        m = dict(shared)
        m["x_all"] = np.concatenate([x_prompt[2 * core], x_prompt[2 * core + 1], x_sample[core]], axis=0)
        m["cvec"] = np.stack([c_ctx, c[core]], axis=0)
        m["st_S"] = sS[core, 0]
        m["st_C"] = sC[core, 0]
        m["st_n"] = sn[core, 0]
        m["st_m"] = sm[core, 0]
        maps.append(m)
    return maps


def kernel(**inputs):
    nc = _get_nc()
    maps = make_in_maps(inputs)
    res = run_bass_kernel_spmd(nc, maps, core_ids=list(range(8))).results
    y_prompt = np.stack([res[i // 2]["y_all"][(i % 2) * 256:(i % 2) * 256 + 256] for i in range(16)], axis=0)
    y_sample = np.stack([res[i]["y_all"][512:1536] for i in range(8)], axis=0)
    new_S = np.concatenate([res[i]["o_S"] for i in range(8)], axis=0)[:, None]
    new_C = np.concatenate([res[i]["o_C"] for i in range(8)], axis=0)[:, None]
    new_n = np.concatenate([res[i]["o_n"] for i in range(8)], axis=0)[:, None]
    new_m = np.concatenate([res[i]["o_m"] for i in range(8)], axis=0)[:, None]
    return (y_prompt.astype(np.float32), y_sample.astype(np.float32), new_S.astype(np.float32),
            new_C.astype(np.float32), new_n.astype(np.float32), new_m.astype(np.float32))
```

```python
import numpy as np
from contextlib import ExitStack
import concourse.bass as bass
import concourse.mybir as mybir
from concourse.bass_utils import run_bass_kernel_spmd

F32 = mybir.dt.float32
BF16 = mybir.dt.bfloat16
AF = mybir.ActivationFunctionType
ALU = mybir.AluOpType
AX = mybir.AxisListType

D = 4096
NTOK = 1536
TP = 256
TS = 1024
N_IN = 36928
EPS = 1e-6
SEQS = [(0, 256, False), (256, 256, False), (512, 1024, True)]

C_GQ, C_GK, C_GV, C_GZ, C_GA = 0, 2048, 4096, 8192, 12288
C_MQ, C_MK, C_MV, C_MZ, C_MO, C_MI, C_MF, C_MG = 12320, 14368, 16416, 20512, 24608, 28704, 28720, 28736


class Buf:
    __slots__ = ("name", "w", "r")

    def __init__(self, name=""):
        self.name = name
        self.w = None
        self.r = {}


class DSem:
    def __init__(self, sem, name):
        self.sem = sem
        self.count = 0
        self.name = name


class Eng:
    def __init__(self, name, eng, sem, is_pe=False):
        self.name = name
        self.eng = eng
        self.sem = sem
        self.count = 0
        self.seen = {}
        self.ops = []
        self.is_pe = is_pe


class Prog:
    def __init__(self, nc, es):
        self.nc = nc
        self.es = es
        self.E = {}
        for name, eng in (("pe", nc.tensor), ("act", nc.scalar), ("dve", nc.vector),
                          ("pool", nc.gpsimd), ("sp", nc.sync)):
            sem = es.enter_context(nc.semaphore("s_" + name))
            self.E[name] = Eng(name, eng, sem, is_pe=(name == "pe"))
        self.dsems = []
        self.psum_banks = []
        self.psum_i = 0
        self.nbuf = 0

    def buf(self, name=""):
        self.nbuf += 1
        return Buf(name or f"b{self.nbuf}")

    def dsem(self, name):
        self.nbuf += 1
        sem = self.es.enter_context(self.nc.semaphore(f"d_{name}_{self.nbuf}"))
        d = DSem(sem, name)
        self.dsems.append(d)
        return d

    def init_psum(self):
        for i in range(8):
            t = self.es.enter_context(self.nc.psum_tensor(f"psb{i}", [128, 512], F32))
            self.psum_banks.append((t, self.buf(f"psb{i}")))

    def psum(self):
        t, b = self.psum_banks[self.psum_i % 8]
        self.psum_i += 1
        return t, b

    def _deps(self, reads, writes):
        deps = []
        for b in reads:
            if b.w is not None:
                deps.append(b.w)
        for b in writes:
            if b.w is not None:
                deps.append(b.w)
            deps.extend(b.r.values())
        return deps

    def _reduce(self, E, deps, skip_sem=None):
        need = {}
        for sem, val in deps:
            if sem is skip_sem:
                continue
            if E.is_pe and sem is E.sem:
                continue
            k = id(sem)
            if E.seen.get(k, 0) >= val:
                continue
            if k not in need or need[k][1] < val:
                need[k] = (sem, val)
        for k, (sem, val) in need.items():
            E.seen[k] = val
        return list(need.values())

    def _mark(self, tok, reads, writes):
        sem, val = tok
        k = id(sem)
        for b in reads:
            if k not in b.r or b.r[k][1] < val:
                b.r[k] = tok
        for b in writes:
            b.w = tok
            b.r = {}

    def op(self, eng, fn, reads=(), writes=(), signal=True):
        E = self.E[eng]
        if not E.is_pe:
            signal = True
        deps = self._deps(reads, writes)
        waits = self._reduce(E, deps)
        tok = (E.sem, E.count + 1)
        if signal:
            E.count += 1
        E.ops.append(("op", fn, waits, signal))
        self._mark(tok, reads, writes)
        return tok

    def dma(self, q, out, in_, ds, reads=(), writes=()):
        E = self.E[q]
        deps = []
        for b in reads:
            if b.w is not None:
                deps.append(b.w)
        for b in writes:
            if b.w is not None and b.w[0] is not ds.sem:
                deps.append(b.w)
            deps.extend(b.r.values())
        for sem, val in deps:
            assert not (sem is ds.sem and val >= ds.count + 16), "self-dependency on DMA semaphore"
        waits = self._reduce(E, deps)
        ds.count += 16
        tok = (ds.sem, ds.count)
        E.ops.append(("dma", (out, in_), waits, ds.sem))
        self._mark(tok, reads, writes)
        return tok

    def barrier(self):
        toks = [(e.sem, e.count) for e in self.E.values() if e.count > 0]
        toks += [(d.sem, d.count) for d in self.dsems if d.count > 0]
        for E in self.E.values():
            waits = self._reduce(E, [t for t in toks if not (t[0] is E.sem)])
            if waits:
                E.ops.append(("wait", None, waits, False))

    def emit(self):
        nc = self.nc
        for e in self.E.values():
            assert e.count < 65000, (e.name, e.count)
        for d in self.dsems:
            assert d.count < 65000, (d.name, d.count)

        def run(E, eng):
            for kind, payload, waits, sig in E.ops:
                for sem, val in waits:
                    eng.wait_ge(sem, val)
                if kind == "op":
                    ins = payload(eng)
                    if sig:
                        ins.then_inc(E.sem, 1)
                elif kind == "dma":
                    out, in_ = payload
                    eng.dma_start(out=out, in_=in_).then_inc(sig, 16)
            E.ops = []

        with nc.Block() as block:
            @block.tensor
            def _(eng):
                run(self.E["pe"], eng)

            @block.scalar
            def _(eng):
                run(self.E["act"], eng)

            @block.vector
            def _(eng):
                run(self.E["dve"], eng)

            @block.gpsimd
            def _(eng):
                run(self.E["pool"], eng)

            @block.sync
            def _(eng):
                run(self.E["sp"], eng)

    def mm(self, out, lhsT, rhs, start, stop, reads, writes):
        self.op("pe", lambda e: e.matmul(out, lhsT, rhs, start=start, stop=stop),
                reads, writes, signal=stop)

    def tr(self, out, in_, ident, reads, writes):
        self.op("pe", lambda e: e.transpose(out, in_, ident), reads, writes, signal=True)

    def act(self, out, in_, func, reads, writes, **kw):
        self.op("act", lambda e: e.activation(out=out, in_=in_, func=func, **kw), reads, writes)

    def ts(self, eng, out, in0, s1, s2, op0, op1, reads, writes):
        if s2 is None:
            self.op(eng, lambda e: e.tensor_scalar(out=out, in0=in0, scalar1=s1, scalar2=None, op0=op0),
                    reads, writes)
        else:
            self.op(eng, lambda e: e.tensor_scalar(out=out, in0=in0, scalar1=s1, scalar2=s2, op0=op0, op1=op1),
                    reads, writes)

    def tt(self, eng, out, in0, in1, op, reads, writes):
        self.op(eng, lambda e: e.tensor_tensor(out=out, in0=in0, in1=in1, op=op), reads, writes)

    def stt(self, out, in0, scalar, in1, op0, op1, reads, writes):
        self.op("dve", lambda e: e.scalar_tensor_tensor(out=out, in0=in0, scalar=scalar, in1=in1,
                                                        op0=op0, op1=op1), reads, writes)

    def cp(self, eng, out, in_, reads, writes):
        self.op(eng, lambda e: e.tensor_copy(out=out, in_=in_), reads, writes)

    def memset(self, eng, out, val, writes):
        self.op(eng, lambda e: e.memset(out, val), (), writes)


class Ring:
    def __init__(self, P, es, name, n, shape, dtype, dsem=False):
        self.slots = []
        for i in range(n):
            P.nbuf += 1
            t = es.enter_context(P.nc.sbuf_tensor(f"{name}{i}_{P.nbuf}", shape, dtype))
            self.slots.append((t, P.buf(f"{name}{i}"), P.dsem(f"{name}{i}") if dsem else None))
        self.i = 0

    def next(self):
        s = self.slots[self.i % len(self.slots)]
        self.i += 1
        return s


def bc_last(ap, n):
    return bass.AP(ap.tensor, ap.offset, [list(x) for x in ap.ap] + [[0, n]])


def build_program(debug=False, stop_after=99):
    nc = bass.Bass("TRN2", target_bir_lowering=False)

    def din(name, shape):
        return nc.dram_tensor(name, shape, F32, kind="ExternalInput").ap()

    def dout(name, shape):
        return nc.dram_tensor(name, shape, F32, kind="ExternalOutput").ap()

    dbg = {}

    def dscr(name, shape, dt=F32):
        if debug and name in debug:
            t = nc.dram_tensor(name, shape, dt, kind="ExternalOutput").ap()
            dbg[name] = t
            return t
        return nc.dram_tensor(name, shape, dt, kind="Internal").ap()

    x_all = din("x_all", [NTOK, D])
    cvec = din("cvec", [2, D])
    st_S = din("st_S", [2, 4, 512, 1024])
    st_C = din("st_C", [2, 8, 256, 512])
    st_n = din("st_n", [2, 8, 256])
    st_m = din("st_m", [2, 8])
    w_ada = din("w_ada", [D, 3 * D])
    b_ada = din("b_ada", [1, 3 * D])
    w_in = din("w_in", [D, N_IN])
    w_a2 = din("w_a2", [2, 16, 2048])
    b_a = din("b_a", [2, 2048])
    b_i = din("b_i", [2, 8])
    b_f = din("b_f", [2, 8])
    b_mg = din("b_mg", [1, 2 * D])
    g_nw = din("g_nw", [1, D])
    m_nw = din("m_nw", [1, D])
    w_gp = din("w_gp", [D, D])
    w_mp = din("w_mp", [D, D])
    w_o = din("w_o", [D, D])
    f_nw = din("f_nw", [1, D])
    y_all = dout("y_all", [NTOK, D])
    o_S = dout("o_S", [2, 2, 4, 512, 1024])
    o_C = dout("o_C", [2, 2, 8, 256, 512])
    o_n = dout("o_n", [2, 2, 8, 256])
    o_m = dout("o_m", [2, 2, 8])
    MODD = dscr("MODD", [2, 3 * D])
    QT = dscr("QT", [2048, NTOK], BF16)
    KT = dscr("KT", [2048, NTOK], BF16)
    VT = dscr("VT", [4096, NTOK], BF16)
    SZT = dscr("SZT", [4096, NTOK], BF16)
    GAT = dscr("GAT", [2, 16, NTOK])
    MQT = dscr("MQT", [2048, NTOK], BF16)
    MKT = dscr("MKT", [2048, NTOK], BF16)
    MVT = dscr("MVT", [4096, NTOK], BF16)
    SZM = dscr("SZM", [4096, NTOK], BF16)
    SGO = dscr("SGO", [4096, NTOK], BF16)
    GIF = dscr("GIF", [4, 8, NTOK])
    G01 = dscr("G01", [8192, NTOK], BF16)
    DECD = dscr("DECD", [1, 2 * 8 * 24])
    PARTG = dscr("PARTG", [NTOK, 4096])
    OG = dscr("OG", [NTOK, 4096])
    PARTM = dscr("PARTM", [NTOK, 4096])
    OM = dscr("OM", [NTOK, 4096])
    UT = dscr("UT", [4096, NTOK], BF16)
    UMT = dscr("UMT", [4096, NTOK], BF16)
    T1 = dscr("T1", [4096, NTOK])
    MIXT = dscr("MIXT", [4096, NTOK], BF16)
    OUTT = dscr("OUTT", [4096, NTOK])

    es = ExitStack()
    with es:
        P = Prog(nc, es)
        P.init_psum()
        es.enter_context(nc.allow_non_contiguous_dma(reason="small strided loads"))

        uniq = [0]

        def sb(stack, name, shape, dt=F32):
            uniq[0] += 1
            return stack.enter_context(nc.sbuf_tensor(f"{name}_{uniq[0]}", shape, dt))

        cb = P.buf("consts")
        ident = sb(es, "ident", [128, 128])
        identb = sb(es, "identb", [128, 128], BF16)
        tri = {}
        P.memset("pool", ident[:], 0.0, [cb])
        P.op("pool", lambda e: e.affine_select(out=ident[:], in_=ident[:], compare_op=ALU.not_equal, fill=1.0,
                                               base=0, pattern=[[-1, 128]], channel_multiplier=1), [cb], [cb])
        P.cp("pool", identb[:], ident[:], [cb], [cb])
        specs = {"U_incl": (-1, 1, ALU.is_ge), "L_incl": (1, -1, ALU.is_ge),
                 "L_strict": (1, -1, ALU.is_gt), "U_strict": (-1, 1, ALU.is_gt)}
        for nm, (cm, st, cmp_) in specs.items():
            t1 = sb(es, "m1_" + nm, [64, 64])
            t2 = sb(es, "m2_" + nm, [64, 64])
            P.memset("pool", t1[:], 1.0, [cb])

            def _sel(e, t1=t1, cm=cm, st=st, cmp_=cmp_):
                return e.affine_select(out=t1[:], in_=t1[:], compare_op=cmp_, fill=0.0, base=0,
                                       pattern=[[st, 64]], channel_multiplier=cm)
            P.op("pool", _sel, [cb], [cb])
            P.ts("pool", t2[:], t1[:], -1.0 / 16.0, None, ALU.mult, None, [cb], [cb])
            tri[nm] = (t1, t2)
        onesb = sb(es, "onesb", [128, 1], BF16)
        P.memset("pool", onesb[:], 1.0, [cb])
        scale1T = sb(es, "scale1T", [128, 32, 2])
        shiftT = sb(es, "shiftT", [128, 32, 2])
        gateT = sb(es, "gateT", [128, 32, 2])
        modT_b = P.buf("modT")
        gnwT = sb(es, "gnwT", [128, 32])
        mnwT = sb(es, "mnwT", [128, 32])
        bmgT = sb(es, "bmgT", [128, 64])
        vec_b = P.buf("vecs")
        dv_ = P.dsem("vecs")
        P.dma("sp", gnwT[:], g_nw.rearrange("o (k p) -> p (o k)", p=128), dv_, writes=[vec_b])
        P.dma("sp", mnwT[:], m_nw.rearrange("o (k p) -> p (o k)", p=128), dv_, writes=[vec_b])
        P.dma("sp", bmgT[:], b_mg.rearrange("o (k p) -> p (o k)", p=128), dv_, writes=[vec_b])

        with ExitStack() as ph:
            c_sb = sb(ph, "c_sb", [2, D]); c_b = P.buf()
            d0 = P.dsem("p0c")
            P.dma("sp", c_sb[:], cvec[:, :], d0, writes=[c_b])
            P.act(c_sb[:], c_sb[:], AF.Silu, [c_b], [c_b])
            scT = sb(ph, "scT", [128, 32, 2]); scT_b = P.buf()
            ps, pb = P.psum()
            for k in range(32):
                P.tr(ps[:, 2 * k:2 * k + 2], c_sb[:, k * 128:(k + 1) * 128], ident[0:2, 0:2], [c_b, cb], [pb])
            P.cp("dve", scT[:].rearrange("p k j -> p (k j)"), ps[:, 0:64], [pb], [scT_b])
            mod_sb = sb(ph, "mod_sb", [2, 3 * D]); mod_b = P.buf()
            bada = sb(ph, "bada", [2, 3 * D]); bada_b = P.buf()
            d1 = P.dsem("p0b")
            P.dma("sp", bada[:], b_ada.partition_broadcast(2)[:, 0, :], d1, writes=[bada_b])
            wr = Ring(P, ph, "wada", 2, [128, 32, 256], F32, dsem=True)
            for nb in range(48):
                wt, wb_, wd = wr.next()
                P.dma("sp", wt[:], w_ada[:, nb * 256:(nb + 1) * 256].rearrange("(k p) n -> p k n", p=128), wd,
                      writes=[wb_])
                ps, pb = P.psum()
                for k in range(32):
                    P.mm(ps[0:2, 0:256], scT[:, k, :], wt[:, k, :], k == 0, k == 31, [scT_b, wb_], [pb])
                P.tt("dve", mod_sb[:, nb * 256:(nb + 1) * 256], ps[0:2, 0:256], bada[:, nb * 256:(nb + 1) * 256],
                     ALU.add, [pb, bada_b], [mod_b])
            P.ts("dve", mod_sb[:, D:2 * D], mod_sb[:, D:2 * D], 1.0, None, ALU.add, None, [mod_b], [mod_b])
            dm = P.dsem("p0m")
            modd_b = P.buf("MODD")
            P.dma("sp", MODD[:, :], mod_sb[:], dm, reads=[mod_b], writes=[modd_b])
            for part, dst in ((0, shiftT), (1, scale1T), (2, gateT)):
                ps, pb = P.psum()
                for k in range(32):
                    P.tr(ps[:, 2 * k:2 * k + 2], mod_sb[:, part * D + k * 128: part * D + (k + 1) * 128],
                         ident[0:2, 0:2], [mod_b, cb], [pb])
                P.cp("dve", dst[:].rearrange("p k j -> p (k j)"), ps[:, 0:64], [pb], [modT_b])
            P.barrier()
            P.emit()
        if stop_after <= 0:
            return nc, dbg

        with ExitStack() as ph:
            hT = sb(ph, "hT", [128, 32, NTOK], BF16); hT_b = P.buf("hT")
            with ExitStack() as p1:
                xr = Ring(P, p1, "xin", 2, [128, D], F32, dsem=True)
                junk = sb(p1, "junk", [128, D], BF16); junk_b = P.buf()
                st_r = Ring(P, p1, "stat", 2, [128, 4], F32)
                for i in range(12):
                    j = 0 if i < 4 else 1
                    xt, xb, xd = xr.next()
                    P.dma("sp", xt[:], x_all[i * 128:(i + 1) * 128, :], xd, writes=[xb])
                    stt_, sbf, _ = st_r.next()
                    P.act(junk[:], xt[:], AF.Square, [xb], [junk_b, sbf], accum_out=stt_[:, 0:1])
                    P.ts("dve", stt_[:, 1:2], stt_[:, 0:1], 1.0 / D, EPS, ALU.mult, ALU.add, [sbf], [sbf])
                    P.act(stt_[:, 2:3], stt_[:, 1:2], AF.Sqrt, [sbf], [sbf])
                    P.op("dve", lambda e, o=stt_[:, 3:4], i_=stt_[:, 2:3]: e.reciprocal(out=o, in_=i_), [sbf], [sbf])
                    P.ts("dve", xt[:], xt[:], stt_[:, 3:4], None, ALU.mult, None, [xb, sbf], [xb])
                    for g in range(8):
                        ps, pb = P.psum()
                        for kk in range(4):
                            k = g * 4 + kk
                            P.tr(ps[:, kk * 128:(kk + 1) * 128], xt[:, k * 128:(k + 1) * 128], ident[:], [xb, cb], [pb])
                        for kk in range(4):
                            k = g * 4 + kk
                            if kk % 2 == 0:
                                P.ts("dve", hT[:, k, i * 128:(i + 1) * 128], ps[:, kk * 128:(kk + 1) * 128],
                                     scale1T[:, k, j:j + 1], shiftT[:, k, j:j + 1], ALU.mult, ALU.add,
                                     [pb, modT_b], [hT_b])
                            else:
                                P.act(hT[:, k, i * 128:(i + 1) * 128], ps[:, kk * 128:(kk + 1) * 128], AF.Identity,
                                      [pb, modT_b], [hT_b], scale=scale1T[:, k, j:j + 1], bias=shiftT[:, k, j:j + 1])
                P.barrier()
                P.emit()

            with ExitStack() as p2:
                wst = Ring(P, p2, "wst", 3, [128, 32, 128], F32, dsem=True)
                wbf = Ring(P, p2, "wbf", 2, [128, 32, 128], BF16)
                stg = Ring(P, p2, "stg", 2, [128, NTOK], BF16, dsem=True)
                stg32 = Ring(P, p2, "stg32", 1, [16, NTOK], F32, dsem=True)
                scr_b = P.buf("scratch_p2")

                def perm_views(out_t, ps, tb, nrows):
                    r0 = (tb - 1) * 8
                    o = out_t[0:nrows, 512:1536].rearrange("p (c r) -> p c r", r=16)[:, :, r0:r0 + 8]
                    i_ = ps[0:nrows, 0:512].rearrange("p (r c) -> p c r", c=64)
                    return o, i_

                def evac(kind, out_t, out_b, ps, pb, tb, nrows, bias_ap=None, scale=1.0, perm=False):
                    if perm and tb > 0:
                        o, i_ = perm_views(out_t, ps, tb, nrows)
                    else:
                        o, i_ = out_t[0:nrows, tb * 512:(tb + 1) * 512], ps[0:nrows, 0:512]
                    if kind == "copy":
                        if scale != 1.0:
                            P.ts("dve", o, i_, scale, None, ALU.mult, None, [pb], [out_b])
                        else:
                            P.cp("dve", o, i_, [pb], [out_b])
                    elif kind == "silu":
                        P.act(o, i_, AF.Silu, [pb], [out_b])
                    elif kind == "sigmoid":
                        if bias_ap is not None:
                            P.act(o, i_, AF.Sigmoid, [pb, vec_b], [out_b], bias=bias_ap)
                        else:
                            P.act(o, i_, AF.Sigmoid, [pb], [out_b])

                sections = [
                    (C_GQ, 16, QT, "copy", 512 ** -0.5, False),
                    (C_GK, 16, KT, "copy", 1.0, False),
                    (C_GV, 32, VT, "copy", 1.0, False),
                    (C_GZ, 32, SZT, "silu", 1.0, False),
                    (C_MQ, 16, MQT, "copy", 256 ** -0.5, True),
                    (C_MK, 16, MKT, "copy", 1.0, True),
                    (C_MV, 32, MVT, "copy", 1.0, True),
                    (C_MZ, 32, SZM, "silu", 1.0, False),
                    (C_MO, 32, SGO, "sigmoid", 1.0, False),
                    (C_MG, 64, G01, "sigmoidb", 1.0, False),
                ]
                blocks = []
                for (c0, nblk, dst, kind, scale, perm) in sections:
                    for b in range(nblk):
                        blocks.append(("reg", c0 + b * 128, dst, b, kind, scale, perm))
                blocks.append(("ga", C_GA, None, 0, None, 1.0, False))
                blocks.append(("gif", C_MI, None, 0, None, 1.0, True))
                nblk_total = len(blocks)

                def issue_load(bi):
                    typ, c0 = blocks[bi][0], blocks[bi][1]
                    wt, wb_, wd = wst.next()
                    ncol = 128 if typ == "reg" else 32
                    P.dma("sp", wt[:, :, 0:ncol], w_in[:, c0:c0 + ncol].rearrange("(k p) n -> p k n", p=128), wd,
                          writes=[wb_])
                    return (wt, wb_)

                loaded = {}
                loaded[0] = issue_load(0)
                loaded[1] = issue_load(1)
                casted = {}

                def do_cast(bi):
                    wt, wb_ = loaded.pop(bi)
                    wbt, wbb, _ = wbf.next()
                    ncol = 128 if blocks[bi][0] == "reg" else 32
                    P.cp("dve", wbt[:, 0:16, 0:ncol], wt[:, 0:16, 0:ncol], [wb_], [wbb])
                    P.cp("pool", wbt[:, 16:32, 0:ncol], wt[:, 16:32, 0:ncol], [wb_], [wbb])
                    casted[bi] = (wbt, wbb)
                do_cast(0)
                for bi, (typ, c0, dst, b, kind, scale, perm) in enumerate(blocks):
                    if bi + 2 < nblk_total:
                        loaded[bi + 2] = issue_load(bi + 2)
                    if bi + 1 < nblk_total:
                        do_cast(bi + 1)
                    wbt, wbb = casted.pop(bi)
                    if typ == "reg":
                        st_t, st_b, st_d = stg.next()
                        for tb in range(3):
                            ps, pb = P.psum()
                            for k in range(32):
                                P.mm(ps[:, 0:512], wbt[:, k, :], hT[:, k, tb * 512:(tb + 1) * 512], k == 0, k == 31,
                                     [wbb, hT_b], [pb])
                            if kind == "sigmoidb":
                                evac("sigmoid", st_t, st_b, ps, pb, tb, 128, bias_ap=bmgT[:, b:b + 1])
                            else:
                                evac(kind, st_t, st_b, ps, pb, tb, 128, scale=scale, perm=perm)
                        P.dma("sp", dst[b * 128:(b + 1) * 128, :], st_t[:], st_d, reads=[st_b], writes=[scr_b])
                    elif typ == "ga":
                        for dr in range(2):
                            st_t, st_b, st_d = stg32.next()
                            for tb in range(3):
                                ps, pb = P.psum()
                                for k in range(32):
                                    P.mm(ps[0:16, 0:512], wbt[:, k, dr * 16:(dr + 1) * 16],
                                         hT[:, k, tb * 512:(tb + 1) * 512], k == 0, k == 31, [wbb, hT_b], [pb])
                                evac("copy", st_t, st_b, ps, pb, tb, 16)
                            P.dma("sp", GAT[dr, :, :], st_t[0:16, :], st_d, reads=[st_b], writes=[scr_b])
                    else:
                        for q in range(4):
                            st_t, st_b, st_d = stg32.next()
                            for tb in range(3):
                                ps, pb = P.psum()
                                for k in range(32):
                                    P.mm(ps[0:8, 0:512], wbt[:, k, q * 8:(q + 1) * 8],
                                         hT[:, k, tb * 512:(tb + 1) * 512], k == 0, k == 31, [wbb, hT_b], [pb])
                                evac("copy", st_t, st_b, ps, pb, tb, 8, perm=True)
                            P.dma("sp", GIF[q, :, :], st_t[0:8, :], st_d, reads=[st_b], writes=[scr_b])
                P.barrier()
                P.emit()
        if stop_after <= 2:
            return nc, dbg

        with ExitStack() as ph:
            gaA = [sb(ph, f"gaA{d_}", [17, NTOK]) for d_ in range(2)]
            ga_b = P.buf("gaA")
            dga = P.dsem("gaA")
            for d_ in range(2):
                P.memset("pool", gaA[d_][:], 1.0, [ga_b])
            for d_ in range(2):
                P.dma("sp", gaA[d_][0:16, :], GAT[d_, :, :], dga, writes=[ga_b])

            WT = sb(ph, "WT", [64, 24, 32]); WT_b = P.buf("WT")
            DEC = sb(ph, "DEC", [128, 384]); DEC_b = P.buf("DEC")
            decd_b = P.buf("DECD")
            with ExitStack() as p3b:
                bi_sb = sb(p3b, "bi_sb", [8, 2]); bf_sb = sb(p3b, "bf_sb", [8, 2]); nbf_sb = sb(p3b, "nbf_sb", [8, 2])
                m0_sb = sb(p3b, "m0_sb", [8, 2]); nm0_sb = sb(p3b, "nm0_sb", [8, 2])
                ones8 = sb(p3b, "ones8", [8, 1024]); o8b = P.buf("ones8")
                P.memset("pool", ones8[:], 1.0, [o8b])
                gb = P.buf("gbias")
                dgb = P.dsem("gbias")
                P.dma("sp", bi_sb[:], b_i.rearrange("d h -> h d"), dgb, writes=[gb])
                P.dma("sp", bf_sb[:], b_f.rearrange("d h -> h d"), dgb, writes=[gb])
                P.dma("sp", m0_sb[:], st_m.rearrange("d h -> h d"), dgb, writes=[gb])
                P.ts("dve", nbf_sb[:], bf_sb[:], -1.0, None, ALU.mult, None, [gb], [gb])
                P.ts("dve", nm0_sb[:], m0_sb[:], -1.0, None, ALU.mult, None, [gb], [gb])
                gr = Ring(P, p3b, "graw", 2, [8, 2, 1024], F32, dsem=True)
                tmpr = {nm: Ring(P, p3b, "g_" + nm, 2, [8, 1024], F32) for nm in
                        ("ig", "lf", "B", "m", "A", "w", "t")}
                smr = Ring(P, p3b, "gsm", 2, [8, 64], F32)
                dec_st = sb(p3b, "dec_st", [8, 2, 24]); dec_stb = P.buf()
                mo_d = [P.dsem("mout0"), P.dsem("mout1")]
                gch = 0
                for si, (tok0, T, is_s) in enumerate(SEQS):
                    nch = T // 64
                    wq = {}
                    for dr in range(2):
                        rv = (lambda a: a[:, ::-1]) if dr == 1 else (lambda a: a)
                        raw, rb, rd = gr.next()
                        P.dma("sp", raw[:, 0, 0:T], GIF[dr, :, tok0:tok0 + T], rd, writes=[rb])
                        P.dma("sp", raw[:, 1, 0:T], GIF[2 + dr, :, tok0:tok0 + T], rd, writes=[rb])
                        ig, igb, _ = tmpr["ig"].next(); lf, lfb, _ = tmpr["lf"].next()
                        Bt, Bb, _ = tmpr["B"].next(); mt, mb, _ = tmpr["m"].next()
                        At, Ab, _ = tmpr["A"].next(); wt_, wtb, _ = tmpr["w"].next(); tt_, ttb, _ = tmpr["t"].next()
                        sm, smb, _ = smr.next()
                        P.ts("dve", ig[:, 0:T], raw[:, 0, 0:T], bi_sb[:, dr:dr + 1], None, ALU.add, None, [rb, gb], [igb])
                        P.act(lf[:, 0:T], raw[:, 1, 0:T], AF.Exp, [rb, gb], [lfb], scale=-1.0, bias=nbf_sb[:, dr:dr + 1])
                        P.act(lf[:, 0:T], lf[:, 0:T], AF.Ln, [lfb], [lfb], bias=1.0)
                        P.ts("dve", lf[:, 0:T], lf[:, 0:T], -1.0, None, ALU.mult, None, [lfb], [lfb])
                        P.op("dve", lambda e, o=rv(Bt[:, 0:T]), a=rv(ones8[:, 0:T]), b_=rv(lf[:, 0:T]):
                             e.tensor_tensor_scan(out=o, data0=a, data1=b_, initial=0.0, op0=ALU.mult, op1=ALU.add),
                             [lfb, o8b], [Bb])
                        init = m0_sb[:, dr:dr + 1] if is_s else 0.0
                        P.op("dve", lambda e, o=rv(mt[:, 0:T]), a=rv(lf[:, 0:T]), b_=rv(ig[:, 0:T]), init=init:
                             e.tensor_tensor_scan(out=o, data0=a, data1=b_, initial=init, op0=ALU.add, op1=ALU.max),
                             [lfb, igb, gb], [mb])
                        last = 63 if dr == 0 else 0
                        Bv = Bt[:, 0:T].rearrange("p (c s) -> p c s", s=64)[:, :, last]
                        mv_ = mt[:, 0:T].rearrange("p (c s) -> p c s", s=64)[:, :, last]
                        Z = sm[:, 0:nch]; Zp = sm[:, 16:16 + nch]; dc = sm[:, 32:32 + nch]
                        P.tt("dve", Z, Bv, mv_, ALU.subtract, [Bb, mb], [smb])
                        if dr == 0:
                            if nch > 1:
                                P.cp("dve", sm[:, 17:16 + nch], sm[:, 0:nch - 1], [smb], [smb])
                            first = sm[:, 16:17]
                        else:
                            if nch > 1:
                                P.cp("dve", sm[:, 16:16 + nch - 1], sm[:, 1:nch], [smb], [smb])
                            first = sm[:, 16 + nch - 1:16 + nch]
                        if is_s:
                            P.cp("dve", first, nm0_sb[:, dr:dr + 1], [gb, smb], [smb])
                        else:
                            P.memset("dve", first, 0.0, [smb])
                        P.tt("dve", dc, Z, Zp, ALU.subtract, [smb], [smb])
                        P.act(dec_st[:, dr, gch:gch + nch], dc, AF.Exp, [smb], [dec_stb])
                        P.tt("dve", At[:, 0:T], ig[:, 0:T], Bt[:, 0:T], ALU.subtract, [igb, Bb], [Ab])
                        Zbc = bc_last(Z, 64)
                        P.tt("dve", wt_[:, 0:T].rearrange("p (c s) -> p c s", s=64),
                             At[:, 0:T].rearrange("p (c s) -> p c s", s=64), Zbc, ALU.add, [Ab, smb], [wtb])
                        P.act(wt_[:, 0:T], wt_[:, 0:T], AF.Exp, [wtb], [wtb])
                        P.tt("dve", tt_[:, 0:T].rearrange("p (c s) -> p c s", s=64), Zbc,
                             Bt[:, 0:T].rearrange("p (c s) -> p c s", s=64), ALU.subtract, [Bb, smb], [ttb])
                        P.act(tt_[:, 0:T], tt_[:, 0:T], AF.Exp, [ttb], [ttb])
                        wq[dr] = (wt_, wtb, tt_, ttb)
                        if not is_s:
                            fin = mt[:, T - 1:T] if dr == 0 else mt[:, 0:1]
                            P.dma("sp", o_m[si, dr:dr + 1, :].rearrange("d h -> h d"), fin, mo_d[dr], reads=[mb])
                    for c in range(nch):
                        ps, pb = P.psum()
                        for dr in range(2):
                            wt_, wtb, tt_, ttb = wq[dr]
                            P.mm(ps[0:64, dr * 8:dr * 8 + 8], wt_[:, c * 64:(c + 1) * 64], ident[0:8, 0:8], True, True,
                                 [wtb, cb], [pb])
                            P.mm(ps[0:64, 16 + dr * 8:24 + dr * 8], tt_[:, c * 64:(c + 1) * 64], ident[0:8, 0:8], True, True,
                                 [ttb, cb], [pb])
                        P.cp("dve", WT[:, gch + c, :], ps[0:64, 0:32], [pb], [WT_b])
                    gch += nch
                ddec = P.dsem("decd")
                P.dma("sp", DECD.rearrange("o (d h c) -> h (o d) c", d=2, h=8), dec_st[:], ddec,
                      reads=[dec_stb], writes=[decd_b])
                ddec2 = P.dsem("decd2")
                P.dma("sp", DEC[:], DECD.partition_broadcast(128)[:, 0, :], ddec2,
                      reads=[decd_b], writes=[DEC_b])
                P.barrier()
                P.emit()

            def scan_gen(kind, sc):
                gla = kind == "gla"
                NH = 4 if gla else 8
                NJ = 4 if gla else 2
                DK = 128 * NJ
                DV = 1024 if gla else 512
                NV = DV // 128
                qsrc, ksrc, vsrc = (QT, KT, VT) if gla else (MQT, MKT, MVT)
                PART, OUT = (PARTG, OG) if gla else (PARTM, OM)
                if True:
                    qT = sb(sc, "u_qT", [128, NJ, 1024], BF16); kT = sb(sc, "u_kT", [128, NJ, 1024], BF16)
                    vT = sb(sc, "u_vT", [128, NV, 1024], BF16)
                    u_b = P.buf("unit"); u_d = P.dsem("unit")
                    ktr = Ring(P, sc, "ktok", 4, [64, DK], BF16); vtr = Ring(P, sc, "vtok", 4, [64, DV], BF16)
                    S = [sb(sc, f"S{d_}", [128, NJ, DV]) for d_ in range(2)]
                    Sb = [sb(sc, f"Sb{d_}", [128, NJ, DV], BF16) for d_ in range(2)]
                    S_b = [P.buf(f"S{d_}") for d_ in range(2)]
                    Sb_b = [P.buf(f"Sb{d_}") for d_ in range(2)]
                    S_d = [P.dsem(f"S{d_}") for d_ in range(2)]
                    if not gla:
                        nst = [sb(sc, f"nst{d_}", [128, 2]) for d_ in range(2)]
                        nbt = [sb(sc, f"nbt{d_}", [128, 2], BF16) for d_ in range(2)]
                    so_d = [P.dsem("stout0"), P.dsem("stout1")]
                    osb = Ring(P, sc, "osb", 2, [64, DV], F32, dsem=True)
                    prt = Ring(P, sc, "prt", 2, [64, DV], F32, dsem=True)
                    if gla:
                        w2u = sb(sc, "w2u", [17, 2, 512]); w2_b = P.buf("w2u"); w2_d = P.dsem("w2u")
                        lar = Ring(P, sc, "la", 2, [64, 512], F32)
                        epr = Ring(P, sc, "ep", 4, [128, 256], F32)
                        enr = Ring(P, sc, "en", 2, [128, 256], F32)
                        err = Ring(P, sc, "er", 2, [64, 512], F32)
                    if gla:
                        qdr = Ring(P, sc, "qd", 4, [128, NJ, 64], BF16)
                        kir = Ring(P, sc, "ki", 4, [128, NJ, 64], BF16)
                    ker = Ring(P, sc, "ke", 4, [64, DK], BF16)
                    atr = Ring(P, sc, "at", 4, [64, 64], BF16)
                    smr2 = Ring(P, sc, "dn", 2, [64, 4], F32)
                    part_bufs = {}
                    gch0 = 0
                    for si, (tok0, T, is_s) in enumerate(SEQS):
                        nch = T // 64
                        for hd in range(NH):
                            P.dma("sp", qT[:, :, 0:T], qsrc[hd * DK:(hd + 1) * DK, tok0:tok0 + T].rearrange(
                                "(j p) t -> p j t", p=128), u_d, writes=[u_b])
                            P.dma("sp", kT[:, :, 0:T], ksrc[hd * DK:(hd + 1) * DK, tok0:tok0 + T].rearrange(
                                "(j p) t -> p j t", p=128), u_d, writes=[u_b])
                            P.dma("sp", vT[:, :, 0:T], vsrc[hd * DV:(hd + 1) * DV, tok0:tok0 + T].rearrange(
                                "(j p) t -> p j t", p=128), u_d, writes=[u_b])
                            if gla:
                                for d_ in range(2):
                                    P.dma("sp", w2u[0:16, d_, :], w_a2[d_, :, hd * 512:(hd + 1) * 512], w2_d, writes=[w2_b])
                                    P.dma("sp", w2u[16:17, d_, :], b_a[d_:d_ + 1, hd * 512:(hd + 1) * 512], w2_d, writes=[w2_b])
                            for dr in range(2):
                                if is_s:
                                    src = (st_S if gla else st_C)[dr, hd, :, :].rearrange("(j p) e -> p j e", p=128)
                                    P.dma("sp", S[dr][:], src, S_d[dr], writes=[S_b[dr]])
                                    if not gla:
                                        P.dma("sp", nst[dr][:], st_n[dr, hd, :].rearrange("(j p) -> p j", p=128), S_d[dr],
                                              writes=[S_b[dr]])
                                else:
                                    P.memset("pool", S[dr][:], 0.0, [S_b[dr]])
                                    if not gla:
                                        P.memset("pool", nst[dr][:], 0.0, [S_b[dr]])
                                if gla:
                                    P.cp("pool", Sb[dr][:], S[dr][:], [S_b[dr]], [Sb_b[dr]])
                            def stageA(it, dr):
                                c = it if dr == 0 else nch - 1 - it
                                second = it >= nch // 2
                                t0 = c * 64
                                g0 = tok0 + t0
                                mask = tri["U_incl" if dr == 0 else "L_incl"][0]
                                ktok, ktb, _ = ktr.next(); vtok, vtb, _ = vtr.next()
                                ps, pb = P.psum()
                                for j in range(NJ):
                                    P.mm(ps[0:64, j * 128:(j + 1) * 128], kT[:, j, t0:t0 + 64], identb[:], True, True,
                                         [u_b, cb], [pb])
                                P.cp("dve", ktok[:], ps[0:64, 0:DK], [pb], [ktb])
                                for h2 in range(DV // 512):
                                    ps, pb = P.psum()
                                    for j in range(4):
                                        P.mm(ps[0:64, j * 128:(j + 1) * 128], vT[:, h2 * 4 + j, t0:t0 + 64], identb[:],
                                             True, True, [u_b, cb], [pb])
                                    P.act(vtok[:, h2 * 512:(h2 + 1) * 512], ps[0:64, 0:512], AF.Copy, [pb], [vtb])
                                if gla:
                                    ps, pb = P.psum()
                                    P.mm(ps[0:64, 0:512], gaA[dr][:, g0:g0 + 64], w2u[:, dr, :],
                                         True, True, [ga_b, w2_b], [pb])
                                    la, lab, _ = lar.next()
                                    P.act(la[:], ps[0:64, 0:512], AF.Exp, [pb], [lab], scale=-1.0)
                                    P.act(la[:], la[:], AF.Ln, [lab], [lab], bias=1.0)
                                    tric = tri["U_incl" if dr == 0 else "L_incl"][1]
                                    tris = tri["L_strict" if dr == 0 else "U_strict"][1]
                                    psc, pcb = P.psum()
                                    for j in range(4):
                                        P.mm(psc[:, j * 64:(j + 1) * 64], la[:, j * 128:(j + 1) * 128], tric[:], True, True,
                                             [lab, cb], [pcb])
                                    psr, prb = P.psum()
                                    P.mm(psr[0:64, 0:512], tris[:], la[:], True, True, [lab, cb], [prb])
                                    ep, epb, _ = epr.next(); en, enb, _ = enr.next(); er, erb, _ = err.next()
                                    P.act(ep[:], psc[:, 0:256], AF.Exp, [pcb], [epb])
                                    P.act(en[:], psc[:, 0:256], AF.Exp, [pcb], [enb], scale=-1.0)
                                    P.act(er[:], psr[0:64, 0:512], AF.Exp, [prb], [erb])
                                    qd, qdb, _ = qdr.next(); ki, kib, _ = kir.next(); ke, keb, _ = ker.next()
                                    P.tt("dve", qd[:], qT[:, :, t0:t0 + 64], ep[:].rearrange("p (j t) -> p j t", t=64),
                                         ALU.mult, [u_b, epb], [qdb])
                                    P.tt("dve", ki[:], kT[:, :, t0:t0 + 64], en[:].rearrange("p (j t) -> p j t", t=64),
                                         ALU.mult, [u_b, enb], [kib])
                                    P.tt("dve", ke[:], ktok[:], er[:], ALU.mult, [ktb, erb], [keb])
                                    q_ap = [qd[:, j, :] for j in range(NJ)]; q_bufs = [qdb]
                                    k_ap = [ki[:, j, :] for j in range(NJ)]; k_bufs = [kib]
                                    last = 63 if dr == 0 else 0
                                    rowsc = [ep[:, j * 64 + last:j * 64 + last + 1] for j in range(NJ)]
                                    rowsc_bufs = [epb]
                                    wcol = None
                                else:
                                    gc = gch0 + c
                                    wcol = WT[:, gc, dr * 8 + hd:dr * 8 + hd + 1]
                                    tcol = WT[:, gc, 16 + dr * 8 + hd:16 + dr * 8 + hd + 1]
                                    di = (dr * 8 + hd) * 24 + gc
                                    dcol = DEC[:, di:di + 1]
                                    ke, keb, _ = ker.next()
                                    P.ts("dve", ke[:], ktok[:], wcol, None, ALU.mult, None, [ktb, WT_b], [keb])
                                    q_ap = [qT[:, j, t0:t0 + 64] for j in range(NJ)]; q_bufs = [u_b]
                                    k_ap = [kT[:, j, t0:t0 + 64] for j in range(NJ)]; k_bufs = [u_b]
                                    rowsc = [dcol] * NJ
                                    rowsc_bufs = [DEC_b]
                                psa, pab = P.psum()
                                for j in range(NJ):
                                    P.mm(psa[0:64, 0:64], k_ap[j], q_ap[j], j == 0, j == NJ - 1, k_bufs + q_bufs, [pab])
                                at, atb, _ = atr.next()
                                if gla:
                                    P.tt("dve", at[:], psa[0:64, 0:64], mask[:], ALU.mult, [pab, cb], [atb])
                                else:
                                    P.stt(at[:], psa[0:64, 0:64], wcol, mask[:], ALU.mult, ALU.mult, [pab, cb, WT_b], [atb])
                                return dict(it=it, dr=dr, c=c, second=second, t0=t0, g0=g0, ktok=ktok, ktb=ktb, vtok=vtok, vtb=vtb,
                                            q_ap=q_ap, q_bufs=q_bufs, rowsc=rowsc, rowsc_bufs=rowsc_bufs, ke=ke, keb=keb, at=at, atb=atb,
                                            dcol=(None if gla else dcol), tcol=(None if gla else tcol))

                            def stageB(x):
                                it, dr, c, second, t0, g0 = x["it"], x["dr"], x["c"], x["second"], x["t0"], x["g0"]
                                ktok, ktb, vtok, vtb = x["ktok"], x["ktb"], x["vtok"], x["vtb"]
                                q_ap, q_bufs, rowsc, rowsc_bufs = x["q_ap"], x["q_bufs"], x["rowsc"], x["rowsc_bufs"]
                                ke, keb, at, atb, dcol, tcol = x["ke"], x["keb"], x["at"], x["atb"], x["dcol"], x["tcol"]
                                if not gla:
                                    P.act(Sb[dr][:], S[dr][:], AF.Copy, [S_b[dr], DEC_b], [Sb_b[dr]], scale=dcol)
                                    P.act(nbt[dr][:], nst[dr][:], AF.Copy, [S_b[dr], DEC_b], [Sb_b[dr]], scale=dcol)
                                ot, otb, otd = osb.next()
                                if second:
                                    pt, ptb, ptd = prt.next()
                                    pbuf = part_bufs[(si, hd, c)]
                                    P.dma("sp", pt[:], PART[g0:g0 + 64, hd * DV:(hd + 1) * DV], ptd, reads=[pbuf], writes=[ptb])
                                if not gla:
                                    psd, pdb = P.psum()
                                    P.mm(psd[0:64, 0:1], at[:], onesb[0:64, :], True, False, [atb, cb], [pdb])
                                    for j in range(NJ):
                                        P.mm(psd[0:64, 0:1], q_ap[j], nbt[dr][:, j:j + 1], False, j == NJ - 1,
                                             q_bufs + [Sb_b[dr]], [pdb])
                                    dn, dnb, _ = smr2.next()
                                    P.act(dn[:, 0:1], psd[0:64, 0:1], AF.Abs, [pdb], [dnb])
                                    P.ts("dve", dn[:, 1:2], dn[:, 0:1], tcol, None, ALU.max, None, [dnb, WT_b], [dnb])
                                    P.op("dve", lambda e, o=dn[:, 2:3], i_=dn[:, 1:2]: e.reciprocal(out=o, in_=i_), [dnb], [dnb])
                                for h2 in range(DV // 512):
                                    pso, pob = P.psum()
                                    P.mm(pso[0:64, 0:512], at[:], vtok[:, h2 * 512:(h2 + 1) * 512], True, False,
                                         [atb, vtb], [pob])
                                    for j in range(NJ):
                                        P.mm(pso[0:64, 0:512], q_ap[j], Sb[dr][:, j, h2 * 512:(h2 + 1) * 512], False, j == NJ - 1,
                                             q_bufs + [Sb_b[dr]], [pob])
                                    osl = ot[:, h2 * 512:(h2 + 1) * 512]
                                    if gla:
                                        if second:
                                            P.tt("dve", osl, pso[0:64, 0:512], pt[:, h2 * 512:(h2 + 1) * 512], ALU.add, [pob, ptb], [otb])
                                        else:
                                            P.act(osl, pso[0:64, 0:512], AF.Copy, [pob], [otb])
                                    else:
                                        if second:
                                            P.stt(osl, pso[0:64, 0:512], dn[:, 2:3], pt[:, h2 * 512:(h2 + 1) * 512], ALU.mult, ALU.add,
                                                  [pob, ptb, dnb], [otb])
                                        else:
                                            P.ts("dve", osl, pso[0:64, 0:512], dn[:, 2:3], None, ALU.mult, None, [pob, dnb], [otb])
                                if second:
                                    if (not gla) and is_s:
                                        dst3 = OUT[tok0:tok0 + T, hd * DV:(hd + 1) * DV].rearrange("(r c) e -> c r e", c=64)
                                        for cl in range(4):
                                            P.dma("sp", dst3[4 * c + cl, :, :], ot[cl * 16:(cl + 1) * 16, :], otd, reads=[otb])
                                    else:
                                        P.dma("sp", OUT[g0:g0 + 64, hd * DV:(hd + 1) * DV], ot[:], otd, reads=[otb])
                                else:
                                    pbuf = P.buf()
                                    part_bufs[(si, hd, c)] = pbuf
                                    P.dma("sp", PART[g0:g0 + 64, hd * DV:(hd + 1) * DV], ot[:], otd, reads=[otb], writes=[pbuf])
                                for j in range(NJ):
                                    for h2 in range(DV // 512):
                                        pss, psb_ = P.psum()
                                        P.mm(pss[:, 0:512], ke[:, j * 128:(j + 1) * 128], vtok[:, h2 * 512:(h2 + 1) * 512], True, True,
                                             [keb, vtb], [psb_])
                                        ssl = S[dr][:, j, h2 * 512:(h2 + 1) * 512]
                                        P.stt(ssl, ssl, rowsc[j], pss[:, 0:512], ALU.mult, ALU.add,
                                              [psb_, S_b[dr], Sb_b[dr]] + rowsc_bufs, [S_b[dr]])
                                    if gla:
                                        P.act(Sb[dr][:, j, :], S[dr][:, j, :], AF.Copy, [S_b[dr]], [Sb_b[dr]])
                                if not gla:
                                    psn, pnb = P.psum()
                                    for j in range(NJ):
                                        P.mm(psn[:, j:j + 1], ke[:, j * 128:(j + 1) * 128], onesb[0:64, :], True, True,
                                             [keb, cb], [pnb])
                                    P.stt(nst[dr][:], nst[dr][:], dcol, psn[:, 0:2], ALU.mult, ALU.add,
                                          [pnb, S_b[dr], Sb_b[dr], DEC_b], [S_b[dr]])

                            pend = {}
                            for it in range(nch + 1):
                                if it < nch:
                                    for dr in range(2):
                                        pend[(it, dr)] = stageA(it, dr)
                                if it >= 1:
                                    for dr in range(2):
                                        stageB(pend.pop((it - 1, dr)))
                                yield
                            if not is_s:
                                for dr in range(2):
                                    dstS = (o_S if gla else o_C)[si, dr, hd, :, :].rearrange("(j p) e -> p j e", p=128)
                                    P.dma("sp", dstS, S[dr][:], so_d[dr], reads=[S_b[dr]])
                                    if not gla:
                                        P.dma("sp", o_n[si, dr, hd, :].rearrange("(j p) -> p j", p=128), nst[dr][:], so_d[dr],
                                              reads=[S_b[dr]])
                        gch0 += nch

            with ExitStack() as sc:
                gens = [scan_gen("gla", sc)]
                pattern = [0]
                if stop_after > 3:
                    gens.append(scan_gen("mlstm", sc))
                    pattern = [0, 1, 1]
                alive = set(range(len(gens)))
                while alive:
                    for gi in pattern:
                        if gi in alive:
                            try:
                                next(gens[gi])
                            except StopIteration:
                                alive.discard(gi)
                P.barrier()
                P.emit()
                print("engine sem counts", {k: e.count for k, e in P.E.items()})
        if stop_after <= 4:
            return nc, dbg

        for branch in ("gla", "mlstm"):
            with ExitStack() as ph:
                src = OG if branch == "gla" else OM
                dstT = UT if branch == "gla" else UMT
                nwT = gnwT if branch == "gla" else mnwT
                g1 = sb(ph, "g1", [128, 32, 512], BF16); g1b = P.buf(); g1d = P.dsem("g1")
                if branch == "mlstm":
                    g2 = sb(ph, "g2", [128, 32, 512], BF16); g2b = P.buf(); g2d = P.dsem("g2")
                ust = sb(ph, "ust", [128, 32, 512], BF16); ustb = P.buf(); ustd = P.dsem("ust")
                oin = Ring(P, ph, "oin", 2, [128, D], F32, dsem=True)
                tmp = sb(ph, "p41tmp", [128, D], F32); tmpb = P.buf()
                sts = Ring(P, ph, "p41s", 2, [128, 40], F32)
                for tb in range(3):
                    P.dma("sp", g1[:], (SZT if branch == "gla" else SZM)[:, tb * 512:(tb + 1) * 512].rearrange(
                        "(k p) t -> p k t", p=128), g1d, writes=[g1b])
                    if branch == "mlstm":
                        P.dma("sp", g2[:], SGO[:, tb * 512:(tb + 1) * 512].rearrange("(k p) t -> p k t", p=128), g2d,
                              writes=[g2b])
                        for k4 in range(4):
                            P.tt("pool", g1[:, k4 * 8:(k4 + 1) * 8, :], g1[:, k4 * 8:(k4 + 1) * 8, :],
                                 g2[:, k4 * 8:(k4 + 1) * 8, :], ALU.mult, [g1b, g2b], [g1b])
                    for ti in range(4):
                        i = tb * 4 + ti
                        ot, ob, od = oin.next()
                        P.dma("sp", ot[:], src[i * 128:(i + 1) * 128, :], od, writes=[ob])
                        s_, s_b, _ = sts.next()
                        NHh = 4 if branch == "gla" else 8
                        E_ = D // NHh
                        o3 = ot[:].rearrange("p (h e) -> p h e", e=E_)
                        sq = tmp[:].rearrange("p (h e) -> p h e", e=E_)
                        if branch == "mlstm":
                            P.op("dve", lambda e, o=s_[:, 0:8], i_=o3: e.tensor_reduce(out=o, in_=i_, op=ALU.add, axis=AX.X),
                                 [ob], [s_b])
                            P.ts("dve", s_[:, 8:16], s_[:, 0:8], 1.0 / E_, None, ALU.mult, None, [s_b], [s_b])
                            P.tt("dve", o3, o3, bc_last(s_[:, 8:16], E_), ALU.subtract, [ob, s_b], [ob])
                        P.tt("pool", sq, o3, o3, ALU.mult, [ob], [tmpb])
                        P.op("dve", lambda e, o=s_[:, 16:16 + NHh], i_=sq: e.tensor_reduce(out=o, in_=i_, op=ALU.add, axis=AX.X),
                             [tmpb], [s_b])
                        P.ts("dve", s_[:, 24:24 + NHh], s_[:, 16:16 + NHh], 1.0 / E_, EPS, ALU.mult, ALU.add, [s_b], [s_b])
                        P.act(s_[:, 24:24 + NHh], s_[:, 24:24 + NHh], AF.Sqrt, [s_b], [s_b])
                        P.op("dve", lambda e, o=s_[:, 32:32 + NHh], i_=s_[:, 24:24 + NHh]: e.reciprocal(out=o, in_=i_), [s_b], [s_b])
                        P.tt("dve", o3, o3, bc_last(s_[:, 32:32 + NHh], E_), ALU.mult, [ob, s_b], [ob])
                        for g in range(8):
                            ps, pb = P.psum()
                            for kk in range(4):
                                k = g * 4 + kk
                                P.tr(ps[:, kk * 128:(kk + 1) * 128], ot[:, k * 128:(k + 1) * 128], ident[:], [ob, cb], [pb])
                            for kk in range(4):
                                k = g * 4 + kk
                                P.stt(ust[:, k, ti * 128:(ti + 1) * 128], ps[:, kk * 128:(kk + 1) * 128], nwT[:, k:k + 1],
                                      g1[:, k, ti * 128:(ti + 1) * 128], ALU.mult, ALU.mult, [pb, vec_b, g1b], [ustb])
                    P.dma("sp", dstT[:, tb * 512:(tb + 1) * 512].rearrange("(k p) t -> p k t", p=128), ust[:], ustd,
                          reads=[ustb])
                P.barrier()
                P.emit()
        if stop_after <= 5:
            return nc, dbg

        tmp32r = [None]

        def gemm_stage(name, W, actsrc, evac_fn, dst, dst_dt):
            with ExitStack() as ph:
                aT = sb(ph, "aT", [128, 32, NTOK], BF16); aT_b = P.buf(); aT_d = P.dsem("aT" + name)
                for tb in range(3):
                    P.dma("sp", aT[:, :, tb * 512:(tb + 1) * 512], actsrc[:, tb * 512:(tb + 1) * 512].rearrange(
                        "(k p) t -> p k t", p=128), aT_d, writes=[aT_b])
                wst = Ring(P, ph, "gwst", 3, [128, 32, 128], F32, dsem=True)
                wbf = Ring(P, ph, "gwbf", 2, [128, 32, 128], BF16)
                stg = Ring(P, ph, "gstg", 2, [128, NTOK], dst_dt, dsem=True)
                aux = Ring(P, ph, "gaux", 2, [128, NTOK], F32, dsem=True) if name == "B" else None
                auxb = Ring(P, ph, "gauxb", 2, [128, NTOK], BF16, dsem=True) if name in ("A", "B") else None
                tmp32r[0] = Ring(P, ph, "gtmp", 2, [128, 512], F32)

                def issue(bi):
                    wt, wb_, wd = wst.next()
                    P.dma("sp", wt[:], W[:, bi * 128:(bi + 1) * 128].rearrange("(k p) n -> p k n", p=128), wd, writes=[wb_])
                    return wt, wb_
                loaded = {0: issue(0), 1: issue(1)}
                casted = {}

                def do_cast(bi):
                    wt, wb_ = loaded.pop(bi)
                    wbt, wbb, _ = wbf.next()
                    P.act(wbt[:, 0:16, :], wt[:, 0:16, :], AF.Copy, [wb_], [wbb])
                    P.act(wbt[:, 16:32, :], wt[:, 16:32, :], AF.Copy, [wb_], [wbb])
                    casted[bi] = (wbt, wbb)
                do_cast(0)
                for bi in range(32):
                    if bi + 2 < 32:
                        loaded[bi + 2] = issue(bi + 2)
                    if bi + 1 < 32:
                        do_cast(bi + 1)
                    wbt, wbb = casted.pop(bi)
                    st_t, st_b, st_d = stg.next()
                    ctx = evac_fn("pre", bi, aux, auxb)
                    for tb in range(3):
                        ps, pb = P.psum()
                        for k in range(32):
                            P.mm(ps[:, 0:512], wbt[:, k, :], aT[:, k, tb * 512:(tb + 1) * 512], k == 0, k == 31, [wbb, aT_b], [pb])
                        evac_fn("evac", bi, aux, auxb, ctx=ctx, ps=ps, pb=pb, tb=tb, st_t=st_t, st_b=st_b)
                    P.dma("sp", dst[bi * 128:(bi + 1) * 128, :], st_t[:], st_d, reads=[st_b])
                P.barrier()
                P.emit()

        def evac_A(mode, bi, aux, auxb, ctx=None, ps=None, pb=None, tb=None, st_t=None, st_b=None):
            if mode == "pre":
                gt, gb_, gd = auxb.next()
                P.dma("sp", gt[:], G01[bi * 128:(bi + 1) * 128, :], gd, writes=[gb_])
                return (gt, gb_)
            gt, gb_ = ctx
            P.tt("dve", st_t[:, tb * 512:(tb + 1) * 512], ps[:, 0:512], gt[:, tb * 512:(tb + 1) * 512], ALU.mult,
                 [pb, gb_], [st_b])

        def evac_B(mode, bi, aux, auxb, ctx=None, ps=None, pb=None, tb=None, st_t=None, st_b=None):
            if mode == "pre":
                gt, gb_, gd = auxb.next()
                P.dma("sp", gt[:], G01[4096 + bi * 128:4096 + (bi + 1) * 128, :], gd, writes=[gb_])
                t1t, t1b, t1d = aux.next()
                P.dma("sp", t1t[:], T1[bi * 128:(bi + 1) * 128, :], t1d, writes=[t1b])
                return (gt, gb_, t1t, t1b)
            gt, gb_, t1t, t1b = ctx
            sl = slice(tb * 512, (tb + 1) * 512)
            tm, tmb, _ = tmp32r[0].next()
            P.tt("dve", tm[:], ps[:, 0:512], gt[:, sl], ALU.mult, [pb, gb_], [tmb])
            P.tt("pool", st_t[:, sl], tm[:], t1t[:, sl], ALU.add, [tmb, t1b], [st_b])

        def evac_C(mode, bi, aux, auxb, ctx=None, ps=None, pb=None, tb=None, st_t=None, st_b=None):
            if mode == "pre":
                return None
            j = 0 if tb == 0 else 1
            P.ts("dve", st_t[:, tb * 512:(tb + 1) * 512], ps[:, 0:512], gateT[:, bi, j:j + 1], None, ALU.mult, None,
                 [pb, modT_b], [st_b])

        gemm_stage("A", w_gp, UT, evac_A, T1, F32)
        gemm_stage("B", w_mp, UMT, evac_B, MIXT, BF16)
        gemm_stage("C", w_o, MIXT, evac_C, OUTT, F32)
        if stop_after <= 6:
            return nc, dbg

        with ExitStack() as ph:
            fnw = sb(ph, "fnw", [128, D]); fnw_b = P.buf(); fd = P.dsem("fnw")
            P.dma("sp", fnw[:], f_nw.partition_broadcast(128)[:, 0, :], fd, writes=[fnw_b])
            xin = Ring(P, ph, "x5", 2, [128, D], F32, dsem=True)
            oin = Ring(P, ph, "o5", 2, [128, 32, 128], F32, dsem=True)
            yst = Ring(P, ph, "y5", 2, [128, D], F32, dsem=True)
            junk = sb(ph, "junk5", [128, D], BF16); junk_b = P.buf()
            sts = Ring(P, ph, "s5", 2, [128, 4], F32)
            for i in range(12):
                xt, xb, xd = xin.next()
                P.dma("sp", xt[:], x_all[i * 128:(i + 1) * 128, :], xd, writes=[xb])
                ot, ob, od = oin.next()
                P.dma("sp", ot[:], OUTT[:, i * 128:(i + 1) * 128].rearrange("(k p) t -> p k t", p=128), od, writes=[ob])
                for g in range(8):
                    ps, pb = P.psum()
                    for kk in range(4):
                        k = g * 4 + kk
                        P.tr(ps[:, kk * 128:(kk + 1) * 128], ot[:, k, :], ident[:], [ob, cb], [pb])
                    P.tt("dve", xt[:, g * 512:(g + 1) * 512], ps[:, 0:512], xt[:, g * 512:(g + 1) * 512], ALU.add,
                         [pb, xb], [xb])
                s_, s_b, _ = sts.next()
                P.act(junk[:], xt[:], AF.Square, [xb], [junk_b, s_b], accum_out=s_[:, 0:1])
                P.ts("dve", s_[:, 1:2], s_[:, 0:1], 1.0 / D, EPS, ALU.mult, ALU.add, [s_b], [s_b])
                P.act(s_[:, 2:3], s_[:, 1:2], AF.Sqrt, [s_b], [s_b])
                P.op("dve", lambda e, o=s_[:, 3:4], i_=s_[:, 2:3]: e.reciprocal(out=o, in_=i_), [s_b], [s_b])
                yt, yb, yd = yst.next()
                P.stt(yt[:], xt[:], s_[:, 3:4], fnw[:], ALU.mult, ALU.mult, [xb, s_b, fnw_b], [yb])
                P.dma("sp", y_all[i * 128:(i + 1) * 128, :], yt[:], yd, reads=[yb])
            P.barrier()
            P.emit()
    return nc, dbg


_CACHE = {}


def _get_nc():
    if "nc" not in _CACHE:
        _CACHE["nc"] = build_program()[0]
    return _CACHE["nc"]


def make_in_maps(inputs):
    f = lambda a: np.ascontiguousarray(np.asarray(a, dtype=np.float32))
    x_prompt, x_sample, c = f(inputs["x_prompt"]), f(inputs["x_sample"]), f(inputs["c"])
    shared = {
        "w_ada": f(inputs["w_ada"][0]), "b_ada": f(inputs["b_ada"]), "w_in": f(inputs["w_in"][0]),
        "w_a2": f(inputs["gla_w_a2"][0]), "b_a": f(inputs["gla_b_a"][0]), "b_i": f(inputs["mlstm_b_i"][0]),
        "b_f": f(inputs["mlstm_b_f"][0]), "b_mg": f(inputs["b_merge"]), "g_nw": f(inputs["gla_norm_w"]),
        "m_nw": f(inputs["mlstm_norm_w"]), "w_gp": f(inputs["w_gla_proj"][0]), "w_mp": f(inputs["w_mlstm_proj"][0]),
        "w_o": f(inputs["w_out"][0]), "f_nw": f(inputs["final_norm_w"]).reshape(1, D),
    }
    c_ctx = f(inputs["c_ctx"])
    sS, sC, sn, sm = (f(inputs[k]) for k in ("state_gla_S", "state_mlstm_C", "state_mlstm_n", "state_mlstm_m"))
    maps = []
    for core in range(8):
        m = dict(shared)
        m["x_all"] = np.concatenate([x_prompt[2 * core], x_prompt[2 * core + 1], x_sample[core]], axis=0)
        m["cvec"] = np.stack([c_ctx, c[core]], axis=0)
        m["st_S"] = sS[core, 0]
        m["st_C"] = sC[core, 0]
        m["st_n"] = sn[core, 0]
        m["st_m"] = sm[core, 0]
        maps.append(m)
    return maps


def kernel(**inputs):
    nc = _get_nc()
    maps = make_in_maps(inputs)
    res = run_bass_kernel_spmd(nc, maps, core_ids=list(range(8))).results
    y_prompt = np.stack([res[i // 2]["y_all"][(i % 2) * 256:(i % 2) * 256 + 256] for i in range(16)], axis=0)
    y_sample = np.stack([res[i]["y_all"][512:1536] for i in range(8)], axis=0)
    new_S = np.concatenate([res[i]["o_S"] for i in range(8)], axis=0)[:, None]
    new_C = np.concatenate([res[i]["o_C"] for i in range(8)], axis=0)[:, None]
    new_n = np.concatenate([res[i]["o_n"] for i in range(8)], axis=0)[:, None]
    new_m = np.concatenate([res[i]["o_m"] for i in range(8)], axis=0)[:, None]
    return (y_prompt.astype(np.float32), y_sample.astype(np.float32), new_S.astype(np.float32),
            new_C.astype(np.float32), new_n.astype(np.float32), new_m.astype(np.float32))
```

```python
import numpy as np
from contextlib import ExitStack
import concourse.bass as bass
import concourse.mybir as mybir
from concourse.bass_utils import run_bass_kernel_spmd

F32 = mybir.dt.float32
BF16 = mybir.dt.bfloat16
AF = mybir.ActivationFunctionType
ALU = mybir.AluOpType
AX = mybir.AxisListType

D = 4096
NTOK = 1536
TP = 256
TS = 1024
N_IN = 36928
EPS = 1e-6
SEQS = [(0, 256, False), (256, 256, False), (512, 1024, True)]

C_GQ, C_GK, C_GV, C_GZ, C_GA = 0, 2048, 4096, 8192, 12288
C_MQ, C_MK, C_MV, C_MZ, C_MO, C_MI, C_MF, C_MG = 12320, 14368, 16416, 20512, 24608, 28704, 28720, 28736


class Buf:
    __slots__ = ("name", "w", "r")

    def __init__(self, name=""):
        self.name = name
        self.w = None
        self.r = {}


class DSem:
    def __init__(self, sem, name):
        self.sem = sem
        self.count = 0
        self.name = name


class Eng:
    def __init__(self, name, eng, sem, is_pe=False):
        self.name = name
        self.eng = eng
        self.sem = sem
        self.count = 0
        self.seen = {}
        self.ops = []
        self.is_pe = is_pe


class Prog:
    def __init__(self, nc, es):
        self.nc = nc
        self.es = es
        self.E = {}
        for name, eng in (("pe", nc.tensor), ("act", nc.scalar), ("dve", nc.vector),
                          ("pool", nc.gpsimd), ("sp", nc.sync)):
            sem = es.enter_context(nc.semaphore("s_" + name))
            self.E[name] = Eng(name, eng, sem, is_pe=(name == "pe"))
        self.dsems = []
        self.psum_banks = []
        self.psum_i = 0
        self.nbuf = 0

    def buf(self, name=""):
        self.nbuf += 1
        return Buf(name or f"b{self.nbuf}")

    def dsem(self, name):
        self.nbuf += 1
        sem = self.es.enter_context(self.nc.semaphore(f"d_{name}_{self.nbuf}"))
        d = DSem(sem, name)
        self.dsems.append(d)
        return d

    def init_psum(self):
        for i in range(8):
            t = self.es.enter_context(self.nc.psum_tensor(f"psb{i}", [128, 512], F32))
            self.psum_banks.append((t, self.buf(f"psb{i}")))

    def psum(self):
        t, b = self.psum_banks[self.psum_i % 8]
        self.psum_i += 1
        return t, b

    def _deps(self, reads, writes):
        deps = []
        for b in reads:
            if b.w is not None:
                deps.append(b.w)
        for b in writes:
            if b.w is not None:
                deps.append(b.w)
            deps.extend(b.r.values())
        return deps

    def _reduce(self, E, deps, skip_sem=None):
        need = {}
        for sem, val in deps:
            if sem is skip_sem:
                continue
            if E.is_pe and sem is E.sem:
                continue
            k = id(sem)
            if E.seen.get(k, 0) >= val:
                continue
            if k not in need or need[k][1] < val:
                need[k] = (sem, val)
        for k, (sem, val) in need.items():
            E.seen[k] = val
        return list(need.values())

    def _mark(self, tok, reads, writes):
        sem, val = tok
        k = id(sem)
        for b in reads:
            if k not in b.r or b.r[k][1] < val:
                b.r[k] = tok
        for b in writes:
            b.w = tok
            b.r = {}

    def op(self, eng, fn, reads=(), writes=(), signal=True):
        E = self.E[eng]
        if not E.is_pe:
            signal = True
        deps = self._deps(reads, writes)
        waits = self._reduce(E, deps)
        tok = (E.sem, E.count + 1)
        if signal:
            E.count += 1
        E.ops.append(("op", fn, waits, signal))
        self._mark(tok, reads, writes)
        return tok

    def dma(self, q, out, in_, ds, reads=(), writes=()):
        E = self.E[q]
        deps = []
        for b in reads:
            if b.w is not None:
                deps.append(b.w)
        for b in writes:
            if b.w is not None and b.w[0] is not ds.sem:
                deps.append(b.w)
            deps.extend(b.r.values())
        for sem, val in deps:
            assert not (sem is ds.sem and val >= ds.count + 16), "self-dependency on DMA semaphore"
        waits = self._reduce(E, deps)
        ds.count += 16
        tok = (ds.sem, ds.count)
        E.ops.append(("dma", (out, in_), waits, ds.sem))
        self._mark(tok, reads, writes)
        return tok

    def barrier(self):
        toks = [(e.sem, e.count) for e in self.E.values() if e.count > 0]
        toks += [(d.sem, d.count) for d in self.dsems if d.count > 0]
        for E in self.E.values():
            waits = self._reduce(E, [t for t in toks if not (t[0] is E.sem)])
            if waits:
                E.ops.append(("wait", None, waits, False))

    def emit(self):
        nc = self.nc
        for e in self.E.values():
            assert e.count < 65000, (e.name, e.count)
        for d in self.dsems:
            assert d.count < 65000, (d.name, d.count)

        def run(E, eng):
            for kind, payload, waits, sig in E.ops:
                for sem, val in waits:
                    eng.wait_ge(sem, val)
                if kind == "op":
                    ins = payload(eng)
                    if sig:
                        ins.then_inc(E.sem, 1)
                elif kind == "dma":
                    out, in_ = payload
                    eng.dma_start(out=out, in_=in_).then_inc(sig, 16)
            E.ops = []

        with nc.Block() as block:
            @block.tensor
            def _(eng):
                run(self.E["pe"], eng)

            @block.scalar
            def _(eng):
                run(self.E["act"], eng)

            @block.vector
            def _(eng):
                run(self.E["dve"], eng)

            @block.gpsimd
            def _(eng):
                run(self.E["pool"], eng)

            @block.sync
            def _(eng):
                run(self.E["sp"], eng)

    def mm(self, out, lhsT, rhs, start, stop, reads, writes):
        self.op("pe", lambda e: e.matmul(out, lhsT, rhs, start=start, stop=stop),
                reads, writes, signal=stop)

    def tr(self, out, in_, ident, reads, writes):
        self.op("pe", lambda e: e.transpose(out, in_, ident), reads, writes, signal=True)

    def act(self, out, in_, func, reads, writes, **kw):
        self.op("act", lambda e: e.activation(out=out, in_=in_, func=func, **kw), reads, writes)

    def ts(self, eng, out, in0, s1, s2, op0, op1, reads, writes):
        if s2 is None:
            self.op(eng, lambda e: e.tensor_scalar(out=out, in0=in0, scalar1=s1, scalar2=None, op0=op0),
                    reads, writes)
        else:
            self.op(eng, lambda e: e.tensor_scalar(out=out, in0=in0, scalar1=s1, scalar2=s2, op0=op0, op1=op1),
                    reads, writes)

    def tt(self, eng, out, in0, in1, op, reads, writes):
        self.op(eng, lambda e: e.tensor_tensor(out=out, in0=in0, in1=in1, op=op), reads, writes)

    def stt(self, out, in0, scalar, in1, op0, op1, reads, writes):
        self.op("dve", lambda e: e.scalar_tensor_tensor(out=out, in0=in0, scalar=scalar, in1=in1,
                                                        op0=op0, op1=op1), reads, writes)

    def cp(self, eng, out, in_, reads, writes):
        self.op(eng, lambda e: e.tensor_copy(out=out, in_=in_), reads, writes)

    def memset(self, eng, out, val, writes):
        self.op(eng, lambda e: e.memset(out, val), (), writes)


class Ring:
    def __init__(self, P, es, name, n, shape, dtype, dsem=False):
        self.slots = []
        for i in range(n):
            P.nbuf += 1
            t = es.enter_context(P.nc.sbuf_tensor(f"{name}{i}_{P.nbuf}", shape, dtype))
            self.slots.append((t, P.buf(f"{name}{i}"), P.dsem(f"{name}{i}") if dsem else None))
        self.i = 0

    def next(self):
        s = self.slots[self.i % len(self.slots)]
        self.i += 1
        return s


def bc_last(ap, n):
    return bass.AP(ap.tensor, ap.offset, [list(x) for x in ap.ap] + [[0, n]])


def build_program(debug=False, stop_after=99):
    nc = bass.Bass("TRN2", target_bir_lowering=False)

    def din(name, shape):
        return nc.dram_tensor(name, shape, F32, kind="ExternalInput").ap()

    def dout(name, shape):
        return nc.dram_tensor(name, shape, F32, kind="ExternalOutput").ap()

    dbg = {}

    def dscr(name, shape, dt=F32):
        if debug and name in debug:
            t = nc.dram_tensor(name, shape, dt, kind="ExternalOutput").ap()
            dbg[name] = t
            return t
        return nc.dram_tensor(name, shape, dt, kind="Internal").ap()

    x_all = din("x_all", [NTOK, D])
    cvec = din("cvec", [2, D])
    st_S = din("st_S", [2, 4, 512, 1024])
    st_C = din("st_C", [2, 8, 256, 512])
    st_n = din("st_n", [2, 8, 256])
    st_m = din("st_m", [2, 8])
    w_ada = din("w_ada", [D, 3 * D])
    b_ada = din("b_ada", [1, 3 * D])
    w_in = din("w_in", [D, N_IN])
    w_a2 = din("w_a2", [2, 16, 2048])
    b_a = din("b_a", [2, 2048])
    b_i = din("b_i", [2, 8])
    b_f = din("b_f", [2, 8])
    b_mg = din("b_mg", [1, 2 * D])
    g_nw = din("g_nw", [1, D])
    m_nw = din("m_nw", [1, D])
    w_gp = din("w_gp", [D, D])
    w_mp = din("w_mp", [D, D])
    w_o = din("w_o", [D, D])
    f_nw = din("f_nw", [1, D])
    y_all = dout("y_all", [NTOK, D])
    o_S = dout("o_S", [2, 2, 4, 512, 1024])
    o_C = dout("o_C", [2, 2, 8, 256, 512])
    o_n = dout("o_n", [2, 2, 8, 256])
    o_m = dout("o_m", [2, 2, 8])
    MODD = dscr("MODD", [2, 3 * D])
    QT = dscr("QT", [2048, NTOK], BF16)
    KT = dscr("KT", [2048, NTOK], BF16)
    VT = dscr("VT", [4096, NTOK], BF16)
    SZT = dscr("SZT", [4096, NTOK], BF16)
    GAT = dscr("GAT", [2, 16, NTOK])
    MQT = dscr("MQT", [2048, NTOK], BF16)
    MKT = dscr("MKT", [2048, NTOK], BF16)
    MVT = dscr("MVT", [4096, NTOK], BF16)
    SZM = dscr("SZM", [4096, NTOK], BF16)
    SGO = dscr("SGO", [4096, NTOK], BF16)
    GIF = dscr("GIF", [4, 8, NTOK])
    G01 = dscr("G01", [8192, NTOK], BF16)
    DECD = dscr("DECD", [1, 2 * 8 * 24])
    PARTG = dscr("PARTG", [NTOK, 4096])
    OG = dscr("OG", [NTOK, 4096])
    PARTM = dscr("PARTM", [NTOK, 4096])
    OM = dscr("OM", [NTOK, 4096])
    UT = dscr("UT", [4096, NTOK], BF16)
    UMT = dscr("UMT", [4096, NTOK], BF16)
    T1 = dscr("T1", [4096, NTOK])
    MIXT = dscr("MIXT", [4096, NTOK], BF16)
    OUTT = dscr("OUTT", [4096, NTOK])

    es = ExitStack()
    with es:
        P = Prog(nc, es)
        P.init_psum()
        es.enter_context(nc.allow_non_contiguous_dma(reason="small strided loads"))

        uniq = [0]

        def sb(stack, name, shape, dt=F32):
            uniq[0] += 1
            return stack.enter_context(nc.sbuf_tensor(f"{name}_{uniq[0]}", shape, dt))

        cb = P.buf("consts")
        ident = sb(es, "ident", [128, 128])
        identb = sb(es, "identb", [128, 128], BF16)
        tri = {}
        P.memset("pool", ident[:], 0.0, [cb])
        P.op("pool", lambda e: e.affine_select(out=ident[:], in_=ident[:], compare_op=ALU.not_equal, fill=1.0,
                                               base=0, pattern=[[-1, 128]], channel_multiplier=1), [cb], [cb])
        P.cp("pool", identb[:], ident[:], [cb], [cb])
        specs = {"U_incl": (-1, 1, ALU.is_ge), "L_incl": (1, -1, ALU.is_ge),
                 "L_strict": (1, -1, ALU.is_gt), "U_strict": (-1, 1, ALU.is_gt)}
        for nm, (cm, st, cmp_) in specs.items():
            t1 = sb(es, "m1_" + nm, [64, 64])
            t2 = sb(es, "m2_" + nm, [64, 64])
            P.memset("pool", t1[:], 1.0, [cb])

            def _sel(e, t1=t1, cm=cm, st=st, cmp_=cmp_):
                return e.affine_select(out=t1[:], in_=t1[:], compare_op=cmp_, fill=0.0, base=0,
                                       pattern=[[st, 64]], channel_multiplier=cm)
            P.op("pool", _sel, [cb], [cb])
            P.ts("pool", t2[:], t1[:], -1.0 / 16.0, None, ALU.mult, None, [cb], [cb])
            tri[nm] = (t1, t2)
        onesb = sb(es, "onesb", [128, 1], BF16)
        P.memset("pool", onesb[:], 1.0, [cb])
        scale1T = sb(es, "scale1T", [128, 32, 2])
        shiftT = sb(es, "shiftT", [128, 32, 2])
        gateT = sb(es, "gateT", [128, 32, 2])
        modT_b = P.buf("modT")
        gnwT = sb(es, "gnwT", [128, 32])
        mnwT = sb(es, "mnwT", [128, 32])
        bmgT = sb(es, "bmgT", [128, 64])
        vec_b = P.buf("vecs")
        dv_ = P.dsem("vecs")
        P.dma("sp", gnwT[:], g_nw.rearrange("o (k p) -> p (o k)", p=128), dv_, writes=[vec_b])
        P.dma("sp", mnwT[:], m_nw.rearrange("o (k p) -> p (o k)", p=128), dv_, writes=[vec_b])
        P.dma("sp", bmgT[:], b_mg.rearrange("o (k p) -> p (o k)", p=128), dv_, writes=[vec_b])

        with ExitStack() as ph:
            c_sb = sb(ph, "c_sb", [2, D]); c_b = P.buf()
            d0 = P.dsem("p0c")
            P.dma("sp", c_sb[:], cvec[:, :], d0, writes=[c_b])
            P.act(c_sb[:], c_sb[:], AF.Silu, [c_b], [c_b])
            scT = sb(ph, "scT", [128, 32, 2]); scT_b = P.buf()
            ps, pb = P.psum()
            for k in range(32):
                P.tr(ps[:, 2 * k:2 * k + 2], c_sb[:, k * 128:(k + 1) * 128], ident[0:2, 0:2], [c_b, cb], [pb])
            P.cp("dve", scT[:].rearrange("p k j -> p (k j)"), ps[:, 0:64], [pb], [scT_b])
            mod_sb = sb(ph, "mod_sb", [2, 3 * D]); mod_b = P.buf()
            bada = sb(ph, "bada", [2, 3 * D]); bada_b = P.buf()
            d1 = P.dsem("p0b")
            P.dma("sp", bada[:], b_ada.partition_broadcast(2)[:, 0, :], d1, writes=[bada_b])
            wr = Ring(P, ph, "wada", 2, [128, 32, 256], F32, dsem=True)
            for nb in range(48):
                wt, wb_, wd = wr.next()
                P.dma("sp", wt[:], w_ada[:, nb * 256:(nb + 1) * 256].rearrange("(k p) n -> p k n", p=128), wd,
                      writes=[wb_])
                ps, pb = P.psum()
                for k in range(32):
                    P.mm(ps[0:2, 0:256], scT[:, k, :], wt[:, k, :], k == 0, k == 31, [scT_b, wb_], [pb])
                P.tt("dve", mod_sb[:, nb * 256:(nb + 1) * 256], ps[0:2, 0:256], bada[:, nb * 256:(nb + 1) * 256],
                     ALU.add, [pb, bada_b], [mod_b])
            P.ts("dve", mod_sb[:, D:2 * D], mod_sb[:, D:2 * D], 1.0, None, ALU.add, None, [mod_b], [mod_b])
            dm = P.dsem("p0m")
            modd_b = P.buf("MODD")
            P.dma("sp", MODD[:, :], mod_sb[:], dm, reads=[mod_b], writes=[modd_b])
            for part, dst in ((0, shiftT), (1, scale1T), (2, gateT)):
                ps, pb = P.psum()
                for k in range(32):
                    P.tr(ps[:, 2 * k:2 * k + 2], mod_sb[:, part * D + k * 128: part * D + (k + 1) * 128],
                         ident[0:2, 0:2], [mod_b, cb], [pb])
                P.cp("dve", dst[:].rearrange("p k j -> p (k j)"), ps[:, 0:64], [pb], [modT_b])
            P.barrier()
            P.emit()
        if stop_after <= 0:
            return nc, dbg

        with ExitStack() as ph:
            hT = sb(ph, "hT", [128, 32, NTOK], BF16); hT_b = P.buf("hT")
            with ExitStack() as p1:
                xr = Ring(P, p1, "xin", 2, [128, D], F32, dsem=True)
                junk = sb(p1, "junk", [128, D], BF16); junk_b = P.buf()
                st_r = Ring(P, p1, "stat", 2, [128, 4], F32)
                for i in range(12):
                    j = 0 if i < 4 else 1
                    xt, xb, xd = xr.next()
                    P.dma("sp", xt[:], x_all[i * 128:(i + 1) * 128, :], xd, writes=[xb])
                    stt_, sbf, _ = st_r.next()
                    P.act(junk[:], xt[:], AF.Square, [xb], [junk_b, sbf], accum_out=stt_[:, 0:1])
                    P.ts("dve", stt_[:, 1:2], stt_[:, 0:1], 1.0 / D, EPS, ALU.mult, ALU.add, [sbf], [sbf])
                    P.act(stt_[:, 2:3], stt_[:, 1:2], AF.Sqrt, [sbf], [sbf])
                    P.op("dve", lambda e, o=stt_[:, 3:4], i_=stt_[:, 2:3]: e.reciprocal(out=o, in_=i_), [sbf], [sbf])
                    P.ts("dve", xt[:], xt[:], stt_[:, 3:4], None, ALU.mult, None, [xb, sbf], [xb])
                    for g in range(8):
                        ps, pb = P.psum()
                        for kk in range(4):
                            k = g * 4 + kk
                            P.tr(ps[:, kk * 128:(kk + 1) * 128], xt[:, k * 128:(k + 1) * 128], ident[:], [xb, cb], [pb])
                        for kk in range(4):
                            k = g * 4 + kk
                            if kk % 2 == 0:
                                P.ts("dve", hT[:, k, i * 128:(i + 1) * 128], ps[:, kk * 128:(kk + 1) * 128],
                                     scale1T[:, k, j:j + 1], shiftT[:, k, j:j + 1], ALU.mult, ALU.add,
                                     [pb, modT_b], [hT_b])
                            else:
                                P.act(hT[:, k, i * 128:(i + 1) * 128], ps[:, kk * 128:(kk + 1) * 128], AF.Identity,
                                      [pb, modT_b], [hT_b], scale=scale1T[:, k, j:j + 1], bias=shiftT[:, k, j:j + 1])
                P.barrier()
                P.emit()

            with ExitStack() as p2:
                wst = Ring(P, p2, "wst", 3, [128, 32, 128], F32, dsem=True)
                wbf = Ring(P, p2, "wbf", 2, [128, 32, 128], BF16)
                stg = Ring(P, p2, "stg", 2, [128, NTOK], BF16, dsem=True)
                stg32 = Ring(P, p2, "stg32", 1, [16, NTOK], F32, dsem=True)
                scr_b = P.buf("scratch_p2")

                def perm_views(out_t, ps, tb, nrows):
                    r0 = (tb - 1) * 8
                    o = out_t[0:nrows, 512:1536].rearrange("p (c r) -> p c r", r=16)[:, :, r0:r0 + 8]
                    i_ = ps[0:nrows, 0:512].rearrange("p (r c) -> p c r", c=64)
                    return o, i_

                def evac(kind, out_t, out_b, ps, pb, tb, nrows, bias_ap=None, scale=1.0, perm=False):
                    if perm and tb > 0:
                        o, i_ = perm_views(out_t, ps, tb, nrows)
                    else:
                        o, i_ = out_t[0:nrows, tb * 512:(tb + 1) * 512], ps[0:nrows, 0:512]
                    if kind == "copy":
                        if scale != 1.0:
                            P.ts("dve", o, i_, scale, None, ALU.mult, None, [pb], [out_b])
                        else:
                            P.cp("dve", o, i_, [pb], [out_b])
                    elif kind == "silu":
                        P.act(o, i_, AF.Silu, [pb], [out_b])
                    elif kind == "sigmoid":
                        if bias_ap is not None:
                            P.act(o, i_, AF.Sigmoid, [pb, vec_b], [out_b], bias=bias_ap)
                        else:
                            P.act(o, i_, AF.Sigmoid, [pb], [out_b])

                sections = [
                    (C_GQ, 16, QT, "copy", 512 ** -0.5, False),
                    (C_GK, 16, KT, "copy", 1.0, False),
                    (C_GV, 32, VT, "copy", 1.0, False),
                    (C_GZ, 32, SZT, "silu", 1.0, False),
                    (C_MQ, 16, MQT, "copy", 256 ** -0.5, True),
                    (C_MK, 16, MKT, "copy", 1.0, True),
                    (C_MV, 32, MVT, "copy", 1.0, True),
                    (C_MZ, 32, SZM, "silu", 1.0, False),
                    (C_MO, 32, SGO, "sigmoid", 1.0, False),
                    (C_MG, 64, G01, "sigmoidb", 1.0, False),
                ]
                blocks = []
                for (c0, nblk, dst, kind, scale, perm) in sections:
                    for b in range(nblk):
                        blocks.append(("reg", c0 + b * 128, dst, b, kind, scale, perm))
                blocks.append(("ga", C_GA, None, 0, None, 1.0, False))
                blocks.append(("gif", C_MI, None, 0, None, 1.0, True))
                nblk_total = len(blocks)

                def issue_load(bi):
                    typ, c0 = blocks[bi][0], blocks[bi][1]
                    wt, wb_, wd = wst.next()
                    ncol = 128 if typ == "reg" else 32
                    P.dma("sp", wt[:, :, 0:ncol], w_in[:, c0:c0 + ncol].rearrange("(k p) n -> p k n", p=128), wd,
                          writes=[wb_])
                    return (wt, wb_)

                loaded = {}
                loaded[0] = issue_load(0)
                loaded[1] = issue_load(1)
                casted = {}

                def do_cast(bi):
                    wt, wb_ = loaded.pop(bi)
                    wbt, wbb, _ = wbf.next()
                    ncol = 128 if blocks[bi][0] == "reg" else 32
                    P.cp("dve", wbt[:, 0:16, 0:ncol], wt[:, 0:16, 0:ncol], [wb_], [wbb])
                    P.cp("pool", wbt[:, 16:32, 0:ncol], wt[:, 16:32, 0:ncol], [wb_], [wbb])
                    casted[bi] = (wbt, wbb)
                do_cast(0)
                for bi, (typ, c0, dst, b, kind, scale, perm) in enumerate(blocks):
                    if bi + 2 < nblk_total:
                        loaded[bi + 2] = issue_load(bi + 2)
                    if bi + 1 < nblk_total:
                        do_cast(bi + 1)
                    wbt, wbb = casted.pop(bi)
                    if typ == "reg":
                        st_t, st_b, st_d = stg.next()
                        for tb in range(3):
                            ps, pb = P.psum()
                            for k in range(32):
                                P.mm(ps[:, 0:512], wbt[:, k, :], hT[:, k, tb * 512:(tb + 1) * 512], k == 0, k == 31,
                                     [wbb, hT_b], [pb])
                            if kind == "sigmoidb":
                                evac("sigmoid", st_t, st_b, ps, pb, tb, 128, bias_ap=bmgT[:, b:b + 1])
                            else:
                                evac(kind, st_t, st_b, ps, pb, tb, 128, scale=scale, perm=perm)
                        P.dma("sp", dst[b * 128:(b + 1) * 128, :], st_t[:], st_d, reads=[st_b], writes=[scr_b])
                    elif typ == "ga":
                        for dr in range(2):
                            st_t, st_b, st_d = stg32.next()
                            for tb in range(3):
                                ps, pb = P.psum()
                                for k in range(32):
                                    P.mm(ps[0:16, 0:512], wbt[:, k, dr * 16:(dr + 1) * 16],
                                         hT[:, k, tb * 512:(tb + 1) * 512], k == 0, k == 31, [wbb, hT_b], [pb])
                                evac("copy", st_t, st_b, ps, pb, tb, 16)
                            P.dma("sp", GAT[dr, :, :], st_t[0:16, :], st_d, reads=[st_b], writes=[scr_b])
                    else:
                        for q in range(4):
                            st_t, st_b, st_d = stg32.next()
                            for tb in range(3):
                                ps, pb = P.psum()
                                for k in range(32):
                                    P.mm(ps[0:8, 0:512], wbt[:, k, q * 8:(q + 1) * 8],
                                         hT[:, k, tb * 512:(tb + 1) * 512], k == 0, k == 31, [wbb, hT_b], [pb])
                                evac("copy", st_t, st_b, ps, pb, tb, 8, perm=True)
                            P.dma("sp", GIF[q, :, :], st_t[0:8, :], st_d, reads=[st_b], writes=[scr_b])
                P.barrier()
                P.emit()
        if stop_after <= 2:
            return nc, dbg

        with ExitStack() as ph:
            gaA = [sb(ph, f"gaA{d_}", [17, NTOK]) for d_ in range(2)]
            ga_b = P.buf("gaA")
            dga = P.dsem("gaA")
            for d_ in range(2):
                P.memset("pool", gaA[d_][:], 1.0, [ga_b])
            for d_ in range(2):
                P.dma("sp", gaA[d_][0:16, :], GAT[d_, :, :], dga, writes=[ga_b])

            WT = sb(ph, "WT", [64, 24, 32]); WT_b = P.buf("WT")
            DEC = sb(ph, "DEC", [128, 384]); DEC_b = P.buf("DEC")
            decd_b = P.buf("DECD")
            with ExitStack() as p3b:
                bi_sb = sb(p3b, "bi_sb", [8, 2]); bf_sb = sb(p3b, "bf_sb", [8, 2]); nbf_sb = sb(p3b, "nbf_sb", [8, 2])
                m0_sb = sb(p3b, "m0_sb", [8, 2]); nm0_sb = sb(p3b, "nm0_sb", [8, 2])
                ones8 = sb(p3b, "ones8", [8, 1024]); o8b = P.buf("ones8")
                P.memset("pool", ones8[:], 1.0, [o8b])
                gb = P.buf("gbias")
                dgb = P.dsem("gbias")
                P.dma("sp", bi_sb[:], b_i.rearrange("d h -> h d"), dgb, writes=[gb])
                P.dma("sp", bf_sb[:], b_f.rearrange("d h -> h d"), dgb, writes=[gb])
                P.dma("sp", m0_sb[:], st_m.rearrange("d h -> h d"), dgb, writes=[gb])
                P.ts("dve", nbf_sb[:], bf_sb[:], -1.0, None, ALU.mult, None, [gb], [gb])
                P.ts("dve", nm0_sb[:], m0_sb[:], -1.0, None, ALU.mult, None, [gb], [gb])
                gr = Ring(P, p3b, "graw", 2, [8, 2, 1024], F32, dsem=True)
                tmpr = {nm: Ring(P, p3b, "g_" + nm, 2, [8, 1024], F32) for nm in
                        ("ig", "lf", "B", "m", "A", "w", "t")}
                smr = Ring(P, p3b, "gsm", 2, [8, 64], F32)
                dec_st = sb(p3b, "dec_st", [8, 2, 24]); dec_stb = P.buf()
                mo_d = [P.dsem("mout0"), P.dsem("mout1")]
                gch = 0
                for si, (tok0, T, is_s) in enumerate(SEQS):
                    nch = T // 64
                    wq = {}
                    for dr in range(2):
                        rv = (lambda a: a[:, ::-1]) if dr == 1 else (lambda a: a)
                        raw, rb, rd = gr.next()
                        P.dma("sp", raw[:, 0, 0:T], GIF[dr, :, tok0:tok0 + T], rd, writes=[rb])
                        P.dma("sp", raw[:, 1, 0:T], GIF[2 + dr, :, tok0:tok0 + T], rd, writes=[rb])
                        ig, igb, _ = tmpr["ig"].next(); lf, lfb, _ = tmpr["lf"].next()
                        Bt, Bb, _ = tmpr["B"].next(); mt, mb, _ = tmpr["m"].next()
                        At, Ab, _ = tmpr["A"].next(); wt_, wtb, _ = tmpr["w"].next(); tt_, ttb, _ = tmpr["t"].next()
                        sm, smb, _ = smr.next()
                        P.ts("dve", ig[:, 0:T], raw[:, 0, 0:T], bi_sb[:, dr:dr + 1], None, ALU.add, None, [rb, gb], [igb])
                        P.act(lf[:, 0:T], raw[:, 1, 0:T], AF.Exp, [rb, gb], [lfb], scale=-1.0, bias=nbf_sb[:, dr:dr + 1])
                        P.act(lf[:, 0:T], lf[:, 0:T], AF.Ln, [lfb], [lfb], bias=1.0)
                        P.ts("dve", lf[:, 0:T], lf[:, 0:T], -1.0, None, ALU.mult, None, [lfb], [lfb])
                        P.op("dve", lambda e, o=rv(Bt[:, 0:T]), a=rv(ones8[:, 0:T]), b_=rv(lf[:, 0:T]):
                             e.tensor_tensor_scan(out=o, data0=a, data1=b_, initial=0.0, op0=ALU.mult, op1=ALU.add),
                             [lfb, o8b], [Bb])
                        init = m0_sb[:, dr:dr + 1] if is_s else 0.0
                        P.op("dve", lambda e, o=rv(mt[:, 0:T]), a=rv(lf[:, 0:T]), b_=rv(ig[:, 0:T]), init=init:
                             e.tensor_tensor_scan(out=o, data0=a, data1=b_, initial=init, op0=ALU.add, op1=ALU.max),
                             [lfb, igb, gb], [mb])
                        last = 63 if dr == 0 else 0
                        Bv = Bt[:, 0:T].rearrange("p (c s) -> p c s", s=64)[:, :, last]
                        mv_ = mt[:, 0:T].rearrange("p (c s) -> p c s", s=64)[:, :, last]
                        Z = sm[:, 0:nch]; Zp = sm[:, 16:16 + nch]; dc = sm[:, 32:32 + nch]
                        P.tt("dve", Z, Bv, mv_, ALU.subtract, [Bb, mb], [smb])
                        if dr == 0:
                            if nch > 1:
                                P.cp("dve", sm[:, 17:16 + nch], sm[:, 0:nch - 1], [smb], [smb])
                            first = sm[:, 16:17]
                        else:
                            if nch > 1:
                                P.cp("dve", sm[:, 16:16 + nch - 1], sm[:, 1:nch], [smb], [smb])
                            first = sm[:, 16 + nch - 1:16 + nch]
                        if is_s:
                            P.cp("dve", first, nm0_sb[:, dr:dr + 1], [gb, smb], [smb])
                        else:
                            P.memset("dve", first, 0.0, [smb])
                        P.tt("dve", dc, Z, Zp, ALU.subtract, [smb], [smb])
                        P.act(dec_st[:, dr, gch:gch + nch], dc, AF.Exp, [smb], [dec_stb])
                        P.tt("dve", At[:, 0:T], ig[:, 0:T], Bt[:, 0:T], ALU.subtract, [igb, Bb], [Ab])
                        Zbc = bc_last(Z, 64)
                        P.tt("dve", wt_[:, 0:T].rearrange("p (c s) -> p c s", s=64),
                             At[:, 0:T].rearrange("p (c s) -> p c s", s=64), Zbc, ALU.add, [Ab, smb], [wtb])
                        P.act(wt_[:, 0:T], wt_[:, 0:T], AF.Exp, [wtb], [wtb])
                        P.tt("dve", tt_[:, 0:T].rearrange("p (c s) -> p c s", s=64), Zbc,
                             Bt[:, 0:T].rearrange("p (c s) -> p c s", s=64), ALU.subtract, [Bb, smb], [ttb])
                        P.act(tt_[:, 0:T], tt_[:, 0:T], AF.Exp, [ttb], [ttb])
                        wq[dr] = (wt_, wtb, tt_, ttb)
                        if not is_s:
                            fin = mt[:, T - 1:T] if dr == 0 else mt[:, 0:1]
                            P.dma("sp", o_m[si, dr:dr + 1, :].rearrange("d h -> h d"), fin, mo_d[dr], reads=[mb])
                    for c in range(nch):
                        ps, pb = P.psum()
                        for dr in range(2):
                            wt_, wtb, tt_, ttb = wq[dr]
                            P.mm(ps[0:64, dr * 8:dr * 8 + 8], wt_[:, c * 64:(c + 1) * 64], ident[0:8, 0:8], True, True,
                                 [wtb, cb], [pb])
                            P.mm(ps[0:64, 16 + dr * 8:24 + dr * 8], tt_[:, c * 64:(c + 1) * 64], ident[0:8, 0:8], True, True,
                                 [ttb, cb], [pb])
                        P.cp("dve", WT[:, gch + c, :], ps[0:64, 0:32], [pb], [WT_b])
                    gch += nch
                ddec = P.dsem("decd")
                P.dma("sp", DECD.rearrange("o (d h c) -> h (o d) c", d=2, h=8), dec_st[:], ddec,
                      reads=[dec_stb], writes=[decd_b])
                ddec2 = P.dsem("decd2")
                P.dma("sp", DEC[:], DECD.partition_broadcast(128)[:, 0, :], ddec2,
                      reads=[decd_b], writes=[DEC_b])
                P.barrier()
                P.emit()

            def scan_gen(kind, sc):
                gla = kind == "gla"
                LQ = "sp" if gla else "act"
                SQ = "pool"
                NH = 4 if gla else 8
                NJ = 4 if gla else 2
                DK = 128 * NJ
                DV = 1024 if gla else 512
                NV = DV // 128
                qsrc, ksrc, vsrc = (QT, KT, VT) if gla else (MQT, MKT, MVT)
                PART, OUT = (PARTG, OG) if gla else (PARTM, OM)
                if True:
                    qT = sb(sc, "u_qT", [128, NJ, 1024], BF16); kT = sb(sc, "u_kT", [128, NJ, 1024], BF16)
                    vT = sb(sc, "u_vT", [128, NV, 1024], BF16)
                    u_b = P.buf("unit"); u_d = P.dsem("unit")
                    ktr = Ring(P, sc, "ktok", 4, [64, DK], BF16); vtr = Ring(P, sc, "vtok", 4, [64, DV], BF16)
                    S = [sb(sc, f"S{d_}", [128, NJ, DV]) for d_ in range(2)]
                    Sb = [sb(sc, f"Sb{d_}", [128, NJ, DV], BF16) for d_ in range(2)]
                    S_b = [P.buf(f"S{d_}") for d_ in range(2)]
                    Sb_b = [P.buf(f"Sb{d_}") for d_ in range(2)]
                    S_d = [P.dsem(f"S{d_}") for d_ in range(2)]
                    if not gla:
                        nst = [sb(sc, f"nst{d_}", [128, 2]) for d_ in range(2)]
                        nbt = [sb(sc, f"nbt{d_}", [128, 2], BF16) for d_ in range(2)]
                    so_d = [P.dsem("stout0"), P.dsem("stout1")]
                    osb = Ring(P, sc, "osb", 2, [64, DV], F32, dsem=True)
                    prt = Ring(P, sc, "prt", 2, [64, DV], F32, dsem=True)
                    if gla:
                        w2u = sb(sc, "w2u", [17, 2, 512]); w2_b = P.buf("w2u"); w2_d = P.dsem("w2u")
                        lar = Ring(P, sc, "la", 2, [64, 512], F32)
                        epr = Ring(P, sc, "ep", 4, [128, 256], F32)
                        enr = Ring(P, sc, "en", 2, [128, 256], F32)
                        err = Ring(P, sc, "er", 2, [64, 512], F32)
                    if gla:
                        qdr = Ring(P, sc, "qd", 4, [128, NJ, 64], BF16)
                        kir = Ring(P, sc, "ki", 4, [128, NJ, 64], BF16)
                    ker = Ring(P, sc, "ke", 4, [64, DK], BF16)
                    atr = Ring(P, sc, "at", 4, [64, 64], BF16)
                    smr2 = Ring(P, sc, "dn", 2, [64, 4], F32)
                    part_bufs = {}
                    gch0 = 0
                    for si, (tok0, T, is_s) in enumerate(SEQS):
                        nch = T // 64
                        for hd in range(NH):
                            P.dma(LQ, qT[:, :, 0:T], qsrc[hd * DK:(hd + 1) * DK, tok0:tok0 + T].rearrange(
                                "(j p) t -> p j t", p=128), u_d, writes=[u_b])
                            P.dma(LQ, kT[:, :, 0:T], ksrc[hd * DK:(hd + 1) * DK, tok0:tok0 + T].rearrange(
                                "(j p) t -> p j t", p=128), u_d, writes=[u_b])
                            P.dma(LQ, vT[:, :, 0:T], vsrc[hd * DV:(hd + 1) * DV, tok0:tok0 + T].rearrange(
                                "(j p) t -> p j t", p=128), u_d, writes=[u_b])
                            if gla:
                                for d_ in range(2):
                                    P.dma(LQ, w2u[0:16, d_, :], w_a2[d_, :, hd * 512:(hd + 1) * 512], w2_d, writes=[w2_b])
                                    P.dma(LQ, w2u[16:17, d_, :], b_a[d_:d_ + 1, hd * 512:(hd + 1) * 512], w2_d, writes=[w2_b])
                            for dr in range(2):
                                if is_s:
                                    src = (st_S if gla else st_C)[dr, hd, :, :].rearrange("(j p) e -> p j e", p=128)
                                    P.dma(LQ, S[dr][:], src, S_d[dr], writes=[S_b[dr]])
                                    if not gla:
                                        P.dma(LQ, nst[dr][:], st_n[dr, hd, :].rearrange("(j p) -> p j", p=128), S_d[dr],
                                              writes=[S_b[dr]])
                                else:
                                    P.memset("pool", S[dr][:], 0.0, [S_b[dr]])
                                    if not gla:
                                        P.memset("pool", nst[dr][:], 0.0, [S_b[dr]])
                                if gla:
                                    P.cp("pool", Sb[dr][:], S[dr][:], [S_b[dr]], [Sb_b[dr]])
                            def stageA(it, dr):
                                c = it if dr == 0 else nch - 1 - it
                                second = it >= nch // 2
                                t0 = c * 64
                                g0 = tok0 + t0
                                mask = tri["U_incl" if dr == 0 else "L_incl"][0]
                                ktok, ktb, _ = ktr.next(); vtok, vtb, _ = vtr.next()
                                ps, pb = P.psum()
                                for j in range(NJ):
                                    P.mm(ps[0:64, j * 128:(j + 1) * 128], kT[:, j, t0:t0 + 64], identb[:], True, True,
                                         [u_b, cb], [pb])
                                P.cp("dve", ktok[:], ps[0:64, 0:DK], [pb], [ktb])
                                for h2 in range(DV // 512):
                                    ps, pb = P.psum()
                                    for j in range(4):
                                        P.mm(ps[0:64, j * 128:(j + 1) * 128], vT[:, h2 * 4 + j, t0:t0 + 64], identb[:],
                                             True, True, [u_b, cb], [pb])
                                    P.act(vtok[:, h2 * 512:(h2 + 1) * 512], ps[0:64, 0:512], AF.Copy, [pb], [vtb])
                                if gla:
                                    ps, pb = P.psum()
                                    P.mm(ps[0:64, 0:512], gaA[dr][:, g0:g0 + 64], w2u[:, dr, :],
                                         True, True, [ga_b, w2_b], [pb])
                                    la, lab, _ = lar.next()
                                    P.act(la[:], ps[0:64, 0:512], AF.Exp, [pb], [lab], scale=-1.0)
                                    P.act(la[:], la[:], AF.Ln, [lab], [lab], bias=1.0)
                                    tric = tri["U_incl" if dr == 0 else "L_incl"][1]
                                    tris = tri["L_strict" if dr == 0 else "U_strict"][1]
                                    psc, pcb = P.psum()
                                    for j in range(4):
                                        P.mm(psc[:, j * 64:(j + 1) * 64], la[:, j * 128:(j + 1) * 128], tric[:], True, True,
                                             [lab, cb], [pcb])
                                    psr, prb = P.psum()
                                    P.mm(psr[0:64, 0:512], tris[:], la[:], True, True, [lab, cb], [prb])
                                    ep, epb, _ = epr.next(); en, enb, _ = enr.next(); er, erb, _ = err.next()
                                    P.act(ep[:], psc[:, 0:256], AF.Exp, [pcb], [epb])
                                    P.act(en[:], psc[:, 0:256], AF.Exp, [pcb], [enb], scale=-1.0)
                                    P.act(er[:], psr[0:64, 0:512], AF.Exp, [prb], [erb])
                                    qd, qdb, _ = qdr.next(); ki, kib, _ = kir.next(); ke, keb, _ = ker.next()
                                    P.tt("dve", qd[:], qT[:, :, t0:t0 + 64], ep[:].rearrange("p (j t) -> p j t", t=64),
                                         ALU.mult, [u_b, epb], [qdb])
                                    P.tt("dve", ki[:], kT[:, :, t0:t0 + 64], en[:].rearrange("p (j t) -> p j t", t=64),
                                         ALU.mult, [u_b, enb], [kib])
                                    P.tt("dve", ke[:], ktok[:], er[:], ALU.mult, [ktb, erb], [keb])
                                    q_ap = [qd[:, j, :] for j in range(NJ)]; q_bufs = [qdb]
                                    k_ap = [ki[:, j, :] for j in range(NJ)]; k_bufs = [kib]
                                    last = 63 if dr == 0 else 0
                                    rowsc = [ep[:, j * 64 + last:j * 64 + last + 1] for j in range(NJ)]
                                    rowsc_bufs = [epb]
                                    wcol = None
                                else:
                                    gc = gch0 + c
                                    wcol = WT[:, gc, dr * 8 + hd:dr * 8 + hd + 1]
                                    tcol = WT[:, gc, 16 + dr * 8 + hd:16 + dr * 8 + hd + 1]
                                    di = (dr * 8 + hd) * 24 + gc
                                    dcol = DEC[:, di:di + 1]
                                    ke, keb, _ = ker.next()
                                    P.ts("dve", ke[:], ktok[:], wcol, None, ALU.mult, None, [ktb, WT_b], [keb])
                                    q_ap = [qT[:, j, t0:t0 + 64] for j in range(NJ)]; q_bufs = [u_b]
                                    k_ap = [kT[:, j, t0:t0 + 64] for j in range(NJ)]; k_bufs = [u_b]
                                    rowsc = [dcol] * NJ
                                    rowsc_bufs = [DEC_b]
                                psa, pab = P.psum()
                                for j in range(NJ):
                                    P.mm(psa[0:64, 0:64], k_ap[j], q_ap[j], j == 0, j == NJ - 1, k_bufs + q_bufs, [pab])
                                at, atb, _ = atr.next()
                                if gla:
                                    P.tt("dve", at[:], psa[0:64, 0:64], mask[:], ALU.mult, [pab, cb], [atb])
                                else:
                                    P.stt(at[:], psa[0:64, 0:64], wcol, mask[:], ALU.mult, ALU.mult, [pab, cb, WT_b], [atb])
                                return dict(it=it, dr=dr, c=c, second=second, t0=t0, g0=g0, ktok=ktok, ktb=ktb, vtok=vtok, vtb=vtb,
                                            q_ap=q_ap, q_bufs=q_bufs, rowsc=rowsc, rowsc_bufs=rowsc_bufs, ke=ke, keb=keb, at=at, atb=atb,
                                            dcol=(None if gla else dcol), tcol=(None if gla else tcol))

                            def stageB(x):
                                it, dr, c, second, t0, g0 = x["it"], x["dr"], x["c"], x["second"], x["t0"], x["g0"]
                                ktok, ktb, vtok, vtb = x["ktok"], x["ktb"], x["vtok"], x["vtb"]
                                q_ap, q_bufs, rowsc, rowsc_bufs = x["q_ap"], x["q_bufs"], x["rowsc"], x["rowsc_bufs"]
                                ke, keb, at, atb, dcol, tcol = x["ke"], x["keb"], x["at"], x["atb"], x["dcol"], x["tcol"]
                                if not gla:
                                    P.act(Sb[dr][:], S[dr][:], AF.Copy, [S_b[dr], DEC_b], [Sb_b[dr]], scale=dcol)
                                    P.act(nbt[dr][:], nst[dr][:], AF.Copy, [S_b[dr], DEC_b], [Sb_b[dr]], scale=dcol)
                                ot, otb, otd = osb.next()
                                if second:
                                    pt, ptb, ptd = prt.next()
                                    pbuf = part_bufs[(si, hd, c)]
                                    P.dma(LQ, pt[:], PART[g0:g0 + 64, hd * DV:(hd + 1) * DV], ptd, reads=[pbuf], writes=[ptb])
                                if not gla:
                                    psd, pdb = P.psum()
                                    P.mm(psd[0:64, 0:1], at[:], onesb[0:64, :], True, False, [atb, cb], [pdb])
                                    for j in range(NJ):
                                        P.mm(psd[0:64, 0:1], q_ap[j], nbt[dr][:, j:j + 1], False, j == NJ - 1,
                                             q_bufs + [Sb_b[dr]], [pdb])
                                    dn, dnb, _ = smr2.next()
                                    P.act(dn[:, 0:1], psd[0:64, 0:1], AF.Abs, [pdb], [dnb])
                                    P.ts("dve", dn[:, 1:2], dn[:, 0:1], tcol, None, ALU.max, None, [dnb, WT_b], [dnb])
                                    P.op("dve", lambda e, o=dn[:, 2:3], i_=dn[:, 1:2]: e.reciprocal(out=o, in_=i_), [dnb], [dnb])
                                for h2 in range(DV // 512):
                                    pso, pob = P.psum()
                                    P.mm(pso[0:64, 0:512], at[:], vtok[:, h2 * 512:(h2 + 1) * 512], True, False,
                                         [atb, vtb], [pob])
                                    for j in range(NJ):
                                        P.mm(pso[0:64, 0:512], q_ap[j], Sb[dr][:, j, h2 * 512:(h2 + 1) * 512], False, j == NJ - 1,
                                             q_bufs + [Sb_b[dr]], [pob])
                                    osl = ot[:, h2 * 512:(h2 + 1) * 512]
                                    if gla:
                                        if second:
                                            P.tt("dve", osl, pso[0:64, 0:512], pt[:, h2 * 512:(h2 + 1) * 512], ALU.add, [pob, ptb], [otb])
                                        else:
                                            P.act(osl, pso[0:64, 0:512], AF.Copy, [pob], [otb])
                                    else:
                                        if second:
                                            P.stt(osl, pso[0:64, 0:512], dn[:, 2:3], pt[:, h2 * 512:(h2 + 1) * 512], ALU.mult, ALU.add,
                                                  [pob, ptb, dnb], [otb])
                                        else:
                                            P.ts("dve", osl, pso[0:64, 0:512], dn[:, 2:3], None, ALU.mult, None, [pob, dnb], [otb])
                                if second:
                                    if (not gla) and is_s:
                                        dst3 = OUT[tok0:tok0 + T, hd * DV:(hd + 1) * DV].rearrange("(r c) e -> c r e", c=64)
                                        for cl in range(4):
                                            P.dma(SQ, dst3[4 * c + cl, :, :], ot[cl * 16:(cl + 1) * 16, :], otd, reads=[otb])
                                    else:
                                        P.dma(SQ, OUT[g0:g0 + 64, hd * DV:(hd + 1) * DV], ot[:], otd, reads=[otb])
                                else:
                                    pbuf = P.buf()
                                    part_bufs[(si, hd, c)] = pbuf
                                    P.dma(SQ, PART[g0:g0 + 64, hd * DV:(hd + 1) * DV], ot[:], otd, reads=[otb], writes=[pbuf])
                                for j in range(NJ):
                                    for h2 in range(DV // 512):
                                        pss, psb_ = P.psum()
                                        P.mm(pss[:, 0:512], ke[:, j * 128:(j + 1) * 128], vtok[:, h2 * 512:(h2 + 1) * 512], True, True,
                                             [keb, vtb], [psb_])
                                        ssl = S[dr][:, j, h2 * 512:(h2 + 1) * 512]
                                        P.stt(ssl, ssl, rowsc[j], pss[:, 0:512], ALU.mult, ALU.add,
                                              [psb_, S_b[dr], Sb_b[dr]] + rowsc_bufs, [S_b[dr]])
                                    if gla:
                                        P.act(Sb[dr][:, j, :], S[dr][:, j, :], AF.Copy, [S_b[dr]], [Sb_b[dr]])
                                if not gla:
                                    psn, pnb = P.psum()
                                    for j in range(NJ):
                                        P.mm(psn[:, j:j + 1], ke[:, j * 128:(j + 1) * 128], onesb[0:64, :], True, True,
                                             [keb, cb], [pnb])
                                    P.stt(nst[dr][:], nst[dr][:], dcol, psn[:, 0:2], ALU.mult, ALU.add,
                                          [pnb, S_b[dr], Sb_b[dr], DEC_b], [S_b[dr]])

                            pend = {}
                            for it in range(nch + 1):
                                if it < nch:
                                    for dr in range(2):
                                        pend[(it, dr)] = stageA(it, dr)
                                if it >= 1:
                                    for dr in range(2):
                                        stageB(pend.pop((it - 1, dr)))
                                yield
                            if not is_s:
                                for dr in range(2):
                                    dstS = (o_S if gla else o_C)[si, dr, hd, :, :].rearrange("(j p) e -> p j e", p=128)
                                    P.dma(SQ, dstS, S[dr][:], so_d[dr], reads=[S_b[dr]])
                                    if not gla:
                                        P.dma(SQ, o_n[si, dr, hd, :].rearrange("(j p) -> p j", p=128), nst[dr][:], so_d[dr],
                                              reads=[S_b[dr]])
                        gch0 += nch

            with ExitStack() as sc:
                gens = [scan_gen("gla", sc)]
                pattern = [0]
                if stop_after > 3:
                    gens.append(scan_gen("mlstm", sc))
                    pattern = [0, 1, 1]
                alive = set(range(len(gens)))
                while alive:
                    for gi in pattern:
                        if gi in alive:
                            try:
                                next(gens[gi])
                            except StopIteration:
                                alive.discard(gi)
                P.barrier()
                P.emit()
                print("engine sem counts", {k: e.count for k, e in P.E.items()})
        if stop_after <= 4:
            return nc, dbg

        for branch in ("gla", "mlstm"):
            with ExitStack() as ph:
                src = OG if branch == "gla" else OM
                dstT = UT if branch == "gla" else UMT
                nwT = gnwT if branch == "gla" else mnwT
                g1 = sb(ph, "g1", [128, 32, 512], BF16); g1b = P.buf(); g1d = P.dsem("g1")
                if branch == "mlstm":
                    g2 = sb(ph, "g2", [128, 32, 512], BF16); g2b = P.buf(); g2d = P.dsem("g2")
                ust = sb(ph, "ust", [128, 32, 512], BF16); ustb = P.buf(); ustd = P.dsem("ust")
                oin = Ring(P, ph, "oin", 2, [128, D], F32, dsem=True)
                tmp = sb(ph, "p41tmp", [128, D], F32); tmpb = P.buf()
                sts = Ring(P, ph, "p41s", 2, [128, 40], F32)
                for tb in range(3):
                    P.dma("sp", g1[:], (SZT if branch == "gla" else SZM)[:, tb * 512:(tb + 1) * 512].rearrange(
                        "(k p) t -> p k t", p=128), g1d, writes=[g1b])
                    if branch == "mlstm":
                        P.dma("sp", g2[:], SGO[:, tb * 512:(tb + 1) * 512].rearrange("(k p) t -> p k t", p=128), g2d,
                              writes=[g2b])
                        for k4 in range(4):
                            P.tt("pool", g1[:, k4 * 8:(k4 + 1) * 8, :], g1[:, k4 * 8:(k4 + 1) * 8, :],
                                 g2[:, k4 * 8:(k4 + 1) * 8, :], ALU.mult, [g1b, g2b], [g1b])
                    for ti in range(4):
                        i = tb * 4 + ti
                        ot, ob, od = oin.next()
                        P.dma("sp", ot[:], src[i * 128:(i + 1) * 128, :], od, writes=[ob])
                        s_, s_b, _ = sts.next()
                        NHh = 4 if branch == "gla" else 8
                        E_ = D // NHh
                        o3 = ot[:].rearrange("p (h e) -> p h e", e=E_)
                        sq = tmp[:].rearrange("p (h e) -> p h e", e=E_)
                        if branch == "mlstm":
                            P.op("dve", lambda e, o=s_[:, 0:8], i_=o3: e.tensor_reduce(out=o, in_=i_, op=ALU.add, axis=AX.X),
                                 [ob], [s_b])
                            P.ts("dve", s_[:, 8:16], s_[:, 0:8], 1.0 / E_, None, ALU.mult, None, [s_b], [s_b])
                            P.tt("dve", o3, o3, bc_last(s_[:, 8:16], E_), ALU.subtract, [ob, s_b], [ob])
                        P.tt("pool", sq, o3, o3, ALU.mult, [ob], [tmpb])
                        P.op("dve", lambda e, o=s_[:, 16:16 + NHh], i_=sq: e.tensor_reduce(out=o, in_=i_, op=ALU.add, axis=AX.X),
                             [tmpb], [s_b])
                        P.ts("dve", s_[:, 24:24 + NHh], s_[:, 16:16 + NHh], 1.0 / E_, EPS, ALU.mult, ALU.add, [s_b], [s_b])
                        P.act(s_[:, 24:24 + NHh], s_[:, 24:24 + NHh], AF.Sqrt, [s_b], [s_b])
                        P.op("dve", lambda e, o=s_[:, 32:32 + NHh], i_=s_[:, 24:24 + NHh]: e.reciprocal(out=o, in_=i_), [s_b], [s_b])
                        P.tt("dve", o3, o3, bc_last(s_[:, 32:32 + NHh], E_), ALU.mult, [ob, s_b], [ob])
                        for g in range(8):
                            ps, pb = P.psum()
                            for kk in range(4):
                                k = g * 4 + kk
                                P.tr(ps[:, kk * 128:(kk + 1) * 128], ot[:, k * 128:(k + 1) * 128], ident[:], [ob, cb], [pb])
                            for kk in range(4):
                                k = g * 4 + kk
                                P.stt(ust[:, k, ti * 128:(ti + 1) * 128], ps[:, kk * 128:(kk + 1) * 128], nwT[:, k:k + 1],
                                      g1[:, k, ti * 128:(ti + 1) * 128], ALU.mult, ALU.mult, [pb, vec_b, g1b], [ustb])
                    P.dma("sp", dstT[:, tb * 512:(tb + 1) * 512].rearrange("(k p) t -> p k t", p=128), ust[:], ustd,
                          reads=[ustb])
                P.barrier()
                P.emit()
        if stop_after <= 5:
            return nc, dbg

        tmp32r = [None]

        def gemm_stage(name, W, actsrc, evac_fn, dst, dst_dt):
            with ExitStack() as ph:
                aT = sb(ph, "aT", [128, 32, NTOK], BF16); aT_b = P.buf(); aT_d = P.dsem("aT" + name)
                for tb in range(3):
                    P.dma("sp", aT[:, :, tb * 512:(tb + 1) * 512], actsrc[:, tb * 512:(tb + 1) * 512].rearrange(
                        "(k p) t -> p k t", p=128), aT_d, writes=[aT_b])
                wst = Ring(P, ph, "gwst", 3, [128, 32, 128], F32, dsem=True)
                wbf = Ring(P, ph, "gwbf", 2, [128, 32, 128], BF16)
                stg = Ring(P, ph, "gstg", 2, [128, NTOK], dst_dt, dsem=True)
                aux = Ring(P, ph, "gaux", 2, [128, NTOK], F32, dsem=True) if name == "B" else None
                auxb = Ring(P, ph, "gauxb", 2, [128, NTOK], BF16, dsem=True) if name in ("A", "B") else None
                tmp32r[0] = Ring(P, ph, "gtmp", 2, [128, 512], F32)

                def issue(bi):
                    wt, wb_, wd = wst.next()
                    P.dma("sp", wt[:], W[:, bi * 128:(bi + 1) * 128].rearrange("(k p) n -> p k n", p=128), wd, writes=[wb_])
                    return wt, wb_
                loaded = {0: issue(0), 1: issue(1)}
                casted = {}

                def do_cast(bi):
                    wt, wb_ = loaded.pop(bi)
                    wbt, wbb, _ = wbf.next()
                    P.act(wbt[:, 0:16, :], wt[:, 0:16, :], AF.Copy, [wb_], [wbb])
                    P.act(wbt[:, 16:32, :], wt[:, 16:32, :], AF.Copy, [wb_], [wbb])
                    casted[bi] = (wbt, wbb)
                do_cast(0)
                for bi in range(32):
                    if bi + 2 < 32:
                        loaded[bi + 2] = issue(bi + 2)
                    if bi + 1 < 32:
                        do_cast(bi + 1)
                    wbt, wbb = casted.pop(bi)
                    st_t, st_b, st_d = stg.next()
                    ctx = evac_fn("pre", bi, aux, auxb)
                    for tb in range(3):
                        ps, pb = P.psum()
                        for k in range(32):
                            P.mm(ps[:, 0:512], wbt[:, k, :], aT[:, k, tb * 512:(tb + 1) * 512], k == 0, k == 31, [wbb, aT_b], [pb])
                        evac_fn("evac", bi, aux, auxb, ctx=ctx, ps=ps, pb=pb, tb=tb, st_t=st_t, st_b=st_b)
                    P.dma("sp", dst[bi * 128:(bi + 1) * 128, :], st_t[:], st_d, reads=[st_b])
                P.barrier()
                P.emit()

        def evac_A(mode, bi, aux, auxb, ctx=None, ps=None, pb=None, tb=None, st_t=None, st_b=None):
            if mode == "pre":
                gt, gb_, gd = auxb.next()
                P.dma("sp", gt[:], G01[bi * 128:(bi + 1) * 128, :], gd, writes=[gb_])
                return (gt, gb_)
            gt, gb_ = ctx
            P.tt("dve", st_t[:, tb * 512:(tb + 1) * 512], ps[:, 0:512], gt[:, tb * 512:(tb + 1) * 512], ALU.mult,
                 [pb, gb_], [st_b])

        def evac_B(mode, bi, aux, auxb, ctx=None, ps=None, pb=None, tb=None, st_t=None, st_b=None):
            if mode == "pre":
                gt, gb_, gd = auxb.next()
                P.dma("sp", gt[:], G01[4096 + bi * 128:4096 + (bi + 1) * 128, :], gd, writes=[gb_])
                t1t, t1b, t1d = aux.next()
                P.dma("sp", t1t[:], T1[bi * 128:(bi + 1) * 128, :], t1d, writes=[t1b])
                return (gt, gb_, t1t, t1b)
            gt, gb_, t1t, t1b = ctx
            sl = slice(tb * 512, (tb + 1) * 512)
            tm, tmb, _ = tmp32r[0].next()
            P.tt("dve", tm[:], ps[:, 0:512], gt[:, sl], ALU.mult, [pb, gb_], [tmb])
            P.tt("pool", st_t[:, sl], tm[:], t1t[:, sl], ALU.add, [tmb, t1b], [st_b])

        def evac_C(mode, bi, aux, auxb, ctx=None, ps=None, pb=None, tb=None, st_t=None, st_b=None):
            if mode == "pre":
                return None
            j = 0 if tb == 0 else 1
            P.ts("dve", st_t[:, tb * 512:(tb + 1) * 512], ps[:, 0:512], gateT[:, bi, j:j + 1], None, ALU.mult, None,
                 [pb, modT_b], [st_b])

        gemm_stage("A", w_gp, UT, evac_A, T1, F32)
        gemm_stage("B", w_mp, UMT, evac_B, MIXT, BF16)
        gemm_stage("C", w_o, MIXT, evac_C, OUTT, F32)
        if stop_after <= 6:
            return nc, dbg

        with ExitStack() as ph:
            fnw = sb(ph, "fnw", [128, D]); fnw_b = P.buf(); fd = P.dsem("fnw")
            P.dma("sp", fnw[:], f_nw.partition_broadcast(128)[:, 0, :], fd, writes=[fnw_b])
            xin = Ring(P, ph, "x5", 2, [128, D], F32, dsem=True)
            oin = Ring(P, ph, "o5", 2, [128, 32, 128], F32, dsem=True)
            yst = Ring(P, ph, "y5", 2, [128, D], F32, dsem=True)
            junk = sb(ph, "junk5", [128, D], BF16); junk_b = P.buf()
            sts = Ring(P, ph, "s5", 2, [128, 4], F32)
            for i in range(12):
                xt, xb, xd = xin.next()
                P.dma("sp", xt[:], x_all[i * 128:(i + 1) * 128, :], xd, writes=[xb])
                ot, ob, od = oin.next()
                P.dma("sp", ot[:], OUTT[:, i * 128:(i + 1) * 128].rearrange("(k p) t -> p k t", p=128), od, writes=[ob])
                for g in range(8):
                    ps, pb = P.psum()
                    for kk in range(4):
                        k = g * 4 + kk
                        P.tr(ps[:, kk * 128:(kk + 1) * 128], ot[:, k, :], ident[:], [ob, cb], [pb])
                    P.tt("dve", xt[:, g * 512:(g + 1) * 512], ps[:, 0:512], xt[:, g * 512:(g + 1) * 512], ALU.add,
                         [pb, xb], [xb])
                s_, s_b, _ = sts.next()
                P.act(junk[:], xt[:], AF.Square, [xb], [junk_b, s_b], accum_out=s_[:, 0:1])
                P.ts("dve", s_[:, 1:2], s_[:, 0:1], 1.0 / D, EPS, ALU.mult, ALU.add, [s_b], [s_b])
                P.act(s_[:, 2:3], s_[:, 1:2], AF.Sqrt, [s_b], [s_b])
                P.op("dve", lambda e, o=s_[:, 3:4], i_=s_[:, 2:3]: e.reciprocal(out=o, in_=i_), [s_b], [s_b])
                yt, yb, yd = yst.next()
                P.stt(yt[:], xt[:], s_[:, 3:4], fnw[:], ALU.mult, ALU.mult, [xb, s_b, fnw_b], [yb])
                P.dma("sp", y_all[i * 128:(i + 1) * 128, :], yt[:], yd, reads=[yb])
            P.barrier()
            P.emit()
    return nc, dbg


_CACHE = {}


def _get_nc():
    if "nc" not in _CACHE:
        _CACHE["nc"] = build_program()[0]
    return _CACHE["nc"]


def make_in_maps(inputs):
    f = lambda a: np.ascontiguousarray(np.asarray(a, dtype=np.float32))
    x_prompt, x_sample, c = f(inputs["x_prompt"]), f(inputs["x_sample"]), f(inputs["c"])
    shared = {
        "w_ada": f(inputs["w_ada"][0]), "b_ada": f(inputs["b_ada"]), "w_in": f(inputs["w_in"][0]),
        "w_a2": f(inputs["gla_w_a2"][0]), "b_a": f(inputs["gla_b_a"][0]), "b_i": f(inputs["mlstm_b_i"][0]),
        "b_f": f(inputs["mlstm_b_f"][0]), "b_mg": f(inputs["b_merge"]), "g_nw": f(inputs["gla_norm_w"]),
        "m_nw": f(inputs["mlstm_norm_w"]), "w_gp": f(inputs["w_gla_proj"][0]), "w_mp": f(inputs["w_mlstm_proj"][0]),
        "w_o": f(inputs["w_out"][0]), "f_nw": f(inputs["final_norm_w"]).reshape(1, D),
    }
    c_ctx = f(inputs["c_ctx"])
    sS, sC, sn, sm = (f(inputs[k]) for k in ("state_gla_S", "state_mlstm_C", "state_mlstm_n", "state_mlstm_m"))
    maps = []
    for core in range(8):
        m = dict(shared)
        m["x_all"] = np.concatenate([x_prompt[2 * core], x_prompt[2 * core + 1], x_sample[core]], axis=0)
        m["cvec"] = np.stack([c_ctx, c[core]], axis=0)
        m["st_S"] = sS[core, 0]
        m["st_C"] = sC[core, 0]
        m["st_n"] = sn[core, 0]
        m["st_m"] = sm[core, 0]
        maps.append(m)
    return maps


def kernel(**inputs):
    nc = _get_nc()
    maps = make_in_maps(inputs)
    res = run_bass_kernel_spmd(nc, maps, core_ids=list(range(8))).results
    y_prompt = np.stack([res[i // 2]["y_all"][(i % 2) * 256:(i % 2) * 256 + 256] for i in range(16)], axis=0)
    y_sample = np.stack([res[i]["y_all"][512:1536] for i in range(8)], axis=0)
    new_S = np.concatenate([res[i]["o_S"] for i in range(8)], axis=0)[:, None]
    new_C = np.concatenate([res[i]["o_C"] for i in range(8)], axis=0)[:, None]
    new_n = np.concatenate([res[i]["o_n"] for i in range(8)], axis=0)[:, None]
    new_m = np.concatenate([res[i]["o_m"] for i in range(8)], axis=0)[:, None]
    return (y_prompt.astype(np.float32), y_sample.astype(np.float32), new_S.astype(np.float32),
            new_C.astype(np.float32), new_n.astype(np.float32), new_m.astype(np.float32))
```

```python
import numpy as np
from contextlib import ExitStack
import concourse.bass as bass
import concourse.mybir as mybir
from concourse.bass_utils import run_bass_kernel_spmd

F32 = mybir.dt.float32
BF16 = mybir.dt.bfloat16
AF = mybir.ActivationFunctionType
ALU = mybir.AluOpType
AX = mybir.AxisListType

D = 4096
NTOK = 1536
TP = 256
TS = 1024
N_IN = 36928
EPS = 1e-6
SEQS = [(0, 256, False), (256, 256, False), (512, 1024, True)]

C_GQ, C_GK, C_GV, C_GZ, C_GA = 0, 2048, 4096, 8192, 12288
C_MQ, C_MK, C_MV, C_MZ, C_MO, C_MI, C_MF, C_MG = 12320, 14368, 16416, 20512, 24608, 28704, 28720, 28736


class Buf:
    __slots__ = ("name", "w", "r")

    def __init__(self, name=""):
        self.name = name
        self.w = None
        self.r = {}


class DSem:
    def __init__(self, sem, name):
        self.sem = sem
        self.count = 0
        self.name = name


class Eng:
    def __init__(self, name, eng, sem, is_pe=False):
        self.name = name
        self.eng = eng
        self.sem = sem
        self.count = 0
        self.seen = {}
        self.ops = []
        self.is_pe = is_pe


class Prog:
    def __init__(self, nc, es):
        self.nc = nc
        self.es = es
        self.E = {}
        for name, eng in (("pe", nc.tensor), ("act", nc.scalar), ("dve", nc.vector),
                          ("pool", nc.gpsimd), ("sp", nc.sync)):
            sem = es.enter_context(nc.semaphore("s_" + name))
            self.E[name] = Eng(name, eng, sem, is_pe=(name == "pe"))
        self.dsems = []
        self.psum_banks = []
        self.psum_i = 0
        self.nbuf = 0

    def buf(self, name=""):
        self.nbuf += 1
        return Buf(name or f"b{self.nbuf}")

    def dsem(self, name):
        self.nbuf += 1
        sem = self.es.enter_context(self.nc.semaphore(f"d_{name}_{self.nbuf}"))
        d = DSem(sem, name)
        self.dsems.append(d)
        return d

    def init_psum(self):
        for i in range(8):
            t = self.es.enter_context(self.nc.psum_tensor(f"psb{i}", [128, 512], F32))
            self.psum_banks.append((t, self.buf(f"psb{i}")))

    def psum(self):
        t, b = self.psum_banks[self.psum_i % 8]
        self.psum_i += 1
        return t, b

    def _deps(self, reads, writes):
        deps = []
        for b in reads:
            if b.w is not None:
                deps.append(b.w)
        for b in writes:
            if b.w is not None:
                deps.append(b.w)
            deps.extend(b.r.values())
        return deps

    def _reduce(self, E, deps, skip_sem=None):
        need = {}
        for sem, val in deps:
            if sem is skip_sem:
                continue
            if E.is_pe and sem is E.sem:
                continue
            k = id(sem)
            if E.seen.get(k, 0) >= val:
                continue
            if k not in need or need[k][1] < val:
                need[k] = (sem, val)
        for k, (sem, val) in need.items():
            E.seen[k] = val
        return list(need.values())

    def _mark(self, tok, reads, writes):
        sem, val = tok
        k = id(sem)
        for b in reads:
            if k not in b.r or b.r[k][1] < val:
                b.r[k] = tok
        for b in writes:
            b.w = tok
            b.r = {}

    def op(self, eng, fn, reads=(), writes=(), signal=True):
        E = self.E[eng]
        if not E.is_pe:
            signal = True
        deps = self._deps(reads, writes)
        waits = self._reduce(E, deps)
        tok = (E.sem, E.count + 1)
        if signal:
            E.count += 1
        E.ops.append(("op", fn, waits, signal))
        self._mark(tok, reads, writes)
        return tok

    def dma(self, q, out, in_, ds, reads=(), writes=()):
        E = self.E[q]
        deps = []
        for b in reads:
            if b.w is not None:
                deps.append(b.w)
        for b in writes:
            if b.w is not None and b.w[0] is not ds.sem:
                deps.append(b.w)
            deps.extend(b.r.values())
        for sem, val in deps:
            assert not (sem is ds.sem and val >= ds.count + 16), "self-dependency on DMA semaphore"
        waits = self._reduce(E, deps)
        ds.count += 16
        tok = (ds.sem, ds.count)
        E.ops.append(("dma", (out, in_), waits, ds.sem))
        self._mark(tok, reads, writes)
        return tok

    def barrier(self):
        toks = [(e.sem, e.count) for e in self.E.values() if e.count > 0]
        toks += [(d.sem, d.count) for d in self.dsems if d.count > 0]
        for E in self.E.values():
            waits = self._reduce(E, [t for t in toks if not (t[0] is E.sem)])
            if waits:
                E.ops.append(("wait", None, waits, False))

    def emit(self):
        nc = self.nc
        for e in self.E.values():
            assert e.count < 65000, (e.name, e.count)
        for d in self.dsems:
            assert d.count < 65000, (d.name, d.count)

        def run(E, eng):
            for kind, payload, waits, sig in E.ops:
                for sem, val in waits:
                    eng.wait_ge(sem, val)
                if kind == "op":
                    ins = payload(eng)
                    if sig:
                        ins.then_inc(E.sem, 1)
                elif kind == "dma":
                    out, in_ = payload
                    eng.dma_start(out=out, in_=in_).then_inc(sig, 16)
            E.ops = []

        with nc.Block() as block:
            @block.tensor
            def _(eng):
                run(self.E["pe"], eng)

            @block.scalar
            def _(eng):
                run(self.E["act"], eng)

            @block.vector
            def _(eng):
                run(self.E["dve"], eng)

            @block.gpsimd
            def _(eng):
                run(self.E["pool"], eng)

            @block.sync
            def _(eng):
                run(self.E["sp"], eng)

    def mm(self, out, lhsT, rhs, start, stop, reads, writes):
        self.op("pe", lambda e: e.matmul(out, lhsT, rhs, start=start, stop=stop),
                reads, writes, signal=stop)

    def tr(self, out, in_, ident, reads, writes):
        self.op("pe", lambda e: e.transpose(out, in_, ident), reads, writes, signal=True)

    def act(self, out, in_, func, reads, writes, **kw):
        self.op("act", lambda e: e.activation(out=out, in_=in_, func=func, **kw), reads, writes)

    def ts(self, eng, out, in0, s1, s2, op0, op1, reads, writes):
        if s2 is None:
            self.op(eng, lambda e: e.tensor_scalar(out=out, in0=in0, scalar1=s1, scalar2=None, op0=op0),
                    reads, writes)
        else:
            self.op(eng, lambda e: e.tensor_scalar(out=out, in0=in0, scalar1=s1, scalar2=s2, op0=op0, op1=op1),
                    reads, writes)

    def tt(self, eng, out, in0, in1, op, reads, writes):
        self.op(eng, lambda e: e.tensor_tensor(out=out, in0=in0, in1=in1, op=op), reads, writes)

    def stt(self, out, in0, scalar, in1, op0, op1, reads, writes):
        self.op("dve", lambda e: e.scalar_tensor_tensor(out=out, in0=in0, scalar=scalar, in1=in1,
                                                        op0=op0, op1=op1), reads, writes)

    def cp(self, eng, out, in_, reads, writes):
        self.op(eng, lambda e: e.tensor_copy(out=out, in_=in_), reads, writes)

    def memset(self, eng, out, val, writes):
        self.op(eng, lambda e: e.memset(out, val), (), writes)


class Ring:
    def __init__(self, P, es, name, n, shape, dtype, dsem=False):
        self.slots = []
        for i in range(n):
            P.nbuf += 1
            t = es.enter_context(P.nc.sbuf_tensor(f"{name}{i}_{P.nbuf}", shape, dtype))
            self.slots.append((t, P.buf(f"{name}{i}"), P.dsem(f"{name}{i}") if dsem else None))
        self.i = 0

    def next(self):
        s = self.slots[self.i % len(self.slots)]
        self.i += 1
        return s


def bc_last(ap, n):
    return bass.AP(ap.tensor, ap.offset, [list(x) for x in ap.ap] + [[0, n]])


def build_program(debug=False, stop_after=99):
    nc = bass.Bass("TRN2", target_bir_lowering=False)

    def din(name, shape):
        return nc.dram_tensor(name, shape, F32, kind="ExternalInput").ap()

    def dout(name, shape):
        return nc.dram_tensor(name, shape, F32, kind="ExternalOutput").ap()

    dbg = {}

    def dscr(name, shape, dt=F32):
        if debug and name in debug:
            t = nc.dram_tensor(name, shape, dt, kind="ExternalOutput").ap()
            dbg[name] = t
            return t
        return nc.dram_tensor(name, shape, dt, kind="Internal").ap()

    x_all = din("x_all", [NTOK, D])
    cvec = din("cvec", [2, D])
    st_S = din("st_S", [2, 4, 512, 1024])
    st_C = din("st_C", [2, 8, 256, 512])
    st_n = din("st_n", [2, 8, 256])
    st_m = din("st_m", [2, 8])
    w_ada = din("w_ada", [D, 3 * D])
    b_ada = din("b_ada", [1, 3 * D])
    w_in = din("w_in", [D, N_IN])
    w_a2 = din("w_a2", [2, 16, 2048])
    b_a = din("b_a", [2, 2048])
    b_i = din("b_i", [2, 8])
    b_f = din("b_f", [2, 8])
    b_mg = din("b_mg", [1, 2 * D])
    g_nw = din("g_nw", [1, D])
    m_nw = din("m_nw", [1, D])
    w_gp = din("w_gp", [D, D])
    w_mp = din("w_mp", [D, D])
    w_o = din("w_o", [D, D])
    f_nw = din("f_nw", [1, D])
    y_all = dout("y_all", [NTOK, D])
    o_S = dout("o_S", [2, 2, 4, 512, 1024])
    o_C = dout("o_C", [2, 2, 8, 256, 512])
    o_n = dout("o_n", [2, 2, 8, 256])
    o_m = dout("o_m", [2, 2, 8])
    MODD = dscr("MODD", [2, 3 * D])
    QT = dscr("QT", [2048, NTOK], BF16)
    KT = dscr("KT", [2048, NTOK], BF16)
    VT = dscr("VT", [4096, NTOK], BF16)
    SZT = dscr("SZT", [4096, NTOK], BF16)
    GAT = dscr("GAT", [2, 16, NTOK])
    MQT = dscr("MQT", [2048, NTOK], BF16)
    MKT = dscr("MKT", [2048, NTOK], BF16)
    MVT = dscr("MVT", [4096, NTOK], BF16)
    SZM = dscr("SZM", [4096, NTOK], BF16)
    SGO = dscr("SGO", [4096, NTOK], BF16)
    GIF = dscr("GIF", [4, 8, NTOK])
    G01 = dscr("G01", [8192, NTOK], BF16)
    DECD = dscr("DECD", [1, 2 * 8 * 24])
    PARTG = dscr("PARTG", [NTOK, 4096])
    OG = dscr("OG", [NTOK, 4096])
    PARTM = dscr("PARTM", [NTOK, 4096])
    OM = dscr("OM", [NTOK, 4096])
    UT = dscr("UT", [4096, NTOK], BF16)
    UMT = dscr("UMT", [4096, NTOK], BF16)
    T1 = dscr("T1", [4096, NTOK])
    MIXT = dscr("MIXT", [4096, NTOK], BF16)
    OUTT = dscr("OUTT", [4096, NTOK])

    es = ExitStack()
    with es:
        P = Prog(nc, es)
        P.init_psum()
        es.enter_context(nc.allow_non_contiguous_dma(reason="small strided loads"))

        uniq = [0]

        def sb(stack, name, shape, dt=F32):
            uniq[0] += 1
            return stack.enter_context(nc.sbuf_tensor(f"{name}_{uniq[0]}", shape, dt))

        cb = P.buf("consts")
        ident = sb(es, "ident", [128, 128])
        identb = sb(es, "identb", [128, 128], BF16)
        tri = {}
        P.memset("pool", ident[:], 0.0, [cb])
        P.op("pool", lambda e: e.affine_select(out=ident[:], in_=ident[:], compare_op=ALU.not_equal, fill=1.0,
                                               base=0, pattern=[[-1, 128]], channel_multiplier=1), [cb], [cb])
        P.cp("pool", identb[:], ident[:], [cb], [cb])
        specs = {"U_incl": (-1, 1, ALU.is_ge), "L_incl": (1, -1, ALU.is_ge),
                 "L_strict": (1, -1, ALU.is_gt), "U_strict": (-1, 1, ALU.is_gt)}
        for nm, (cm, st, cmp_) in specs.items():
            t1 = sb(es, "m1_" + nm, [64, 64])
            t2 = sb(es, "m2_" + nm, [64, 64])
            P.memset("pool", t1[:], 1.0, [cb])

            def _sel(e, t1=t1, cm=cm, st=st, cmp_=cmp_):
                return e.affine_select(out=t1[:], in_=t1[:], compare_op=cmp_, fill=0.0, base=0,
                                       pattern=[[st, 64]], channel_multiplier=cm)
            P.op("pool", _sel, [cb], [cb])
            P.ts("pool", t2[:], t1[:], -1.0 / 16.0, None, ALU.mult, None, [cb], [cb])
            tri[nm] = (t1, t2)
        onesb = sb(es, "onesb", [128, 1], BF16)
        P.memset("pool", onesb[:], 1.0, [cb])
        scale1T = sb(es, "scale1T", [128, 32, 2])
        shiftT = sb(es, "shiftT", [128, 32, 2])
        gateT = sb(es, "gateT", [128, 32, 2])
        modT_b = P.buf("modT")
        gnwT = sb(es, "gnwT", [128, 32])
        mnwT = sb(es, "mnwT", [128, 32])
        bmgT = sb(es, "bmgT", [128, 64])
        vec_b = P.buf("vecs")
        dv_ = P.dsem("vecs")
        P.dma("sp", gnwT[:], g_nw.rearrange("o (k p) -> p (o k)", p=128), dv_, writes=[vec_b])
        P.dma("sp", mnwT[:], m_nw.rearrange("o (k p) -> p (o k)", p=128), dv_, writes=[vec_b])
        P.dma("sp", bmgT[:], b_mg.rearrange("o (k p) -> p (o k)", p=128), dv_, writes=[vec_b])

        with ExitStack() as ph:
            c_sb = sb(ph, "c_sb", [2, D]); c_b = P.buf()
            d0 = P.dsem("p0c")
            P.dma("sp", c_sb[:], cvec[:, :], d0, writes=[c_b])
            P.act(c_sb[:], c_sb[:], AF.Silu, [c_b], [c_b])
            scT = sb(ph, "scT", [128, 32, 2]); scT_b = P.buf()
            ps, pb = P.psum()
            for k in range(32):
                P.tr(ps[:, 2 * k:2 * k + 2], c_sb[:, k * 128:(k + 1) * 128], ident[0:2, 0:2], [c_b, cb], [pb])
            P.cp("dve", scT[:].rearrange("p k j -> p (k j)"), ps[:, 0:64], [pb], [scT_b])
            mod_sb = sb(ph, "mod_sb", [2, 3 * D]); mod_b = P.buf()
            bada = sb(ph, "bada", [2, 3 * D]); bada_b = P.buf()
            d1 = P.dsem("p0b")
            P.dma("sp", bada[:], b_ada.partition_broadcast(2)[:, 0, :], d1, writes=[bada_b])
            wr = Ring(P, ph, "wada", 2, [128, 32, 256], F32, dsem=True)
            for nb in range(48):
                wt, wb_, wd = wr.next()
                P.dma("sp", wt[:], w_ada[:, nb * 256:(nb + 1) * 256].rearrange("(k p) n -> p k n", p=128), wd,
                      writes=[wb_])
                ps, pb = P.psum()
                for k in range(32):
                    P.mm(ps[0:2, 0:256], scT[:, k, :], wt[:, k, :], k == 0, k == 31, [scT_b, wb_], [pb])
                P.tt("dve", mod_sb[:, nb * 256:(nb + 1) * 256], ps[0:2, 0:256], bada[:, nb * 256:(nb + 1) * 256],
                     ALU.add, [pb, bada_b], [mod_b])
            P.ts("dve", mod_sb[:, D:2 * D], mod_sb[:, D:2 * D], 1.0, None, ALU.add, None, [mod_b], [mod_b])
            dm = P.dsem("p0m")
            modd_b = P.buf("MODD")
            P.dma("sp", MODD[:, :], mod_sb[:], dm, reads=[mod_b], writes=[modd_b])
            for part, dst in ((0, shiftT), (1, scale1T), (2, gateT)):
                ps, pb = P.psum()
                for k in range(32):
                    P.tr(ps[:, 2 * k:2 * k + 2], mod_sb[:, part * D + k * 128: part * D + (k + 1) * 128],
                         ident[0:2, 0:2], [mod_b, cb], [pb])
                P.cp("dve", dst[:].rearrange("p k j -> p (k j)"), ps[:, 0:64], [pb], [modT_b])
            P.barrier()
            P.emit()
        if stop_after <= 0:
            return nc, dbg

        with ExitStack() as ph:
            hT = sb(ph, "hT", [128, 32, NTOK], BF16); hT_b = P.buf("hT")
            with ExitStack() as p1:
                xr = Ring(P, p1, "xin", 2, [128, D], F32, dsem=True)
                junk = sb(p1, "junk", [128, D], BF16); junk_b = P.buf()
                st_r = Ring(P, p1, "stat", 2, [128, 4], F32)
                for i in range(12):
                    j = 0 if i < 4 else 1
                    xt, xb, xd = xr.next()
                    P.dma("sp", xt[:], x_all[i * 128:(i + 1) * 128, :], xd, writes=[xb])
                    stt_, sbf, _ = st_r.next()
                    P.act(junk[:], xt[:], AF.Square, [xb], [junk_b, sbf], accum_out=stt_[:, 0:1])
                    P.ts("dve", stt_[:, 1:2], stt_[:, 0:1], 1.0 / D, EPS, ALU.mult, ALU.add, [sbf], [sbf])
                    P.act(stt_[:, 2:3], stt_[:, 1:2], AF.Sqrt, [sbf], [sbf])
                    P.op("dve", lambda e, o=stt_[:, 3:4], i_=stt_[:, 2:3]: e.reciprocal(out=o, in_=i_), [sbf], [sbf])
                    P.ts("dve", xt[:], xt[:], stt_[:, 3:4], None, ALU.mult, None, [xb, sbf], [xb])
                    for g in range(8):
                        ps, pb = P.psum()
                        for kk in range(4):
                            k = g * 4 + kk
                            P.tr(ps[:, kk * 128:(kk + 1) * 128], xt[:, k * 128:(k + 1) * 128], ident[:], [xb, cb], [pb])
                        for kk in range(4):
                            k = g * 4 + kk
                            if kk % 2 == 0:
                                P.ts("dve", hT[:, k, i * 128:(i + 1) * 128], ps[:, kk * 128:(kk + 1) * 128],
                                     scale1T[:, k, j:j + 1], shiftT[:, k, j:j + 1], ALU.mult, ALU.add,
                                     [pb, modT_b], [hT_b])
                            else:
                                P.act(hT[:, k, i * 128:(i + 1) * 128], ps[:, kk * 128:(kk + 1) * 128], AF.Identity,
                                      [pb, modT_b], [hT_b], scale=scale1T[:, k, j:j + 1], bias=shiftT[:, k, j:j + 1])
                P.barrier()
                P.emit()

            with ExitStack() as p2:
                wst = Ring(P, p2, "wst", 3, [128, 32, 128], F32, dsem=True)
                wbf = Ring(P, p2, "wbf", 2, [128, 32, 128], BF16)
                stg = Ring(P, p2, "stg", 2, [128, NTOK], BF16, dsem=True)
                stg32 = Ring(P, p2, "stg32", 1, [16, NTOK], F32, dsem=True)
                scr_b = P.buf("scratch_p2")

                def perm_views(out_t, ps, tb, nrows):
                    r0 = (tb - 1) * 8
                    o = out_t[0:nrows, 512:1536].rearrange("p (c r) -> p c r", r=16)[:, :, r0:r0 + 8]
                    i_ = ps[0:nrows, 0:512].rearrange("p (r c) -> p c r", c=64)
                    return o, i_

                def evac(kind, out_t, out_b, ps, pb, tb, nrows, bias_ap=None, scale=1.0, perm=False):
                    if perm and tb > 0:
                        o, i_ = perm_views(out_t, ps, tb, nrows)
                    else:
                        o, i_ = out_t[0:nrows, tb * 512:(tb + 1) * 512], ps[0:nrows, 0:512]
                    if kind == "copy":
                        if scale != 1.0:
                            P.ts("dve", o, i_, scale, None, ALU.mult, None, [pb], [out_b])
                        else:
                            P.cp("dve", o, i_, [pb], [out_b])
                    elif kind == "silu":
                        P.act(o, i_, AF.Silu, [pb], [out_b])
                    elif kind == "sigmoid":
                        if bias_ap is not None:
                            P.act(o, i_, AF.Sigmoid, [pb, vec_b], [out_b], bias=bias_ap)
                        else:
                            P.act(o, i_, AF.Sigmoid, [pb], [out_b])

                sections = [
                    (C_GQ, 16, QT, "copy", 512 ** -0.5, False),
                    (C_GK, 16, KT, "copy", 1.0, False),
                    (C_GV, 32, VT, "copy", 1.0, False),
                    (C_GZ, 32, SZT, "silu", 1.0, False),
                    (C_MQ, 16, MQT, "copy", 256 ** -0.5, True),
                    (C_MK, 16, MKT, "copy", 1.0, True),
                    (C_MV, 32, MVT, "copy", 1.0, True),
                    (C_MZ, 32, SZM, "silu", 1.0, False),
                    (C_MO, 32, SGO, "sigmoid", 1.0, False),
                    (C_MG, 64, G01, "sigmoidb", 1.0, False),
                ]
                blocks = []
                for (c0, nblk, dst, kind, scale, perm) in sections:
                    for b in range(nblk):
                        blocks.append(("reg", c0 + b * 128, dst, b, kind, scale, perm))
                blocks.append(("ga", C_GA, None, 0, None, 1.0, False))
                blocks.append(("gif", C_MI, None, 0, None, 1.0, True))
                nblk_total = len(blocks)

                def issue_load(bi):
                    typ, c0 = blocks[bi][0], blocks[bi][1]
                    wt, wb_, wd = wst.next()
                    ncol = 128 if typ == "reg" else 32
                    P.dma("sp", wt[:, :, 0:ncol], w_in[:, c0:c0 + ncol].rearrange("(k p) n -> p k n", p=128), wd,
                          writes=[wb_])
                    return (wt, wb_)

                loaded = {}
                loaded[0] = issue_load(0)
                loaded[1] = issue_load(1)
                casted = {}

                def do_cast(bi):
                    wt, wb_ = loaded.pop(bi)
                    wbt, wbb, _ = wbf.next()
                    ncol = 128 if blocks[bi][0] == "reg" else 32
                    P.cp("dve", wbt[:, 0:16, 0:ncol], wt[:, 0:16, 0:ncol], [wb_], [wbb])
                    P.cp("pool", wbt[:, 16:32, 0:ncol], wt[:, 16:32, 0:ncol], [wb_], [wbb])
                    casted[bi] = (wbt, wbb)
                do_cast(0)
                for bi, (typ, c0, dst, b, kind, scale, perm) in enumerate(blocks):
                    if bi + 2 < nblk_total:
                        loaded[bi + 2] = issue_load(bi + 2)
                    if bi + 1 < nblk_total:
                        do_cast(bi + 1)
                    wbt, wbb = casted.pop(bi)
                    if typ == "reg":
                        st_t, st_b, st_d = stg.next()
                        for tb in range(3):
                            ps, pb = P.psum()
                            for k in range(32):
                                P.mm(ps[:, 0:512], wbt[:, k, :], hT[:, k, tb * 512:(tb + 1) * 512], k == 0, k == 31,
                                     [wbb, hT_b], [pb])
                            if kind == "sigmoidb":
                                evac("sigmoid", st_t, st_b, ps, pb, tb, 128, bias_ap=bmgT[:, b:b + 1])
                            else:
                                evac(kind, st_t, st_b, ps, pb, tb, 128, scale=scale, perm=perm)
                        P.dma("sp", dst[b * 128:(b + 1) * 128, :], st_t[:], st_d, reads=[st_b], writes=[scr_b])
                    elif typ == "ga":
                        for dr in range(2):
                            st_t, st_b, st_d = stg32.next()
                            for tb in range(3):
                                ps, pb = P.psum()
                                for k in range(32):
                                    P.mm(ps[0:16, 0:512], wbt[:, k, dr * 16:(dr + 1) * 16],
                                         hT[:, k, tb * 512:(tb + 1) * 512], k == 0, k == 31, [wbb, hT_b], [pb])
                                evac("copy", st_t, st_b, ps, pb, tb, 16)
                            P.dma("sp", GAT[dr, :, :], st_t[0:16, :], st_d, reads=[st_b], writes=[scr_b])
                    else:
                        for q in range(4):
                            st_t, st_b, st_d = stg32.next()
                            for tb in range(3):
                                ps, pb = P.psum()
                                for k in range(32):
                                    P.mm(ps[0:8, 0:512], wbt[:, k, q * 8:(q + 1) * 8],
                                         hT[:, k, tb * 512:(tb + 1) * 512], k == 0, k == 31, [wbb, hT_b], [pb])
                                evac("copy", st_t, st_b, ps, pb, tb, 8, perm=True)
                            P.dma("sp", GIF[q, :, :], st_t[0:8, :], st_d, reads=[st_b], writes=[scr_b])
                P.barrier()
                P.emit()
        if stop_after <= 2:
            return nc, dbg

        with ExitStack() as ph:
            gaA = [sb(ph, f"gaA{d_}", [17, NTOK]) for d_ in range(2)]
            ga_b = P.buf("gaA")
            dga = P.dsem("gaA")
            for d_ in range(2):
                P.memset("pool", gaA[d_][:], 1.0, [ga_b])
            for d_ in range(2):
                P.dma("sp", gaA[d_][0:16, :], GAT[d_, :, :], dga, writes=[ga_b])

            WT = sb(ph, "WT", [64, 24, 32]); WT_b = P.buf("WT")
            DEC = sb(ph, "DEC", [128, 384]); DEC_b = P.buf("DEC")
            decd_b = P.buf("DECD")
            with ExitStack() as p3b:
                bi_sb = sb(p3b, "bi_sb", [8, 2]); bf_sb = sb(p3b, "bf_sb", [8, 2]); nbf_sb = sb(p3b, "nbf_sb", [8, 2])
                m0_sb = sb(p3b, "m0_sb", [8, 2]); nm0_sb = sb(p3b, "nm0_sb", [8, 2])
                ones8 = sb(p3b, "ones8", [8, 1024]); o8b = P.buf("ones8")
                P.memset("pool", ones8[:], 1.0, [o8b])
                gb = P.buf("gbias")
                dgb = P.dsem("gbias")
                P.dma("sp", bi_sb[:], b_i.rearrange("d h -> h d"), dgb, writes=[gb])
                P.dma("sp", bf_sb[:], b_f.rearrange("d h -> h d"), dgb, writes=[gb])
                P.dma("sp", m0_sb[:], st_m.rearrange("d h -> h d"), dgb, writes=[gb])
                P.ts("dve", nbf_sb[:], bf_sb[:], -1.0, None, ALU.mult, None, [gb], [gb])
                P.ts("dve", nm0_sb[:], m0_sb[:], -1.0, None, ALU.mult, None, [gb], [gb])
                gr = Ring(P, p3b, "graw", 2, [8, 2, 1024], F32, dsem=True)
                tmpr = {nm: Ring(P, p3b, "g_" + nm, 2, [8, 1024], F32) for nm in
                        ("ig", "lf", "B", "m", "A", "w", "t")}
                smr = Ring(P, p3b, "gsm", 2, [8, 64], F32)
                dec_st = sb(p3b, "dec_st", [8, 2, 24]); dec_stb = P.buf()
                mo_d = [P.dsem("mout0"), P.dsem("mout1")]
                gch = 0
                for si, (tok0, T, is_s) in enumerate(SEQS):
                    nch = T // 64
                    wq = {}
                    for dr in range(2):
                        rv = (lambda a: a[:, ::-1]) if dr == 1 else (lambda a: a)
                        raw, rb, rd = gr.next()
                        P.dma("sp", raw[:, 0, 0:T], GIF[dr, :, tok0:tok0 + T], rd, writes=[rb])
                        P.dma("sp", raw[:, 1, 0:T], GIF[2 + dr, :, tok0:tok0 + T], rd, writes=[rb])
                        ig, igb, _ = tmpr["ig"].next(); lf, lfb, _ = tmpr["lf"].next()
                        Bt, Bb, _ = tmpr["B"].next(); mt, mb, _ = tmpr["m"].next()
                        At, Ab, _ = tmpr["A"].next(); wt_, wtb, _ = tmpr["w"].next(); tt_, ttb, _ = tmpr["t"].next()
                        sm, smb, _ = smr.next()
                        P.ts("dve", ig[:, 0:T], raw[:, 0, 0:T], bi_sb[:, dr:dr + 1], None, ALU.add, None, [rb, gb], [igb])
                        P.act(lf[:, 0:T], raw[:, 1, 0:T], AF.Exp, [rb, gb], [lfb], scale=-1.0, bias=nbf_sb[:, dr:dr + 1])
                        P.act(lf[:, 0:T], lf[:, 0:T], AF.Ln, [lfb], [lfb], bias=1.0)
                        P.ts("dve", lf[:, 0:T], lf[:, 0:T], -1.0, None, ALU.mult, None, [lfb], [lfb])
                        P.op("dve", lambda e, o=rv(Bt[:, 0:T]), a=rv(ones8[:, 0:T]), b_=rv(lf[:, 0:T]):
                             e.tensor_tensor_scan(out=o, data0=a, data1=b_, initial=0.0, op0=ALU.mult, op1=ALU.add),
                             [lfb, o8b], [Bb])
                        init = m0_sb[:, dr:dr + 1] if is_s else 0.0
                        P.op("dve", lambda e, o=rv(mt[:, 0:T]), a=rv(lf[:, 0:T]), b_=rv(ig[:, 0:T]), init=init:
                             e.tensor_tensor_scan(out=o, data0=a, data1=b_, initial=init, op0=ALU.add, op1=ALU.max),
                             [lfb, igb, gb], [mb])
                        last = 63 if dr == 0 else 0
                        Bv = Bt[:, 0:T].rearrange("p (c s) -> p c s", s=64)[:, :, last]
                        mv_ = mt[:, 0:T].rearrange("p (c s) -> p c s", s=64)[:, :, last]
                        Z = sm[:, 0:nch]; Zp = sm[:, 16:16 + nch]; dc = sm[:, 32:32 + nch]
                        P.tt("dve", Z, Bv, mv_, ALU.subtract, [Bb, mb], [smb])
                        if dr == 0:
                            if nch > 1:
                                P.cp("dve", sm[:, 17:16 + nch], sm[:, 0:nch - 1], [smb], [smb])
                            first = sm[:, 16:17]
                        else:
                            if nch > 1:
                                P.cp("dve", sm[:, 16:16 + nch - 1], sm[:, 1:nch], [smb], [smb])
                            first = sm[:, 16 + nch - 1:16 + nch]
                        if is_s:
                            P.cp("dve", first, nm0_sb[:, dr:dr + 1], [gb, smb], [smb])
                        else:
                            P.memset("dve", first, 0.0, [smb])
                        P.tt("dve", dc, Z, Zp, ALU.subtract, [smb], [smb])
                        P.act(dec_st[:, dr, gch:gch + nch], dc, AF.Exp, [smb], [dec_stb])
                        P.tt("dve", At[:, 0:T], ig[:, 0:T], Bt[:, 0:T], ALU.subtract, [igb, Bb], [Ab])
                        Zbc = bc_last(Z, 64)
                        P.tt("dve", wt_[:, 0:T].rearrange("p (c s) -> p c s", s=64),
                             At[:, 0:T].rearrange("p (c s) -> p c s", s=64), Zbc, ALU.add, [Ab, smb], [wtb])
                        P.act(wt_[:, 0:T], wt_[:, 0:T], AF.Exp, [wtb], [wtb])
                        P.tt("dve", tt_[:, 0:T].rearrange("p (c s) -> p c s", s=64), Zbc,
                             Bt[:, 0:T].rearrange("p (c s) -> p c s", s=64), ALU.subtract, [Bb, smb], [ttb])
                        P.act(tt_[:, 0:T], tt_[:, 0:T], AF.Exp, [ttb], [ttb])
                        wq[dr] = (wt_, wtb, tt_, ttb)
                        if not is_s:
                            fin = mt[:, T - 1:T] if dr == 0 else mt[:, 0:1]
                            P.dma("sp", o_m[si, dr:dr + 1, :].rearrange("d h -> h d"), fin, mo_d[dr], reads=[mb])
                    for c in range(nch):
                        ps, pb = P.psum()
                        for dr in range(2):
                            wt_, wtb, tt_, ttb = wq[dr]
                            P.mm(ps[0:64, dr * 8:dr * 8 + 8], wt_[:, c * 64:(c + 1) * 64], ident[0:8, 0:8], True, True,
                                 [wtb, cb], [pb])
                            P.mm(ps[0:64, 16 + dr * 8:24 + dr * 8], tt_[:, c * 64:(c + 1) * 64], ident[0:8, 0:8], True, True,
                                 [ttb, cb], [pb])
                        P.cp("dve", WT[:, gch + c, :], ps[0:64, 0:32], [pb], [WT_b])
                    gch += nch
                ddec = P.dsem("decd")
                P.dma("sp", DECD.rearrange("o (d h c) -> h (o d) c", d=2, h=8), dec_st[:], ddec,
                      reads=[dec_stb], writes=[decd_b])
                ddec2 = P.dsem("decd2")
                P.dma("sp", DEC[:], DECD.partition_broadcast(128)[:, 0, :], ddec2,
                      reads=[decd_b], writes=[DEC_b])
                P.barrier()
                P.emit()

            def scan_gen(kind, sc):
                gla = kind == "gla"
                LQ = "sp" if gla else "act"
                SQ = "pool"
                NH = 4 if gla else 8
                NJ = 4 if gla else 2
                DK = 128 * NJ
                DV = 1024 if gla else 512
                NV = DV // 128
                qsrc, ksrc, vsrc = (QT, KT, VT) if gla else (MQT, MKT, MVT)
                PART, OUT = (PARTG, OG) if gla else (PARTM, OM)
                if True:
                    qT = sb(sc, "u_qT", [128, NJ, 1024], BF16); kT = sb(sc, "u_kT", [128, NJ, 1024], BF16)
                    vT = sb(sc, "u_vT", [128, NV, 1024], BF16)
                    u_b = P.buf("unit"); u_d = P.dsem("unit")
                    ktr = Ring(P, sc, "ktok", 4, [64, DK], BF16); vtr = Ring(P, sc, "vtok", 4, [64, DV], BF16)
                    S = [sb(sc, f"S{d_}", [128, NJ, DV]) for d_ in range(2)]
                    Sb = [sb(sc, f"Sb{d_}", [128, NJ, DV], BF16) for d_ in range(2)]
                    S_b = [P.buf(f"S{d_}") for d_ in range(2)]
                    Sb_b = [P.buf(f"Sb{d_}") for d_ in range(2)]
                    S_d = [P.dsem(f"S{d_}") for d_ in range(2)]
                    if not gla:
                        nst = [sb(sc, f"nst{d_}", [128, 2]) for d_ in range(2)]
                        nbt = [sb(sc, f"nbt{d_}", [128, 2], BF16) for d_ in range(2)]
                    so_d = [P.dsem("stout0"), P.dsem("stout1")]
                    osb = Ring(P, sc, "osb", 2, [64, DV], F32, dsem=True)
                    prt = Ring(P, sc, "prt", 2, [64, DV], F32, dsem=True)
                    if gla:
                        w2u = sb(sc, "w2u", [17, 2, 512]); w2_b = P.buf("w2u"); w2_d = P.dsem("w2u")
                        lar = Ring(P, sc, "la", 2, [64, 512], F32)
                        epr = Ring(P, sc, "ep", 4, [128, 256], F32)
                        enr = Ring(P, sc, "en", 2, [128, 256], F32)
                        err = Ring(P, sc, "er", 2, [64, 512], F32)
                    if gla:
                        qdr = Ring(P, sc, "qd", 4, [128, NJ, 64], BF16)
                        kir = Ring(P, sc, "ki", 4, [128, NJ, 64], BF16)
                    ker = Ring(P, sc, "ke", 4, [64, DK], BF16)
                    atr = Ring(P, sc, "at", 4, [64, 64], BF16)
                    smr2 = Ring(P, sc, "dn", 2, [64, 4], F32)
                    part_bufs = {}
                    gch0 = 0
                    for si, (tok0, T, is_s) in enumerate(SEQS):
                        nch = T // 64
                        for hd in range(NH):
                            P.dma(LQ, qT[:, :, 0:T], qsrc[hd * DK:(hd + 1) * DK, tok0:tok0 + T].rearrange(
                                "(j p) t -> p j t", p=128), u_d, writes=[u_b])
                            P.dma(LQ, kT[:, :, 0:T], ksrc[hd * DK:(hd + 1) * DK, tok0:tok0 + T].rearrange(
                                "(j p) t -> p j t", p=128), u_d, writes=[u_b])
                            P.dma(LQ, vT[:, :, 0:T], vsrc[hd * DV:(hd + 1) * DV, tok0:tok0 + T].rearrange(
                                "(j p) t -> p j t", p=128), u_d, writes=[u_b])
                            if gla:
                                for d_ in range(2):
                                    P.dma(LQ, w2u[0:16, d_, :], w_a2[d_, :, hd * 512:(hd + 1) * 512], w2_d, writes=[w2_b])
                                    P.dma(LQ, w2u[16:17, d_, :], b_a[d_:d_ + 1, hd * 512:(hd + 1) * 512], w2_d, writes=[w2_b])
                            for dr in range(2):
                                if is_s:
                                    src = (st_S if gla else st_C)[dr, hd, :, :].rearrange("(j p) e -> p j e", p=128)
                                    P.dma(LQ, S[dr][:], src, S_d[dr], writes=[S_b[dr]])
                                    if not gla:
                                        P.dma(LQ, nst[dr][:], st_n[dr, hd, :].rearrange("(j p) -> p j", p=128), S_d[dr],
                                              writes=[S_b[dr]])
                                else:
                                    P.memset("pool", S[dr][:], 0.0, [S_b[dr]])
                                    if not gla:
                                        P.memset("pool", nst[dr][:], 0.0, [S_b[dr]])
                                if gla:
                                    P.cp("pool", Sb[dr][:], S[dr][:], [S_b[dr]], [Sb_b[dr]])
                            def stageA(it, dr):
                                c = it if dr == 0 else nch - 1 - it
                                second = it >= nch // 2
                                t0 = c * 64
                                g0 = tok0 + t0
                                mask = tri["U_incl" if dr == 0 else "L_incl"][0]
                                ktok, ktb, _ = ktr.next(); vtok, vtb, _ = vtr.next()
                                ps, pb = P.psum()
                                for j in range(NJ):
                                    P.mm(ps[0:64, j * 128:(j + 1) * 128], kT[:, j, t0:t0 + 64], identb[:], True, True,
                                         [u_b, cb], [pb])
                                P.cp("dve", ktok[:], ps[0:64, 0:DK], [pb], [ktb])
                                for h2 in range(DV // 512):
                                    ps, pb = P.psum()
                                    for j in range(4):
                                        P.mm(ps[0:64, j * 128:(j + 1) * 128], vT[:, h2 * 4 + j, t0:t0 + 64], identb[:],
                                             True, True, [u_b, cb], [pb])
                                    P.act(vtok[:, h2 * 512:(h2 + 1) * 512], ps[0:64, 0:512], AF.Copy, [pb], [vtb])
                                if gla:
                                    ps, pb = P.psum()
                                    P.mm(ps[0:64, 0:512], gaA[dr][:, g0:g0 + 64], w2u[:, dr, :],
                                         True, True, [ga_b, w2_b], [pb])
                                    la, lab, _ = lar.next()
                                    P.act(la[:], ps[0:64, 0:512], AF.Exp, [pb], [lab], scale=-1.0)
                                    P.act(la[:], la[:], AF.Ln, [lab], [lab], bias=1.0)
                                    tric = tri["U_incl" if dr == 0 else "L_incl"][1]
                                    tris = tri["L_strict" if dr == 0 else "U_strict"][1]
                                    psc, pcb = P.psum()
                                    for j in range(4):
                                        P.mm(psc[:, j * 64:(j + 1) * 64], la[:, j * 128:(j + 1) * 128], tric[:], True, True,
                                             [lab, cb], [pcb])
                                    psr, prb = P.psum()
                                    P.mm(psr[0:64, 0:512], tris[:], la[:], True, True, [lab, cb], [prb])
                                    ep, epb, _ = epr.next(); en, enb, _ = enr.next(); er, erb, _ = err.next()
                                    P.act(ep[:], psc[:, 0:256], AF.Exp, [pcb], [epb])
                                    P.act(en[:], psc[:, 0:256], AF.Exp, [pcb], [enb], scale=-1.0)
                                    P.act(er[:], psr[0:64, 0:512], AF.Exp, [prb], [erb])
                                    qd, qdb, _ = qdr.next(); ki, kib, _ = kir.next(); ke, keb, _ = ker.next()
                                    P.tt("dve", qd[:], qT[:, :, t0:t0 + 64], ep[:].rearrange("p (j t) -> p j t", t=64),
                                         ALU.mult, [u_b, epb], [qdb])
                                    P.tt("dve", ki[:], kT[:, :, t0:t0 + 64], en[:].rearrange("p (j t) -> p j t", t=64),
                                         ALU.mult, [u_b, enb], [kib])
                                    P.tt("dve", ke[:], ktok[:], er[:], ALU.mult, [ktb, erb], [keb])
                                    q_ap = [qd[:, j, :] for j in range(NJ)]; q_bufs = [qdb]
                                    k_ap = [ki[:, j, :] for j in range(NJ)]; k_bufs = [kib]
                                    last = 63 if dr == 0 else 0
                                    rowsc = [ep[:, j * 64 + last:j * 64 + last + 1] for j in range(NJ)]
                                    rowsc_bufs = [epb]
                                    wcol = None
                                else:
                                    gc = gch0 + c
                                    wcol = WT[:, gc, dr * 8 + hd:dr * 8 + hd + 1]
                                    tcol = WT[:, gc, 16 + dr * 8 + hd:16 + dr * 8 + hd + 1]
                                    di = (dr * 8 + hd) * 24 + gc
                                    dcol = DEC[:, di:di + 1]
                                    ke, keb, _ = ker.next()
                                    P.ts("dve", ke[:], ktok[:], wcol, None, ALU.mult, None, [ktb, WT_b], [keb])
                                    q_ap = [qT[:, j, t0:t0 + 64] for j in range(NJ)]; q_bufs = [u_b]
                                    k_ap = [kT[:, j, t0:t0 + 64] for j in range(NJ)]; k_bufs = [u_b]
                                    rowsc = [dcol] * NJ
                                    rowsc_bufs = [DEC_b]
                                psa, pab = P.psum()
                                for j in range(NJ):
                                    P.mm(psa[0:64, 0:64], k_ap[j], q_ap[j], j == 0, j == NJ - 1, k_bufs + q_bufs, [pab])
                                at, atb, _ = atr.next()
                                if gla:
                                    P.tt("dve", at[:], psa[0:64, 0:64], mask[:], ALU.mult, [pab, cb], [atb])
                                else:
                                    P.stt(at[:], psa[0:64, 0:64], wcol, mask[:], ALU.mult, ALU.mult, [pab, cb, WT_b], [atb])
                                return dict(it=it, dr=dr, c=c, second=second, t0=t0, g0=g0, ktok=ktok, ktb=ktb, vtok=vtok, vtb=vtb,
                                            q_ap=q_ap, q_bufs=q_bufs, rowsc=rowsc, rowsc_bufs=rowsc_bufs, ke=ke, keb=keb, at=at, atb=atb,
                                            dcol=(None if gla else dcol), tcol=(None if gla else tcol))

                            def stageB(x):
                                it, dr, c, second, t0, g0 = x["it"], x["dr"], x["c"], x["second"], x["t0"], x["g0"]
                                ktok, ktb, vtok, vtb = x["ktok"], x["ktb"], x["vtok"], x["vtb"]
                                q_ap, q_bufs, rowsc, rowsc_bufs = x["q_ap"], x["q_bufs"], x["rowsc"], x["rowsc_bufs"]
                                ke, keb, at, atb, dcol, tcol = x["ke"], x["keb"], x["at"], x["atb"], x["dcol"], x["tcol"]
                                if not gla:
                                    P.act(Sb[dr][:], S[dr][:], AF.Copy, [S_b[dr], DEC_b], [Sb_b[dr]], scale=dcol)
                                    P.act(nbt[dr][:], nst[dr][:], AF.Copy, [S_b[dr], DEC_b], [Sb_b[dr]], scale=dcol)
                                for j in range(NJ):
                                    for h2 in range(DV // 512):
                                        pss, psb_ = P.psum()
                                        P.mm(pss[:, 0:512], ke[:, j * 128:(j + 1) * 128], vtok[:, h2 * 512:(h2 + 1) * 512], True, True,
                                             [keb, vtb], [psb_])
                                        ssl = S[dr][:, j, h2 * 512:(h2 + 1) * 512]
                                        P.stt(ssl, ssl, rowsc[j], pss[:, 0:512], ALU.mult, ALU.add,
                                              [psb_, S_b[dr], Sb_b[dr]] + rowsc_bufs, [S_b[dr]])
                                if not gla:
                                    psn, pnb = P.psum()
                                    for j in range(NJ):
                                        P.mm(psn[:, j:j + 1], ke[:, j * 128:(j + 1) * 128], onesb[0:64, :], True, True,
                                             [keb, cb], [pnb])
                                    P.stt(nst[dr][:], nst[dr][:], dcol, psn[:, 0:2], ALU.mult, ALU.add,
                                          [pnb, S_b[dr], Sb_b[dr], DEC_b], [S_b[dr]])
                                ot, otb, otd = osb.next()
                                if second:
                                    pt, ptb, ptd = prt.next()
                                    pbuf = part_bufs[(si, hd, c)]
                                    P.dma(LQ, pt[:], PART[g0:g0 + 64, hd * DV:(hd + 1) * DV], ptd, reads=[pbuf], writes=[ptb])
                                if not gla:
                                    psd, pdb = P.psum()
                                    P.mm(psd[0:64, 0:1], at[:], onesb[0:64, :], True, False, [atb, cb], [pdb])
                                    for j in range(NJ):
                                        P.mm(psd[0:64, 0:1], q_ap[j], nbt[dr][:, j:j + 1], False, j == NJ - 1,
                                             q_bufs + [Sb_b[dr]], [pdb])
                                    dn, dnb, _ = smr2.next()
                                    P.act(dn[:, 0:1], psd[0:64, 0:1], AF.Abs, [pdb], [dnb])
                                    P.ts("dve", dn[:, 1:2], dn[:, 0:1], tcol, None, ALU.max, None, [dnb, WT_b], [dnb])
                                    P.op("dve", lambda e, o=dn[:, 2:3], i_=dn[:, 1:2]: e.reciprocal(out=o, in_=i_), [dnb], [dnb])
                                for h2 in range(DV // 512):
                                    pso, pob = P.psum()
                                    P.mm(pso[0:64, 0:512], at[:], vtok[:, h2 * 512:(h2 + 1) * 512], True, False,
                                         [atb, vtb], [pob])
                                    for j in range(NJ):
                                        P.mm(pso[0:64, 0:512], q_ap[j], Sb[dr][:, j, h2 * 512:(h2 + 1) * 512], False, j == NJ - 1,
                                             q_bufs + [Sb_b[dr]], [pob])
                                    osl = ot[:, h2 * 512:(h2 + 1) * 512]
                                    if gla:
                                        if second:
                                            P.tt("dve", osl, pso[0:64, 0:512], pt[:, h2 * 512:(h2 + 1) * 512], ALU.add, [pob, ptb], [otb])
                                        else:
                                            P.act(osl, pso[0:64, 0:512], AF.Copy, [pob], [otb])
                                    else:
                                        if second:
                                            P.stt(osl, pso[0:64, 0:512], dn[:, 2:3], pt[:, h2 * 512:(h2 + 1) * 512], ALU.mult, ALU.add,
                                                  [pob, ptb, dnb], [otb])
                                        else:
                                            P.ts("dve", osl, pso[0:64, 0:512], dn[:, 2:3], None, ALU.mult, None, [pob, dnb], [otb])
                                if second:
                                    if (not gla) and is_s:
                                        dst3 = OUT[tok0:tok0 + T, hd * DV:(hd + 1) * DV].rearrange("(r c) e -> c r e", c=64)
                                        for cl in range(4):
                                            P.dma(SQ, dst3[4 * c + cl, :, :], ot[cl * 16:(cl + 1) * 16, :], otd, reads=[otb])
                                    else:
                                        P.dma(SQ, OUT[g0:g0 + 64, hd * DV:(hd + 1) * DV], ot[:], otd, reads=[otb])
                                else:
                                    pbuf = P.buf()
                                    part_bufs[(si, hd, c)] = pbuf
                                    P.dma(SQ, PART[g0:g0 + 64, hd * DV:(hd + 1) * DV], ot[:], otd, reads=[otb], writes=[pbuf])
                                if gla:
                                    for j in range(NJ):
                                        P.act(Sb[dr][:, j, :], S[dr][:, j, :], AF.Copy, [S_b[dr]], [Sb_b[dr]])

                            pend = {}
                            for it in range(nch + 1):
                                if it < nch:
                                    for dr in range(2):
                                        pend[(it, dr)] = stageA(it, dr)
                                if it >= 1:
                                    for dr in range(2):
                                        stageB(pend.pop((it - 1, dr)))
                                yield
                            if not is_s:
                                for dr in range(2):
                                    dstS = (o_S if gla else o_C)[si, dr, hd, :, :].rearrange("(j p) e -> p j e", p=128)
                                    P.dma(SQ, dstS, S[dr][:], so_d[dr], reads=[S_b[dr]])
                                    if not gla:
                                        P.dma(SQ, o_n[si, dr, hd, :].rearrange("(j p) -> p j", p=128), nst[dr][:], so_d[dr],
                                              reads=[S_b[dr]])
                        gch0 += nch

            with ExitStack() as sc:
                gens = [scan_gen("gla", sc)]
                pattern = [0]
                if stop_after > 3:
                    gens.append(scan_gen("mlstm", sc))
                    pattern = [0, 1, 1]
                alive = set(range(len(gens)))
                while alive:
                    for gi in pattern:
                        if gi in alive:
                            try:
                                next(gens[gi])
                            except StopIteration:
                                alive.discard(gi)
                P.barrier()
                P.emit()
                print("engine sem counts", {k: e.count for k, e in P.E.items()})
        if stop_after <= 4:
            return nc, dbg

        for branch in ("gla", "mlstm"):
            with ExitStack() as ph:
                src = OG if branch == "gla" else OM
                dstT = UT if branch == "gla" else UMT
                nwT = gnwT if branch == "gla" else mnwT
                g1 = sb(ph, "g1", [128, 32, 512], BF16); g1b = P.buf(); g1d = P.dsem("g1")
                if branch == "mlstm":
                    g2 = sb(ph, "g2", [128, 32, 512], BF16); g2b = P.buf(); g2d = P.dsem("g2")
                ust = sb(ph, "ust", [128, 32, 512], BF16); ustb = P.buf(); ustd = P.dsem("ust")
                oin = Ring(P, ph, "oin", 2, [128, D], F32, dsem=True)
                tmp = sb(ph, "p41tmp", [128, D], F32); tmpb = P.buf()
                sts = Ring(P, ph, "p41s", 2, [128, 40], F32)
                for tb in range(3):
                    P.dma("sp", g1[:], (SZT if branch == "gla" else SZM)[:, tb * 512:(tb + 1) * 512].rearrange(
                        "(k p) t -> p k t", p=128), g1d, writes=[g1b])
                    if branch == "mlstm":
                        P.dma("sp", g2[:], SGO[:, tb * 512:(tb + 1) * 512].rearrange("(k p) t -> p k t", p=128), g2d,
                              writes=[g2b])
                        for k4 in range(4):
                            P.tt("pool", g1[:, k4 * 8:(k4 + 1) * 8, :], g1[:, k4 * 8:(k4 + 1) * 8, :],
                                 g2[:, k4 * 8:(k4 + 1) * 8, :], ALU.mult, [g1b, g2b], [g1b])
                    for ti in range(4):
                        i = tb * 4 + ti
                        ot, ob, od = oin.next()
                        P.dma("sp", ot[:], src[i * 128:(i + 1) * 128, :], od, writes=[ob])
                        s_, s_b, _ = sts.next()
                        NHh = 4 if branch == "gla" else 8
                        E_ = D // NHh
                        o3 = ot[:].rearrange("p (h e) -> p h e", e=E_)
                        sq = tmp[:].rearrange("p (h e) -> p h e", e=E_)
                        if branch == "mlstm":
                            P.op("dve", lambda e, o=s_[:, 0:8], i_=o3: e.tensor_reduce(out=o, in_=i_, op=ALU.add, axis=AX.X),
                                 [ob], [s_b])
                            P.ts("dve", s_[:, 8:16], s_[:, 0:8], 1.0 / E_, None, ALU.mult, None, [s_b], [s_b])
                            P.tt("dve", o3, o3, bc_last(s_[:, 8:16], E_), ALU.subtract, [ob, s_b], [ob])
                        P.tt("pool", sq, o3, o3, ALU.mult, [ob], [tmpb])
                        P.op("dve", lambda e, o=s_[:, 16:16 + NHh], i_=sq: e.tensor_reduce(out=o, in_=i_, op=ALU.add, axis=AX.X),
                             [tmpb], [s_b])
                        P.ts("dve", s_[:, 24:24 + NHh], s_[:, 16:16 + NHh], 1.0 / E_, EPS, ALU.mult, ALU.add, [s_b], [s_b])
                        P.act(s_[:, 24:24 + NHh], s_[:, 24:24 + NHh], AF.Sqrt, [s_b], [s_b])
                        P.op("dve", lambda e, o=s_[:, 32:32 + NHh], i_=s_[:, 24:24 + NHh]: e.reciprocal(out=o, in_=i_), [s_b], [s_b])
                        P.tt("dve", o3, o3, bc_last(s_[:, 32:32 + NHh], E_), ALU.mult, [ob, s_b], [ob])
                        for g in range(8):
                            ps, pb = P.psum()
                            for kk in range(4):
                                k = g * 4 + kk
                                P.tr(ps[:, kk * 128:(kk + 1) * 128], ot[:, k * 128:(k + 1) * 128], ident[:], [ob, cb], [pb])
                            for kk in range(4):
                                k = g * 4 + kk
                                P.stt(ust[:, k, ti * 128:(ti + 1) * 128], ps[:, kk * 128:(kk + 1) * 128], nwT[:, k:k + 1],
                                      g1[:, k, ti * 128:(ti + 1) * 128], ALU.mult, ALU.mult, [pb, vec_b, g1b], [ustb])
                    P.dma("sp", dstT[:, tb * 512:(tb + 1) * 512].rearrange("(k p) t -> p k t", p=128), ust[:], ustd,
                          reads=[ustb])
                P.barrier()
                P.emit()
        if stop_after <= 5:
            return nc, dbg

        tmp32r = [None]

        def gemm_stage(name, W, actsrc, evac_fn, dst, dst_dt):
            with ExitStack() as ph:
                aT = sb(ph, "aT", [128, 32, NTOK], BF16); aT_b = P.buf(); aT_d = P.dsem("aT" + name)
                for tb in range(3):
                    P.dma("sp", aT[:, :, tb * 512:(tb + 1) * 512], actsrc[:, tb * 512:(tb + 1) * 512].rearrange(
                        "(k p) t -> p k t", p=128), aT_d, writes=[aT_b])
                wst = Ring(P, ph, "gwst", 3, [128, 32, 128], F32, dsem=True)
                wbf = Ring(P, ph, "gwbf", 2, [128, 32, 128], BF16)
                stg = Ring(P, ph, "gstg", 2, [128, NTOK], dst_dt, dsem=True)
                aux = Ring(P, ph, "gaux", 2, [128, NTOK], F32, dsem=True) if name == "B" else None
                auxb = Ring(P, ph, "gauxb", 2, [128, NTOK], BF16, dsem=True) if name in ("A", "B") else None
                tmp32r[0] = Ring(P, ph, "gtmp", 2, [128, 512], F32)

                def issue(bi):
                    wt, wb_, wd = wst.next()
                    P.dma("sp", wt[:], W[:, bi * 128:(bi + 1) * 128].rearrange("(k p) n -> p k n", p=128), wd, writes=[wb_])
                    return wt, wb_
                loaded = {0: issue(0), 1: issue(1)}
                casted = {}

                def do_cast(bi):
                    wt, wb_ = loaded.pop(bi)
                    wbt, wbb, _ = wbf.next()
                    P.act(wbt[:, 0:16, :], wt[:, 0:16, :], AF.Copy, [wb_], [wbb])
                    P.act(wbt[:, 16:32, :], wt[:, 16:32, :], AF.Copy, [wb_], [wbb])
                    casted[bi] = (wbt, wbb)
                do_cast(0)
                for bi in range(32):
                    if bi + 2 < 32:
                        loaded[bi + 2] = issue(bi + 2)
                    if bi + 1 < 32:
                        do_cast(bi + 1)
                    wbt, wbb = casted.pop(bi)
                    st_t, st_b, st_d = stg.next()
                    ctx = evac_fn("pre", bi, aux, auxb)
                    for tb in range(3):
                        ps, pb = P.psum()
                        for k in range(32):
                            P.mm(ps[:, 0:512], wbt[:, k, :], aT[:, k, tb * 512:(tb + 1) * 512], k == 0, k == 31, [wbb, aT_b], [pb])
                        evac_fn("evac", bi, aux, auxb, ctx=ctx, ps=ps, pb=pb, tb=tb, st_t=st_t, st_b=st_b)
                    P.dma("sp", dst[bi * 128:(bi + 1) * 128, :], st_t[:], st_d, reads=[st_b])
                P.barrier()
                P.emit()

        def evac_A(mode, bi, aux, auxb, ctx=None, ps=None, pb=None, tb=None, st_t=None, st_b=None):
            if mode == "pre":
                gt, gb_, gd = auxb.next()
                P.dma("sp", gt[:], G01[bi * 128:(bi + 1) * 128, :], gd, writes=[gb_])
                return (gt, gb_)
            gt, gb_ = ctx
            P.tt("dve", st_t[:, tb * 512:(tb + 1) * 512], ps[:, 0:512], gt[:, tb * 512:(tb + 1) * 512], ALU.mult,
                 [pb, gb_], [st_b])

        def evac_B(mode, bi, aux, auxb, ctx=None, ps=None, pb=None, tb=None, st_t=None, st_b=None):
            if mode == "pre":
                gt, gb_, gd = auxb.next()
                P.dma("sp", gt[:], G01[4096 + bi * 128:4096 + (bi + 1) * 128, :], gd, writes=[gb_])
                t1t, t1b, t1d = aux.next()
                P.dma("sp", t1t[:], T1[bi * 128:(bi + 1) * 128, :], t1d, writes=[t1b])
                return (gt, gb_, t1t, t1b)
            gt, gb_, t1t, t1b = ctx
            sl = slice(tb * 512, (tb + 1) * 512)
            tm, tmb, _ = tmp32r[0].next()
            P.tt("dve", tm[:], ps[:, 0:512], gt[:, sl], ALU.mult, [pb, gb_], [tmb])
            P.tt("pool", st_t[:, sl], tm[:], t1t[:, sl], ALU.add, [tmb, t1b], [st_b])

        def evac_C(mode, bi, aux, auxb, ctx=None, ps=None, pb=None, tb=None, st_t=None, st_b=None):
            if mode == "pre":
                return None
            j = 0 if tb == 0 else 1
            P.ts("dve", st_t[:, tb * 512:(tb + 1) * 512], ps[:, 0:512], gateT[:, bi, j:j + 1], None, ALU.mult, None,
                 [pb, modT_b], [st_b])

        gemm_stage("A", w_gp, UT, evac_A, T1, F32)
        gemm_stage("B", w_mp, UMT, evac_B, MIXT, BF16)
        gemm_stage("C", w_o, MIXT, evac_C, OUTT, F32)
        if stop_after <= 6:
            return nc, dbg

        with ExitStack() as ph:
            fnw = sb(ph, "fnw", [128, D]); fnw_b = P.buf(); fd = P.dsem("fnw")
            P.dma("sp", fnw[:], f_nw.partition_broadcast(128)[:, 0, :], fd, writes=[fnw_b])
            xin = Ring(P, ph, "x5", 2, [128, D], F32, dsem=True)
            oin = Ring(P, ph, "o5", 2, [128, 32, 128], F32, dsem=True)
            yst = Ring(P, ph, "y5", 2, [128, D], F32, dsem=True)
            junk = sb(ph, "junk5", [128, D], BF16); junk_b = P.buf()
            sts = Ring(P, ph, "s5", 2, [128, 4], F32)
            for i in range(12):
                xt, xb, xd = xin.next()
                P.dma("sp", xt[:], x_all[i * 128:(i + 1) * 128, :], xd, writes=[xb])
                ot, ob, od = oin.next()
                P.dma("sp", ot[:], OUTT[:, i * 128:(i + 1) * 128].rearrange("(k p) t -> p k t", p=128), od, writes=[ob])
                for g in range(8):
                    ps, pb = P.psum()
                    for kk in range(4):
                        k = g * 4 + kk
                        P.tr(ps[:, kk * 128:(kk + 1) * 128], ot[:, k, :], ident[:], [ob, cb], [pb])
                    P.tt("dve", xt[:, g * 512:(g + 1) * 512], ps[:, 0:512], xt[:, g * 512:(g + 1) * 512], ALU.add,
                         [pb, xb], [xb])
                s_, s_b, _ = sts.next()
                P.act(junk[:], xt[:], AF.Square, [xb], [junk_b, s_b], accum_out=s_[:, 0:1])
                P.ts("dve", s_[:, 1:2], s_[:, 0:1], 1.0 / D, EPS, ALU.mult, ALU.add, [s_b], [s_b])
                P.act(s_[:, 2:3], s_[:, 1:2], AF.Sqrt, [s_b], [s_b])
                P.op("dve", lambda e, o=s_[:, 3:4], i_=s_[:, 2:3]: e.reciprocal(out=o, in_=i_), [s_b], [s_b])
                yt, yb, yd = yst.next()
                P.stt(yt[:], xt[:], s_[:, 3:4], fnw[:], ALU.mult, ALU.mult, [xb, s_b, fnw_b], [yb])
                P.dma("sp", y_all[i * 128:(i + 1) * 128, :], yt[:], yd, reads=[yb])
            P.barrier()
            P.emit()
    return nc, dbg


_CACHE = {}


def _get_nc():
    if "nc" not in _CACHE:
        _CACHE["nc"] = build_program()[0]
    return _CACHE["nc"]


def make_in_maps(inputs):
    f = lambda a: np.ascontiguousarray(np.asarray(a, dtype=np.float32))
    x_prompt, x_sample, c = f(inputs["x_prompt"]), f(inputs["x_sample"]), f(inputs["c"])
    shared = {
        "w_ada": f(inputs["w_ada"][0]), "b_ada": f(inputs["b_ada"]), "w_in": f(inputs["w_in"][0]),
        "w_a2": f(inputs["gla_w_a2"][0]), "b_a": f(inputs["gla_b_a"][0]), "b_i": f(inputs["mlstm_b_i"][0]),
        "b_f": f(inputs["mlstm_b_f"][0]), "b_mg": f(inputs["b_merge"]), "g_nw": f(inputs["gla_norm_w"]),
        "m_nw": f(inputs["mlstm_norm_w"]), "w_gp": f(inputs["w_gla_proj"][0]), "w_mp": f(inputs["w_mlstm_proj"][0]),
        "w_o": f(inputs["w_out"][0]), "f_nw": f(inputs["final_norm_w"]).reshape(1, D),
    }
    c_ctx = f(inputs["c_ctx"])
    sS, sC, sn, sm = (f(inputs[k]) for k in ("state_gla_S", "state_mlstm_C", "state_mlstm_n", "state_mlstm_m"))
    maps = []
    for core in range(8):
        m = dict(shared)
        m["x_all"] = np.concatenate([x_prompt[2 * core], x_prompt[2 * core + 1], x_sample[core]], axis=0)
        m["cvec"] = np.stack([c_ctx, c[core]], axis=0)
        m["st_S"] = sS[core, 0]
        m["st_C"] = sC[core, 0]
        m["st_n"] = sn[core, 0]
        m["st_m"] = sm[core, 0]
        maps.append(m)
    return maps


def kernel(**inputs):
    nc = _get_nc()
    maps = make_in_maps(inputs)
    res = run_bass_kernel_spmd(nc, maps, core_ids=list(range(8))).results
    y_prompt = np.stack([res[i // 2]["y_all"][(i % 2) * 256:(i % 2) * 256 + 256] for i in range(16)], axis=0)
    y_sample = np.stack([res[i]["y_all"][512:1536] for i in range(8)], axis=0)
    new_S = np.concatenate([res[i]["o_S"] for i in range(8)], axis=0)[:, None]
    new_C = np.concatenate([res[i]["o_C"] for i in range(8)], axis=0)[:, None]
    new_n = np.concatenate([res[i]["o_n"] for i in range(8)], axis=0)[:, None]
    new_m = np.concatenate([res[i]["o_m"] for i in range(8)], axis=0)[:, None]
    return (y_prompt.astype(np.float32), y_sample.astype(np.float32), new_S.astype(np.float32),
            new_C.astype(np.float32), new_n.astype(np.float32), new_m.astype(np.float32))
```
